# Optimizing a Trainium2 kernel written in Bass

```python
import math
import jax, jax.numpy as jnp
from jax import lax
import numpy as np

D_MODEL = 1024
BATCH = 4
SEQ = 4096
DEPTH = 1

N_META = 16
ROPE_THETA = 500000.0
NORM_EPS = 1e-5

DA_HEADS = 8
DA_HEAD_DIM = 64
DA_V_DIM = 2 * DA_HEAD_DIM
DA_QK_WIDTH = DA_HEADS * 2 * DA_HEAD_DIM
DA_WIDTH = DA_HEADS * DA_V_DIM
DA_ROT_DIM = DA_HEAD_DIM // 4
Q_BLOCK = 128

GLA_HEADS = 4
GLA_KEY_DIM = D_MODEL // 2
GLA_VAL_DIM = D_MODEL
GLA_DK = GLA_KEY_DIM // GLA_HEADS
GLA_DV = GLA_VAL_DIM // GLA_HEADS
GLA_GATE_RANK = 16
GLA_GATE_NORMALIZER = 16.0
GLA_CHUNK = 64

IN_SPLITS = (DA_QK_WIDTH, DA_QK_WIDTH, DA_WIDTH, DA_WIDTH,
             GLA_KEY_DIM, GLA_KEY_DIM, GLA_VAL_DIM, GLA_VAL_DIM,
             GLA_GATE_RANK,
             D_MODEL, D_MODEL)
W_IN_COLS = 4 * 1024 + 512 + 512 + 1024 + 1024 + 16 + 2 * 1024

kernel_name = "hybrid_diffattn_gla_gated_block"


def rms_norm(x, w, eps=NORM_EPS):
    xf = x.astype(jnp.float32)
    y = xf * lax.rsqrt(jnp.mean(xf * xf, axis=-1, keepdims=True) + eps)
    return (y * w.astype(jnp.float32)).astype(x.dtype)


def partial_rope(x, pos):
    half = DA_ROT_DIM // 2
    inv_freq = ROPE_THETA ** (-jnp.arange(half, dtype=jnp.float32) / half)
    ang = pos.astype(jnp.float32)[:, None] * inv_freq[None, :]
    cos, sin = jnp.cos(ang), jnp.sin(ang)
    xr = x[..., :DA_ROT_DIM].astype(jnp.float32)
    x1, x2 = xr[..., :half], xr[..., half:]
    rot = jnp.concatenate([x1 * cos - x2 * sin, x2 * cos + x1 * sin], axis=-1).astype(x.dtype)
    return jnp.concatenate([rot, x[..., DA_ROT_DIM:]], axis=-1)


def diff_attn_block(q1, q2, q_pos, k1, k2, v, k_pos, lam):
    scale = DA_HEAD_DIM ** -0.5
    mask = k_pos[None, :] <= q_pos[:, None]

    def probs(q, k):
        s = jnp.einsum("bhqd,bhkd->bhqk", q, k).astype(jnp.float32) * scale
        return jax.nn.softmax(jnp.where(mask, s, -jnp.inf), axis=-1)

    p = probs(q1, k1) - lam * probs(q2, k2)
    return jnp.einsum("bhqk,bhkv->bhqv", p.astype(v.dtype), v)


def diff_attention(q1, q2, k1, k2, v, pos, lam):
    B, H, L, d = q1.shape
    S = L - N_META
    nb = S // Q_BLOCK
    o_meta = diff_attn_block(q1[:, :, :N_META], q2[:, :, :N_META], pos[:N_META],
                             k1[:, :, :N_META], k2[:, :, :N_META], v[:, :, :N_META],
                             pos[:N_META], lam)

    def blocks(t):
        return t[:, :, N_META:].reshape(B, H, nb, Q_BLOCK, t.shape[-1]).transpose(2, 0, 1, 3, 4)

    pos_b = pos[N_META:].reshape(nb, Q_BLOCK)
    o_real = lax.map(lambda a: diff_attn_block(a[0], a[1], a[2], k1, k2, v, pos, lam),
                     (blocks(q1), blocks(q2), pos_b))
    o_real = o_real.transpose(1, 2, 0, 3, 4).reshape(B, H, S, v.shape[-1])
    return jnp.concatenate([o_meta, o_real], axis=2)


def gla_chunk(state, q, k, v, lg):
    f32 = jnp.float32
    C = q.shape[2]
    qf = q.astype(f32) * (GLA_DK ** -0.5)
    kf, vf = k.astype(f32), v.astype(f32)
    b = jnp.cumsum(lg.astype(f32), axis=2)
    causal = jnp.tril(jnp.ones((C, C), dtype=bool))[None, None, :, :, None]
    decay = jnp.exp(jnp.where(causal, b[:, :, :, None, :] - b[:, :, None, :, :], -jnp.inf))
    a = jnp.einsum("bhtc,bhjc,bhtjc->bhtj", qf, kf, decay)
    o = (jnp.einsum("bhtj,bhjv->bhtv", a, vf)
         + jnp.einsum("bhtc,bhcv->bhtv", qf * jnp.exp(b), state))
    b_last = b[:, :, -1]
    new_state = (jnp.exp(b_last)[..., None] * state
                 + jnp.einsum("bhjc,bhjv->bhcv", kf * jnp.exp(b_last[:, :, None, :] - b), vf))
    return new_state, o.astype(v.dtype)


def gla(q, k, v, lg):
    B, H, L, _ = q.shape
    S = L - N_META
    nc = S // GLA_CHUNK
    s0 = jnp.zeros((B, H, GLA_DK, GLA_DV), jnp.float32)
    s1, o_meta = gla_chunk(s0, q[:, :, :N_META], k[:, :, :N_META], v[:, :, :N_META], lg[:, :, :N_META])

    def chunks(t):
        return t[:, :, N_META:].reshape(B, H, nc, GLA_CHUNK, t.shape[-1]).transpose(2, 0, 1, 3, 4)

    _, o_real = lax.scan(lambda st, xs: gla_chunk(st, *xs), s1,
                         (chunks(q), chunks(k), chunks(v), chunks(lg)))
    o_real = o_real.transpose(1, 2, 0, 3, 4).reshape(B, H, S, GLA_DV)
    return jnp.concatenate([o_meta, o_real], axis=2)


def hybrid_layer(h, pos, lam_init, norm_w, w_in, lam_q1, lam_k1, lam_q2, lam_k2, da_subln_w,
                 gla_gate_w2, gla_gate_b, gla_norm_w, w_branch_a, w_branch_b, w_out):
    B, L, _ = h.shape
    u = rms_norm(h, norm_w)
    proj = u @ w_in
    (a_q, a_k, a_v, a_z, g_q, g_k, g_v, g_z, g_lr, gate_a, gate_b) = jnp.split(
        proj, list(np.cumsum(IN_SPLITS)[:-1]), axis=-1)

    qa = a_q.reshape(B, L, DA_HEADS, 2, DA_HEAD_DIM).transpose(3, 0, 2, 1, 4)
    ka = a_k.reshape(B, L, DA_HEADS, 2, DA_HEAD_DIM).transpose(3, 0, 2, 1, 4)
    va = a_v.reshape(B, L, DA_HEADS, DA_V_DIM).transpose(0, 2, 1, 3)
    q1, q2 = partial_rope(qa[0], pos), partial_rope(qa[1], pos)
    k1, k2 = partial_rope(ka[0], pos), partial_rope(ka[1], pos)
    lam = (jnp.exp(jnp.sum(lam_q1.astype(jnp.float32) * lam_k1.astype(jnp.float32)))
           - jnp.exp(jnp.sum(lam_q2.astype(jnp.float32) * lam_k2.astype(jnp.float32)))
           + lam_init)
    o_a = diff_attention(q1, q2, k1, k2, va, pos, lam)
    o_a = rms_norm(o_a, da_subln_w) * (1.0 - lam_init)
    o_a = o_a.transpose(0, 2, 1, 3).reshape(B, L, DA_WIDTH) * jax.nn.silu(a_z)
    y_a = o_a @ w_branch_a

    qb = g_q.reshape(B, L, GLA_HEADS, GLA_DK).transpose(0, 2, 1, 3)
    kb = g_k.reshape(B, L, GLA_HEADS, GLA_DK).transpose(0, 2, 1, 3)
    vb = g_v.reshape(B, L, GLA_HEADS, GLA_DV).transpose(0, 2, 1, 3)
    gk = (g_lr @ gla_gate_w2 + gla_gate_b).astype(jnp.float32)
    lg = (jax.nn.log_sigmoid(gk) / GLA_GATE_NORMALIZER).reshape(B, L, GLA_HEADS, GLA_DK).transpose(0, 2, 1, 3)
    o_b = rms_norm(gla(qb, kb, vb, lg), gla_norm_w)
    o_b = o_b.transpose(0, 2, 1, 3).reshape(B, L, GLA_VAL_DIM) * jax.nn.silu(g_z)
    y_b = o_b @ w_branch_b

    merged = jax.nn.sigmoid(gate_a) * y_a + jax.nn.sigmoid(gate_b) * y_b
    return h + merged @ w_out


def setup_inputs(seed: int = 0) -> dict:
    key = jax.random.key(seed)
    ks = jax.random.split(key, 17)
    f32 = jnp.float32
    n = lambda k, shape, s: jax.random.normal(k, shape, f32) * s
    return {
        "x": n(ks[0], (BATCH, SEQ, D_MODEL), 1.0),
        "meta_tokens": n(ks[1], (N_META, D_MODEL), 1.0),
        "norm_w": 1.0 + n(ks[2], (DEPTH, D_MODEL), 0.02),
        "w_in": n(ks[3], (DEPTH, D_MODEL, W_IN_COLS), D_MODEL ** -0.5),
        "lam_q1": n(ks[4], (DEPTH, DA_HEAD_DIM), 0.1),
        "lam_k1": n(ks[5], (DEPTH, DA_HEAD_DIM), 0.1),
        "lam_q2": n(ks[6], (DEPTH, DA_HEAD_DIM), 0.1),
        "lam_k2": n(ks[7], (DEPTH, DA_HEAD_DIM), 0.1),
        "da_subln_w": 1.0 + n(ks[8], (DEPTH, DA_V_DIM), 0.02),
        "gla_gate_w2": n(ks[9], (DEPTH, GLA_GATE_RANK, GLA_KEY_DIM), GLA_GATE_RANK ** -0.5),
        "gla_gate_b": n(ks[10], (DEPTH, GLA_KEY_DIM), 0.01),
        "gla_norm_w": 1.0 + n(ks[11], (DEPTH, GLA_DV), 0.02),
        "w_branch_a": n(ks[12], (DEPTH, DA_WIDTH, D_MODEL), DA_WIDTH ** -0.5),
        "w_branch_b": n(ks[13], (DEPTH, GLA_VAL_DIM, D_MODEL), GLA_VAL_DIM ** -0.5),
        "w_out": n(ks[14], (DEPTH, D_MODEL, D_MODEL), D_MODEL ** -0.5),
        "final_norm_w": 1.0 + n(ks[15], (D_MODEL,), 0.02),
    }


def reference(x, meta_tokens, norm_w, w_in, lam_q1, lam_k1, lam_q2, lam_k2, da_subln_w,
              gla_gate_w2, gla_gate_b, gla_norm_w, w_branch_a, w_branch_b, w_out, final_norm_w):
    B, S, D = x.shape
    meta = jnp.broadcast_to(meta_tokens.astype(x.dtype)[None], (B, N_META, D))
    h = jnp.concatenate([meta, x], axis=1)
    pos = jnp.arange(N_META + S, dtype=jnp.int32)
    for layer in range(DEPTH):
        lam_init = 0.8 - 0.6 * math.exp(-0.3 * layer)
        h = hybrid_layer(h, pos, lam_init, norm_w[layer], w_in[layer], lam_q1[layer], lam_k1[layer],
                         lam_q2[layer], lam_k2[layer], da_subln_w[layer], gla_gate_w2[layer],
                         gla_gate_b[layer], gla_norm_w[layer], w_branch_a[layer], w_branch_b[layer],
                         w_out[layer])
    return rms_norm(h, final_norm_w)[:, N_META:]
```

```python
import contextlib
import os
import numpy as np
import concourse.bass as bass
import concourse.mybir as mybir
from concourse.bass_utils import run_bass_kernel_spmd

F32 = mybir.dt.float32
BF16 = mybir.dt.bfloat16
AF = mybir.ActivationFunctionType
ALU = mybir.AluOpType
AX = mybir.AxisListType

ENGS = ("pe", "act", "dve", "pool", "sp")

NMETA = 16
NOTH = 17
NOWN = 16
OTH0 = NMETA
OWN0 = NMETA + NOTH * 128
NTOK = OWN0 + NOWN * 128
NBLK = 1 + NOTH + NOWN
EPS = 1e-5
C_AQ, C_AK, C_AV, C_AZ = 0, 1024, 2048, 3072
C_GQ, C_GK, C_GV, C_GZ, C_GLR, C_GA, C_GB = 4096, 4608, 5120, 6144, 7168, 7184, 8208
WCOLS = 9232


class Sched:
    N_DMA_SEMS = {"sp": 8, "pool": 6}

    def __init__(self, nc=None, need=None):
        self.nc = nc
        self.rec = nc is None
        self.need_in = need if need is not None else set()
        self.need = set()
        self.seq = {e: 0 for e in ENGS}
        self.sigcount = {e: 0 for e in ENGS}
        self.sigmap = {}
        self.waited = {e: {} for e in ENGS}
        self.lastw = {}
        self.reads = {}
        self.esems = None
        self.dsems = None
        self.dma_n = {q: 0 for q in self.N_DMA_SEMS}
        self.dma_waited = {e: {} for e in ENGS}
        self.ninstr = 0

    def eng(self, e):
        nc = self.nc
        return {"pe": nc.tensor, "act": nc.scalar, "dve": nc.vector,
                "pool": nc.gpsimd, "sp": nc.sync}[e]

    def _wait_tok(self, e, tok):
        if tok[0] == "eng":
            _, pe_, ps_ = tok
            if self.rec:
                self.need.add((pe_, ps_))
                return
            val = self.sigmap[(pe_, ps_)]
            if self.waited[e].get(pe_, 0) >= val:
                return
            self.waited[e][pe_] = val
            self.eng(e).wait_ge(self.esems[pe_], val)
        else:
            _, q, idx, val = tok
            if self.rec:
                return
            key = (q, idx)
            if self.dma_waited[e].get(key, 0) >= val:
                return
            self.dma_waited[e][key] = val
            self.eng(e).wait_ge(self.dsems[q][idx], val)

    def _deps(self, e, reads, writes):
        toks = []
        for k in reads:
            w = self.lastw.get(k)
            if w is not None:
                toks.append(w)
        for k in writes:
            w = self.lastw.get(k)
            if w is not None:
                toks.append(w)
            for r in self.reads.get(k, ()):
                toks.append(r)
        best = {}
        out = []
        for t in toks:
            if t[0] == "eng":
                if t[1] == "pe" and e == "pe":
                    continue
                if best.get(t[1], -1) < t[2]:
                    best[t[1]] = t[2]
            elif t not in out:
                out.append(t)
        for pe_, ps_ in best.items():
            out.append(("eng", pe_, ps_))
        return out

    def _commit(self, tok, reads, writes):
        for k in reads:
            lst = self.reads.setdefault(k, [])
            if tok[0] == "eng":
                lst[:] = [r for r in lst if not (r[0] == "eng" and r[1] == tok[1])]
            lst.append(tok)
        for k in writes:
            self.lastw[k] = tok
            self.reads[k] = []

    def op(self, e, fn, reads=(), writes=()):
        self.ninstr += 1
        psr = [k for k in reads if k.startswith("ps")]
        if psr:
            reads = [k for k in reads if not k.startswith("ps")]
            writes = list(writes) + psr
        for t in self._deps(e, reads, writes):
            self._wait_tok(e, t)
        s = self.seq[e]
        self.seq[e] += 1
        tok = ("eng", e, s)
        if not self.rec:
            ins = fn()
            if (e, s) in self.need_in:
                ins.then_inc(self.esems[e], 1)
                self.sigcount[e] += 1
                self.sigmap[(e, s)] = self.sigcount[e]
        self._commit(tok, reads, writes)
        return tok

    def dma(self, q, fn, reads=(), writes=()):
        self.ninstr += 1
        n = self.dma_n[q]
        P = self.N_DMA_SEMS[q]
        idx = n % P
        val = 16 * (n // P + 1)
        self.dma_n[q] += 1
        if n >= P:
            self._wait_tok(q, ("dma", q, idx, val - 16))
        for t in self._deps(q, reads, writes):
            self._wait_tok(q, t)
        tok = ("dma", q, idx, val)
        if not self.rec:
            fn().then_inc(self.dsems[q][idx], 16)
        self._commit(tok, reads, writes)
        return tok

    def wait_all_dma(self, e):
        for q, P in self.N_DMA_SEMS.items():
            n = self.dma_n[q]
            for idx in range(min(P, n)):
                cnt = (n - 1 - idx) // P + 1
                self._wait_tok(e, ("dma", q, idx, 16 * cnt))

    def barrier(self):
        toks = []
        for e in ("pe", "act", "dve", "pool"):
            if self.seq[e] > 0:
                toks.append(("eng", e, self.seq[e] - 1))
        for e in ENGS:
            for t in toks:
                if t[1] != e:
                    self._wait_tok(e, t)
            self.wait_all_dma(e)
        self.lastw = {}
        self.reads = {}


class Arena:
    def __init__(self, nbytes):
        self.nbytes = nbytes
        self.top = 0
        self.t = None
        self.peak = 0

    def mark(self):
        return self.top

    def reset(self, m):
        self.top = m

    def alloc(self, shape, dt):
        esz = 2 if dt == BF16 else 4
        n = int(np.prod(shape[1:]))
        nb = (n * esz + 63) // 64 * 64
        off = self.top
        self.top += nb
        self.peak = max(self.peak, self.top)
        assert self.top <= self.nbytes, f"arena overflow {self.top} > {self.nbytes}"
        if self.t is None:
            return None
        ap = self.t[:, off // 2: off // 2 + n * esz // 2]
        if dt == F32:
            ap = ap.bitcast(F32)
        if len(shape) == 3:
            ap = ap.rearrange("p (a b) -> p a b", a=shape[1])
        elif len(shape) == 4:
            ap = ap.rearrange("p (a b c) -> p a b c", a=shape[1], b=shape[2])
        return ap


ARENA_BYTES = 212736


def build(S, debug=False, stop=None):
    nc = S.nc
    rec = S.rec
    D = {}
    T = {}
    top = contextlib.ExitStack()

    A = Arena(ARENA_BYTES)
    if not rec:
        A.t = top.enter_context(nc.sbuf_tensor("arena", [128, ARENA_BYTES // 2], BF16))

    def sbuf(es, name, shape, dt):
        return A.alloc(shape, dt)

    if not rec:
        def din(name, shape, dt=F32):
            D[name] = nc.dram_tensor(name, shape, dt, kind="ExternalInput").ap()
        din("xT", [128, 8, NTOK])
        din("xo", [NOWN, 128, 1024])
        din("w_in", [1024, WCOLS])
        din("w_ba", [1024, 1024])
        din("w_bb", [1024, 1024])
        din("w_out", [1024, 1024])
        din("normw", [128, 8])
        din("lamv", [1, 256])
        din("sublnw", [1, 128])
        din("gnw", [1, 256])
        din("fnw", [1, 1024])
        din("gateb", [128, 4])
        din("w2", [16, 512])
        din("cs16", [128, NBLK, 16])
        din("sn16", [128, NBLK, 16])
        din("fcol", [128, 2])
        din("tri", [128, 128])
        din("identf", [128, 128])
        D["out"] = nc.dram_tensor("out", [NOWN, 128, 1024], F32, kind="ExternalOutput").ap()
        if debug:
            D["d_ut"] = nc.dram_tensor("d_ut", [128, 8, NTOK], BF16, kind="ExternalOutput").ap()
            D["d_ob"] = nc.dram_tensor("d_ob", [128, NOWN, 1024], BF16, kind="ExternalOutput").ap()
            D["d_oa"] = nc.dram_tensor("d_oa", [128, NOWN, 1024], BF16, kind="ExternalOutput").ap()
        S.esems = {e: top.enter_context(nc.semaphore("es_" + e)) for e in ENGS}
        S.dsems = {q: [top.enter_context(nc.semaphore(f"ds_{q}{i}")) for i in range(n)]
                   for q, n in S.N_DMA_SEMS.items()}
        WIN = D["w_in"].rearrange("(c p) n -> p c n", p=128)
        WBA = D["w_ba"].rearrange("(c p) n -> p c n", p=128)
        WBB = D["w_bb"].rearrange("(c p) n -> p c n", p=128)
        WOUT = D["w_out"].rearrange("(c p) n -> p c n", p=128)

    uT = sbuf(None, "uT", [128, 8, NTOK], BF16)
    ident = sbuf(None, "ident", [128, 128], BF16)
    tri = sbuf(None, "tris", [128, 128], F32)
    ones_bf = sbuf(None, "ones_bf", [128, 128], BF16)
    normw = sbuf(None, "normws", [128, 8], F32)
    fcol = sbuf(None, "fcols", [128, 2], F32)
    lams = sbuf(None, "lams", [128, 8], F32)
    gateb = sbuf(None, "gatebs", [128, 4], F32)
    negb = sbuf(None, "negb", [128, 4], F32)
    MARK_P = A.mark()
    lamv = sbuf(None, "lamvs", [128, 256], F32)
    lamt = sbuf(None, "lamt", [128, 128], F32)
    PS = [None] * 8
    PSB = [None] * 8
    if not rec:
        for i in range(8):
            PS[i] = top.enter_context(nc.psum_tensor(f"ps{i}", [128, 512], F32))
            PSB[i] = PS[i][:].bitcast(BF16)

    def pk(i, part="a"):
        return f"ps{i}"

    def pw(i):
        return [f"ps{i}"]

    def ld(dst, src, key, q="sp"):
        S.dma(q, lambda: S.eng(q).dma_start(out=dst(), in_=src()), writes=[key])

    ld(lambda: normw[:], lambda: D["normw"][:, :], "normw")
    ld(lambda: ident[:], lambda: D["identf"][:, :], "ident", q="pool")
    ld(lambda: tri[:], lambda: D["tri"][:, :], "tri")
    ld(lambda: fcol[:], lambda: D["fcol"][:, :], "fcol")
    ld(lambda: gateb[:], lambda: D["gateb"][:, :], "gateb")
    ld(lambda: lamv[:], lambda: D["lamv"][0:1, :].partition_broadcast(128), "lamv")
    S.op("pool", lambda: nc.gpsimd.memset(ones_bf[:], 1.0), writes=["ones_bf"])
    S.op("pool", lambda: nc.gpsimd.tensor_scalar(out=negb[:], in0=gateb[:], scalar1=-1.0, scalar2=None,
                                                 op0=ALU.mult), reads=["gateb"], writes=["negb"])
    S.op("dve", lambda: nc.vector.tensor_tensor(out=lamt[:, 0:64], in0=lamv[:, 0:64], in1=lamv[:, 64:128],
                                                op=ALU.mult), reads=["lamv"], writes=["lamt0"])
    S.op("dve", lambda: nc.vector.tensor_tensor(out=lamt[:, 64:128], in0=lamv[:, 128:192], in1=lamv[:, 192:256],
                                                op=ALU.mult), reads=["lamv"], writes=["lamt1"])
    S.op("dve", lambda: nc.vector.reduce_sum(out=lams[:, 0:1], in_=lamt[:, 0:64], axis=AX.X),
         reads=["lamt0"], writes=["lams0"])
    S.op("dve", lambda: nc.vector.reduce_sum(out=lams[:, 1:2], in_=lamt[:, 64:128], axis=AX.X),
         reads=["lamt1"], writes=["lams1"])
    S.op("act", lambda: nc.scalar.activation(out=lams[:, 2:4], in_=lams[:, 0:2], func=AF.Exp),
         reads=["lams0", "lams1"], writes=["lams23"])
    S.op("dve", lambda: nc.vector.tensor_tensor(out=lams[:, 5:6], in0=lams[:, 3:4], in1=lams[:, 2:3],
                                                op=ALU.subtract), reads=["lams23"], writes=["lams5"])
    S.op("dve", lambda: nc.vector.tensor_scalar(out=lams[:, 4:5], in0=lams[:, 5:6], scalar1=-0.2, scalar2=None,
                                                op0=ALU.add), reads=["lams5"], writes=["neglam"])

    A.reset(MARK_P)
    o_b = sbuf(None, "o_b", [128, NOWN, 1024], BF16)
    MARK_P1 = A.mark()
    gnw = sbuf(None, "gnws", [128, 256], F32)
    w2b = sbuf(None, "w2b", [128, 512], BF16)
    ld(lambda: gnw[:], lambda: D["gnw"][0:1, :].partition_broadcast(128), "gnw")
    ld(lambda: w2b[0:16, :], lambda: D["w2"][:, :], "w2b", q="pool")
    w_gq = sbuf(None, "w_gq", [128, 8, 512], BF16)
    w_gk = sbuf(None, "w_gk", [128, 8, 512], BF16)
    w_gv = sbuf(None, "w_gv", [128, 8, 1024], BF16)
    w_glr = sbuf(None, "w_glr", [128, 8, 16], BF16)
    for c in range(8):
        S.dma("pool", lambda c=c: nc.gpsimd.dma_start(out=w_gk[:, c, :], in_=WIN[:, c, C_GK:C_GK + 512]), writes=["w_gk"])
        S.dma("pool", lambda c=c: nc.gpsimd.dma_start(out=w_gv[:, c, :], in_=WIN[:, c, C_GV:C_GV + 1024]), writes=["w_gv"])
        S.dma("pool", lambda c=c: nc.gpsimd.dma_start(out=w_gq[:, c, :], in_=WIN[:, c, C_GQ:C_GQ + 512]), writes=["w_gq"])
    S.dma("pool", lambda: nc.gpsimd.dma_start(out=w_glr[:], in_=WIN[:, :, C_GLR:C_GLR + 16]), writes=["w_glr"])
    MARK_G = A.mark()
    A.reset(ARENA_BYTES - 57344)
    assert A.top >= MARK_G
    xs = [sbuf(None, f"xs{i}", [128, 8, 512], F32) for i in range(2)]
    sq = [sbuf(None, f"sq{i}", [128, 8, 512], BF16) for i in range(2)]
    rb = [sbuf(None, f"rb{i}", [128, 512], F32) for i in range(2)]
    chunks = [(0, 16)]
    t = 16
    while t < NTOK:
        n = min(512, NTOK - t)
        chunks.append((t, n))
        t += n
    for ci, (t0, n) in enumerate(chunks):
        b = ci % 2
        S.dma("sp", lambda b=b, t0=t0, n=n: nc.sync.dma_start(out=xs[b][:, :, 0:n], in_=D["xT"][:, :, t0:t0 + n]),
              writes=[f"xs{b}"])
        S.op("act", lambda b=b, n=n: nc.scalar.activation(out=sq[b][:, :, 0:n], in_=xs[b][:, :, 0:n], func=AF.Square),
             reads=[f"xs{b}"], writes=[f"sq{b}"])
        for c in range(8):
            S.op("pe", lambda b=b, n=n, c=c: nc.tensor.matmul(PS[b][:, 0:n], lhsT=ones_bf[:, :], rhs=sq[b][:, c, 0:n],
                                                             start=(c == 0), stop=(c == 7)),
                 reads=[f"sq{b}", "ones_bf"], writes=[pk(b)])
        S.op("act", lambda b=b, n=n: nc.scalar.activation(out=rb[b][:, 0:n], in_=PS[b][:, 0:n], func=AF.Ln,
                                                          bias=EPS, scale=1.0 / 1024.0),
             reads=[pk(b)], writes=[f"rb{b}"])
        S.op("act", lambda b=b, n=n: nc.scalar.activation(out=rb[b][:, 0:n], in_=rb[b][:, 0:n], func=AF.Exp, scale=-0.5),
             reads=[f"rb{b}"], writes=[f"rb{b}"])
        for c in range(8):
            S.op("dve", lambda b=b, n=n, c=c, t0=t0: nc.vector.scalar_tensor_tensor(
                out=uT[:, c, t0:t0 + n], in0=xs[b][:, c, 0:n], scalar=normw[:, c:c + 1], in1=rb[b][:, 0:n],
                op0=ALU.mult, op1=ALU.mult), reads=[f"xs{b}", f"rb{b}", "normw"], writes=["uT"])
    S.barrier()
    A.reset(MARK_G)
    if debug:
        S.dma("sp", lambda: nc.sync.dma_start(out=D["d_ut"][:, :, :], in_=uT[:]), reads=["uT"])
    if stop == "p0":
        S.wait_all_dma("sp")
        S.arena_peak = A.peak
        top.close()
        return

    glrT = sbuf(None, "glrT", [128, 512], BF16)
    spb = sbuf(None, "spb", [128, 512], F32)
    cb = sbuf(None, "cb", [128, 512], F32)
    enb = sbuf(None, "enb", [128, 512], F32)
    eb = sbuf(None, "eb", [128, 512], F32)
    ktmp = sbuf(None, "ktmp", [128, 512], F32)
    khT = sbuf(None, "khT", [128, 512], BF16)
    smask = sbuf(None, "smask", [128, 512], F32)
    dd = sbuf(None, "dd", [128, 4], F32)
    qtT = sbuf(None, "qtT", [128, 4, 512], BF16)
    ktT = sbuf(None, "ktT", [128, 4, 512], BF16)
    kh_own = sbuf(None, "kh_own", [128, 4, 4, 128], BF16)
    kh_oth = sbuf(None, "kh_oth", [128, 4, 4, 128], BF16)
    kh_pre = sbuf(None, "kh_pre", [128, 2, 4, 128], BF16)
    v_own = sbuf(None, "v_own", [128, 4, 1024], BF16)
    v_oth = sbuf(None, "v_oth", [128, 4, 1024], BF16)
    v_pre = sbuf(None, "v_pre", [128, 2, 1024], BF16)
    dec_own = sbuf(None, "dec_own", [128, 4, 4], F32)
    dec_oth = sbuf(None, "dec_oth", [128, 4, 4], F32)
    dec_pre = sbuf(None, "dec_pre", [128, 4, 2], F32)
    Sst = sbuf(None, "Sst", [128, 4, 256], F32)
    Sbf = sbuf(None, "Sbf", [128, 4, 256], BF16)
    a1 = sbuf(None, "a1", [128, 4], F32)
    AT = sbuf(None, "AT", [128, 512], BF16)
    ssq = sbuf(None, "ssq", [128, 8], F32)
    junk = sbuf(None, "junk", [128, 256], F32)

    S.op("pool", lambda: nc.gpsimd.memset(smask[:], 1.0), writes=["smask"])
    S.op("pool", lambda: nc.gpsimd.memset(smask[:].rearrange("p (b t) -> p b t", t=128)[:, :, 0:1], 0.0),
         writes=["smask"])

    DKS = 128 ** -0.5

    def gla_pre(kind, t0, nblk, bs):
        n = nblk * bs
        own = kind == "own"
        kh_t = {"own": kh_own, "oth": kh_oth, "meta": kh_pre, "X": kh_pre}[kind]
        v_t = {"own": v_own, "oth": v_oth, "meta": v_pre, "X": v_pre}[kind]
        dec_t = {"own": dec_own, "oth": dec_oth, "meta": dec_pre, "X": dec_pre}[kind]
        kkey = {"own": "kh_own", "oth": "kh_oth", "meta": "kh_pre0", "X": "kh_pre1"}[kind]
        vkey = {"own": "v_own", "oth": "v_oth", "meta": "v_pre0", "X": "v_pre1"}[kind]
        dkey = {"own": "dec_own", "oth": "dec_oth", "meta": "dec_pre0", "X": "dec_pre1"}[kind]
        boff = 1 if kind == "X" else 0
        for c in range(8):
            S.op("pe", lambda c=c: nc.tensor.matmul(PS[0][0:16, 0:n], lhsT=w_glr[:, c, :], rhs=uT[:, c, t0:t0 + n],
                                                    start=(c == 0), stop=(c == 7)),
                 reads=["w_glr", "uT"], writes=[pk(0)])
        S.op("dve", lambda: nc.vector.tensor_copy(glrT[0:16, 0:n], PS[0][0:16, 0:n]), reads=[pk(0)], writes=["glrT"])
        for j in range(nblk):
            for half in range(2):
                bank = 5 + half
                for c in range(8):
                    S.op("pe", lambda c=c, j=j, half=half, bank=bank: nc.tensor.matmul(
                        PS[bank][0:bs, :], lhsT=uT[:, c, t0 + j * bs:t0 + (j + 1) * bs],
                        rhs=w_gv[:, c, half * 512:(half + 1) * 512], start=(c == 0), stop=(c == 7)),
                        reads=["uT", "w_gv"], writes=pw(bank))
                S.op("act", lambda j=j, half=half, bank=bank: nc.scalar.copy(
                    v_t[0:bs, boff + j, half * 512:(half + 1) * 512], PS[bank][0:bs, :]),
                    reads=pw(bank), writes=[vkey])
        for h in range(4):
            hs = slice(h * 128, (h + 1) * 128)
            S.op("pe", lambda hs=hs: nc.tensor.matmul(PS[1][:, 0:n], lhsT=w2b[0:16, hs], rhs=glrT[0:16, 0:n],
                                                      start=True, stop=True),
                 reads=["w2b", "glrT"], writes=pw(1))
            S.op("act", lambda h=h: nc.scalar.activation(out=spb[:, 0:n], in_=PS[1][:, 0:n], func=AF.Exp,
                                                         bias=negb[:, h:h + 1], scale=-1.0),
                 reads=pw(1) + ["negb"], writes=["spb"])
            S.op("act", lambda: nc.scalar.activation(out=spb[:, 0:n], in_=spb[:, 0:n], func=AF.Ln, bias=1.0, scale=1.0),
                 reads=["spb"], writes=["spb"])
            S.op("dve", lambda: nc.vector.tensor_tensor_scan(out=cb[:, 0:n], data0=smask[:, 0:n], data1=spb[:, 0:n],
                                                             initial=0.0, op0=ALU.mult, op1=ALU.add),
                 reads=["smask", "spb"], writes=["cb"])
            S.op("act", lambda: nc.scalar.activation(out=enb[:, 0:n], in_=cb[:, 0:n], func=AF.Exp, scale=1.0 / 16.0),
                 reads=["cb"], writes=["enb"])
            for c in range(8):
                S.op("pe", lambda c=c, hs=hs: nc.tensor.matmul(PS[2][:, 0:n], lhsT=w_gk[:, c, hs], rhs=uT[:, c, t0:t0 + n],
                                                               start=(c == 0), stop=(c == 7)),
                     reads=["w_gk", "uT"], writes=pw(2))
            if own:
                S.op("act", lambda: nc.scalar.activation(out=eb[:, 0:n], in_=cb[:, 0:n], func=AF.Exp, scale=-1.0 / 16.0),
                     reads=["cb"], writes=["eb"])
                S.op("dve", lambda h=h: nc.vector.tensor_tensor(out=ktT[:, h, 0:n], in0=PS[2][:, 0:n], in1=enb[:, 0:n],
                                                                op=ALU.mult),
                     reads=pw(2) + ["enb"], writes=[f"ktT{h}"])
                S.op("pool", lambda h=h: nc.gpsimd.tensor_copy(
                    dec_t[:, h, 0:nblk], eb[:, 0:n].rearrange("p (b t) -> p b t", t=bs)[:, :, bs - 1]),
                    reads=["eb"], writes=[dkey])
                S.op("pool", lambda h=h: nc.gpsimd.tensor_tensor(
                    out=khT[:, 0:n].rearrange("p (b t) -> p b t", t=bs),
                    in0=ktT[:, h, 0:n].rearrange("p (b t) -> p b t", t=bs),
                    in1=eb[:, 0:n].rearrange("p (b t) -> p b t", t=bs)[:, :, bs - 1:bs].to_broadcast([128, nblk, bs]),
                    op=ALU.mult), reads=[f"ktT{h}", "eb"], writes=["khT"])
                for c in range(8):
                    S.op("pe", lambda c=c, hs=hs: nc.tensor.matmul(PS[3][:, 0:n], lhsT=w_gq[:, c, hs],
                                                                   rhs=uT[:, c, t0:t0 + n], start=(c == 0), stop=(c == 7)),
                         reads=["w_gq", "uT"], writes=[pk(3)])
                S.op("dve", lambda h=h: nc.vector.scalar_tensor_tensor(
                    out=qtT[:, h, 0:n], in0=PS[3][:, 0:n], scalar=DKS, in1=eb[:, 0:n], op0=ALU.mult, op1=ALU.mult),
                    reads=[pk(3), "eb"], writes=[f"qtT{h}"])
            else:
                S.op("act", lambda h=h: nc.scalar.activation(
                    out=dec_t[:, h, boff:boff + nblk], in_=cb[:, 0:n].rearrange("p (b t) -> p b t", t=bs)[:, :, bs - 1],
                    func=AF.Exp, scale=-1.0 / 16.0), reads=["cb"], writes=[dkey])
                S.op("dve", lambda: nc.vector.tensor_tensor(out=ktmp[:, 0:n], in0=PS[2][:, 0:n], in1=enb[:, 0:n],
                                                            op=ALU.mult), reads=pw(2) + ["enb"], writes=["ktmp"])
                S.op("pool", lambda h=h: nc.gpsimd.tensor_tensor(
                    out=khT[:, 0:n].rearrange("p (b t) -> p b t", t=bs),
                    in0=ktmp[:, 0:n].rearrange("p (b t) -> p b t", t=bs),
                    in1=dec_t[:, h, boff:boff + nblk].unsqueeze(2).to_broadcast([128, nblk, bs]),
                    op=ALU.mult), reads=["ktmp", dkey], writes=["khT"])
            for j in range(nblk):
                S.op("pe", lambda j=j: nc.tensor.transpose(PSB[4][0:bs, j * 128:(j + 1) * 128],
                                                           khT[:, j * bs:(j + 1) * bs], ident[:]),
                     reads=["khT", "ident"], writes=[pk(4)])
            S.op("dve", lambda h=h: nc.vector.tensor_copy(
                kh_t[0:bs, boff:boff + nblk, h, :],
                PSB[4][0:bs, 0:nblk * 128].rearrange("p (b d) -> p b d", d=128)),
                reads=[pk(4)], writes=[kkey])

    def state_update(h, kh_t, blk, v_t, vblk, dec_t, dblk, bs, kkey, vkey, dkey, first=False):
        bank = 1 + (h // 2)
        cs_ = slice((h % 2) * 256, (h % 2) * 256 + 256)
        hv = slice(h * 256, (h + 1) * 256)
        S.op("pe", lambda: nc.tensor.matmul(PS[bank][:, cs_], lhsT=kh_t[0:bs, blk, h, :], rhs=v_t[0:bs, vblk, hv],
                                            start=True, stop=True),
             reads=[kkey, vkey], writes=[pk(bank, "ab"[h % 2])])
        if first:
            S.op("dve", lambda: nc.vector.tensor_copy(Sst[:, h, :], PS[bank][:, cs_]),
                 reads=[pk(bank, "ab"[h % 2])], writes=[f"S{h}"])
        else:
            S.op("dve", lambda: nc.vector.scalar_tensor_tensor(
                out=Sst[:, h, :], in0=Sst[:, h, :], scalar=dec_t[:, h, dblk:dblk + 1], in1=PS[bank][:, cs_],
                op0=ALU.mult, op1=ALU.add), reads=[f"S{h}", dkey, pk(bank, "ab"[h % 2])], writes=[f"S{h}"])

    def cast_state(h):
        S.op("pool", lambda: nc.gpsimd.tensor_copy(Sbf[:, h, :], Sst[:, h, :]), reads=[f"S{h}"], writes=[f"Sbf{h}"])

    def early_exit():
        S.barrier()
        if debug:
            S.dma("sp", lambda: nc.sync.dma_start(out=D["d_ob"][:, :, :], in_=o_b[:]), reads=["o_b"])
        S.wait_all_dma("sp")
        S.arena_peak = A.peak
        top.close()

    gla_pre("meta", 0, 1, 16)
    if stop == "g1":
        return early_exit()
    gla_pre("X", OTH0, 1, 128)
    if stop == "g1b":
        return early_exit()
    for h in range(4):
        state_update(h, kh_pre, 0, v_pre, 0, dec_pre, 0, 16, "kh_pre0", "v_pre0", "dec_pre0", first=True)
    S.op("dve", lambda: nc.vector.tensor_scalar(out=a1[:], in0=dec_pre[:, :, 1], scalar1=fcol[:, 0:1],
                                                scalar2=fcol[:, 1:2], op0=ALU.mult, op1=ALU.add),
         reads=["dec_pre1", "fcol"], writes=["a1"])
    for h in range(4):
        bank = 1 + (h // 2)
        cs_ = slice((h % 2) * 256, (h % 2) * 256 + 256)
        hv = slice(h * 256, (h + 1) * 256)
        S.op("pe", lambda h=h, bank=bank, cs_=cs_, hv=hv: nc.tensor.matmul(
            PS[bank][:, cs_], lhsT=kh_pre[:, 1, h, :], rhs=v_pre[:, 1, hv], start=True, stop=True),
            reads=["kh_pre1", "v_pre1"], writes=[pk(bank, "ab"[h % 2])])
        S.op("dve", lambda h=h: nc.vector.tensor_scalar(out=junk[:], in0=Sst[:, h, :], scalar1=a1[:, h:h + 1],
                                                        scalar2=None, op0=ALU.mult),
             reads=[f"S{h}", "a1"], writes=["junk"])
        S.op("dve", lambda h=h, bank=bank, cs_=cs_: nc.vector.scalar_tensor_tensor(
            out=Sst[:, h, :], in0=PS[bank][:, cs_], scalar=fcol[:, 0:1], in1=junk[:], op0=ALU.mult, op1=ALU.add),
            reads=[pk(bank, "ab"[h % 2]), "junk", "fcol"], writes=[f"S{h}"])
        cast_state(h)

    if stop == "g2":
        return early_exit()
    for s in range(4):
        gla_pre("own", OWN0 + s * 512, 4, 128)
        gla_pre("oth", OTH0 + 128 + s * 512, 4, 128)
        if stop == "g3":
            return early_exit()
        for j in range(4):
            i = 4 * s + j
            bsl = slice(j * 128, (j + 1) * 128)
            for h in range(4):
                S.op("pe", lambda h=h, bsl=bsl: nc.tensor.matmul(PS[7][:, h * 128:(h + 1) * 128], lhsT=ktT[:, h, bsl],
                                                                 rhs=qtT[:, h, bsl], start=True, stop=True),
                     reads=[f"ktT{h}", f"qtT{h}"], writes=[pk(7)])
            S.op("dve", lambda: nc.vector.tensor_tensor(
                out=AT[:].rearrange("p (h t) -> p h t", t=128), in0=PS[7][:].rearrange("p (h t) -> p h t", t=128),
                in1=tri[:].unsqueeze(1).to_broadcast([128, 4, 128]), op=ALU.mult),
                reads=[pk(7), "tri"], writes=["AT"])
            if stop == "g4":
                return early_exit()
            for h in range(4):
                bank = 5 + (h // 2)
                cs_ = slice((h % 2) * 256, (h % 2) * 256 + 256)
                hv = slice(h * 256, (h + 1) * 256)
                S.op("pe", lambda h=h, bank=bank, cs_=cs_, hv=hv, j=j: nc.tensor.matmul(
                    PS[bank][:, cs_], lhsT=AT[:, h * 128:(h + 1) * 128], rhs=v_own[:, j, hv], start=True, stop=False),
                    reads=["AT", "v_own"], writes=[pk(bank, "ab"[h % 2])])
                S.op("pe", lambda h=h, bank=bank, cs_=cs_, bsl=bsl: nc.tensor.matmul(
                    PS[bank][:, cs_], lhsT=qtT[:, h, bsl], rhs=Sbf[:, h, :], start=False, stop=True),
                    reads=[f"qtT{h}", f"Sbf{h}"], writes=[pk(bank, "ab"[h % 2])])
            if stop == "g5":
                return early_exit()
            for h in range(4):
                state_update(h, kh_own, j, v_own, j, dec_own, j, 128, "kh_own", "v_own", "dec_own")
                cast_state(h)
            if stop == "g6":
                return early_exit()
            for h in range(4):
                bank = 5 + (h // 2)
                cs_ = slice((h % 2) * 256, (h % 2) * 256 + 256)
                S.op("act", lambda h=h, bank=bank, cs_=cs_: nc.scalar.activation(
                    out=junk[:], in_=PS[bank][:, cs_], func=AF.Square, accum_out=ssq[:, h:h + 1]),
                    reads=[pk(bank, "ab"[h % 2])], writes=["junk", f"ssq{h}"])
            S.op("act", lambda: nc.scalar.activation(out=ssq[:, 4:8], in_=ssq[:, 0:4], func=AF.Ln, bias=EPS,
                                                     scale=1.0 / 256.0),
                 reads=[f"ssq{h}" for h in range(4)], writes=["ssqln"])
            S.op("act", lambda: nc.scalar.activation(out=ssq[:, 4:8], in_=ssq[:, 4:8], func=AF.Exp, scale=-0.5),
                 reads=["ssqln"], writes=["ssqr"])
            for h in range(4):
                bank = 5 + (h // 2)
                cs_ = slice((h % 2) * 256, (h % 2) * 256 + 256)
                S.op("dve", lambda h=h, bank=bank, cs_=cs_, i=i: nc.vector.scalar_tensor_tensor(
                    out=o_b[:, i, h * 256:(h + 1) * 256], in0=PS[bank][:, cs_], scalar=ssq[:, 4 + h:5 + h],
                    in1=gnw[:], op0=ALU.mult, op1=ALU.mult),
                    reads=[pk(bank, "ab"[h % 2]), "ssqr", "gnw"], writes=["o_b"])
            if stop == "g7":
                return early_exit()
            for h in range(4):
                state_update(h, kh_oth, j, v_oth, j, dec_oth, j, 128, "kh_oth", "v_oth", "dec_oth")
                cast_state(h)
            if stop == "g8":
                return early_exit()
        if stop == "g9":
            return early_exit()
        if stop == "g10" and s == 1:
            return early_exit()
    S.barrier()
    A.reset(MARK_P1)
    if debug:
        S.dma("sp", lambda: nc.sync.dma_start(out=D["d_ob"][:, :, :], in_=o_b[:]), reads=["o_b"])
    if stop == "gla":
        S.wait_all_dma("sp")
        S.arena_peak = A.peak
        top.close()
        return

    o_a = sbuf(None, "o_a", [128, NOWN, 1024], BF16)
    MARK_P2 = A.mark()
    cs16 = sbuf(None, "cs16s", [128, NBLK, 16], F32)
    sn16 = sbuf(None, "sn16s", [128, NBLK, 16], F32)
    sw8 = sbuf(None, "sw8", [128, 128], F32)
    ld(lambda: cs16[:], lambda: D["cs16"][:, :, :], "cs16")
    ld(lambda: sn16[:], lambda: D["sn16"][:, :, :], "sn16")
    ld(lambda: sw8[:], lambda: D["sublnw"][0:1, :].partition_broadcast(128), "sw8")
    S.op("pool", lambda: nc.gpsimd.tensor_scalar(out=sw8[:], in0=sw8[:], scalar1=0.8, scalar2=None,
                                                 op0=ALU.mult), reads=["sw8"], writes=["sw8"])
    wq_sb = [sbuf(None, f"wq{i}", [128, 8, 256], BF16) for i in range(2)]
    wk_sb = sbuf(None, "wk", [128, 8, 256], BF16)
    wv_sb = sbuf(None, "wv", [128, 8, 256], BF16)
    KT = sbuf(None, "KT", [128, 2, NTOK], BF16)
    Vaug = sbuf(None, "Vaug", [128, NBLK, 2, 130], BF16)
    Pt = [[sbuf(None, f"Pt{b}{r}", [128, 512], BF16) for r in range(2)] for b in range(2)]
    kf = [sbuf(None, f"kf{i}", [128, 256], F32) for i in range(3)]
    kbb = [sbuf(None, f"kbb{i}", [128, 256], BF16) for i in range(3)]
    rt1 = [sbuf(None, f"rt1{i}", [128, 4, 16], F32) for i in range(3)]
    rt2 = [sbuf(None, f"rt2{i}", [128, 4, 16], F32) for i in range(3)]
    o2 = sbuf(None, "o2", [128, 2, 128], F32)
    ssa = sbuf(None, "ssa", [128, 4], F32)
    ssap = [ssa, sbuf(None, "ssa_b", [128, 4], F32)]
    oraw = [sbuf(None, f"oraw{k}", [128, 258], F32) for k in range(4)]
    rlr = [sbuf(None, f"rlr{k}", [128, 2], F32) for k in range(4)]
    t2r = [sbuf(None, f"t2r{k}", [128, 128], F32) for k in range(4)]
    o2p = [o2, sbuf(None, "o2_b", [128, 2, 128], F32)]

    S.op("pool", lambda: nc.gpsimd.memset(Vaug[:, :, :, 128:130], 1.0), writes=["Vones"])
    S.op("pool", lambda: nc.gpsimd.memset(Vaug[:, 0, :, :], 0.0), reads=["Vones"], writes=["Vones", "V0"])
    S.op("pool", lambda: nc.gpsimd.tensor_copy(Vaug[:, 0, :, 128:129], tri[:, 15:16].unsqueeze(1).to_broadcast([128, 2, 1])),
         reads=["tri", "Vones"], writes=["Vones"])
    S.op("pool", lambda: nc.gpsimd.tensor_copy(Vaug[:, 1, :, 128:129], fcol[:, 0:1].unsqueeze(1).to_broadcast([128, 2, 1])),
         reads=["fcol", "Vones"], writes=["Vones"])

    def blk_t0(blk):
        if blk == 0:
            return 0, 16
        if blk <= NOTH:
            return OTH0 + (blk - 1) * 128, 128
        return OWN0 + (blk - 1 - NOTH) * 128, 128

    rope_n = [0]
    zeros_bf = sbuf(None, "zeros_bf", [128, 128], BF16)
    S.op("pool", lambda: nc.gpsimd.memset(zeros_bf[:], 0.0), writes=["zeros_bf"])
    QTz = [sbuf(None, f"QTz{k}", [128, 2, 256], BF16) for k in range(2)]
    for k in range(2):
        S.op("pool", lambda k=k: nc.gpsimd.memset(QTz[k][:], 0.0), writes=[f"QT{k}"])

    def rope(src_bank, blk, bs, cast_eng="dve"):
        r = rope_n[0] % 3
        rope_n[0] += 1
        S.op("dve", lambda: nc.vector.tensor_copy(kf[r][0:bs, :], PS[src_bank][0:bs, 0:256]),
             reads=[pk(src_bank)], writes=[f"kf{r}"])
        if cast_eng == "act":
            S.op("act", lambda: nc.scalar.copy(kbb[r][0:bs, :], kf[r][0:bs, :]),
                 reads=[f"kf{r}"], writes=[f"kbb{r}"])
        else:
            S.op("dve", lambda: nc.vector.tensor_copy(kbb[r][0:bs, :], kf[r][0:bs, :]), reads=[f"kf{r}"], writes=[f"kbb{r}"])
        rv = lambda t_: t_[0:bs, :].rearrange("p (g d) -> p g d", g=4)[:, :, 0:16]
        S.op("pool", lambda: nc.gpsimd.tensor_tensor(out=rt1[r][0:bs], in0=rv(kf[r]),
                                                     in1=cs16[0:bs, blk:blk + 1, :].to_broadcast([bs, 4, 16]), op=ALU.mult),
             reads=[f"kf{r}", "cs16"], writes=[f"rt1{r}"])
        S.op("pool", lambda: nc.gpsimd.tensor_tensor(out=rt2[r][0:bs, :, 0:8], in0=rv(kf[r])[:, :, 8:16],
                                                     in1=sn16[0:bs, blk:blk + 1, 0:8].to_broadcast([bs, 4, 8]), op=ALU.mult),
             reads=[f"kf{r}", "sn16"], writes=[f"rt2a{r}"])
        S.op("pool", lambda: nc.gpsimd.tensor_tensor(out=rt2[r][0:bs, :, 8:16], in0=rv(kf[r])[:, :, 0:8],
                                                     in1=sn16[0:bs, blk:blk + 1, 8:16].to_broadcast([bs, 4, 8]), op=ALU.mult),
             reads=[f"kf{r}", "sn16"], writes=[f"rt2b{r}"])
        S.op("pool", lambda: nc.gpsimd.tensor_tensor(out=rv(kbb[r]), in0=rt1[r][0:bs], in1=rt2[r][0:bs], op=ALU.add),
             reads=[f"rt1{r}", f"rt2a{r}", f"rt2b{r}", f"kbb{r}"], writes=[f"kbb{r}"])
        return r

    def transposes(r, bs, bank, dst_fn, dkey):
        for h in range(2):
            S.op("pe", lambda h=h: nc.tensor.transpose(PSB[bank][:, h * 128:h * 128 + bs],
                                                       kbb[r][0:bs, h * 128:(h + 1) * 128], ident[0:bs, 0:bs]),
                 reads=[f"kbb{r}", "ident"], writes=[pk(bank)])
        S.op("dve", lambda: nc.vector.tensor_copy(
            dst_fn(), PSB[bank][:, 0:256].rearrange("p (h t) -> p h t", t=128)[:, :, 0:bs]),
            reads=[pk(bank)], writes=[dkey])

    def load_w(dst, col0, key):
        S.dma("pool", lambda: nc.gpsimd.dma_start(out=dst[:], in_=WIN[:, :, col0:col0 + 256]), writes=[key])

    load_w(wk_sb, C_AK, "wk")
    load_w(wv_sb, C_AV, "wv")
    load_w(wq_sb[0], C_AQ, "wq0")
    for hp in range(4):
        wq = wq_sb[hp % 2]
        wqk = f"wq{hp % 2}"
        if hp + 1 < 4:
            load_w(wq_sb[(hp + 1) % 2], C_AQ + (hp + 1) * 256, f"wq{(hp + 1) % 2}")
        prev = None
        for blk in range(NBLK):
            t0, bs = blk_t0(blk)
            kbank = (6, 2)[blk % 2]
            vbank = (7, 3)[blk % 2]
            for c in range(8):
                S.op("pe", lambda c=c, t0=t0, bs=bs, kbank=kbank: nc.tensor.matmul(
                    PS[kbank][0:bs, 0:256], lhsT=uT[:, c, t0:t0 + bs], rhs=wk_sb[:, c, :], start=(c == 0), stop=(c == 7)),
                    reads=["uT", "wk"], writes=[pk(kbank)])
            r = rope(kbank, blk, bs, cast_eng="act")
            for c in range(8):
                S.op("pe", lambda c=c, t0=t0, bs=bs, vbank=vbank: nc.tensor.matmul(
                    PS[vbank][0:bs, 0:256], lhsT=uT[:, c, t0:t0 + bs], rhs=wv_sb[:, c, :], start=(c == 0), stop=(c == 7)),
                    reads=["uT", "wv"], writes=[pk(vbank)])
            S.op("act", lambda blk=blk, bs=bs, vbank=vbank: nc.scalar.copy(
                Vaug[0:bs, blk, :, 0:128], PS[vbank][0:bs, 0:256].rearrange("p (h d) -> p h d", d=128)),
                reads=[pk(vbank)], writes=[f"V{blk}"])
            if prev is not None:
                pr, pbs, pt0, pblk = prev
                transposes(pr, pbs, (0, 1)[pblk % 2], lambda pt0=pt0, pbs=pbs: KT[:, :, pt0:pt0 + pbs], f"KT{pblk}")
            prev = (r, bs, t0, blk)
        pr, pbs, pt0, pblk = prev
        transposes(pr, pbs, (0, 1)[pblk % 2], lambda: KT[:, :, pt0:pt0 + pbs], f"KT{pblk}")
        if hp + 1 < 4:
            load_w(wk_sb, C_AK + (hp + 1) * 256, "wk")
            load_w(wv_sb, C_AV + (hp + 1) * 256, "wv")

        def q_proj(i):
            t0q = OWN0 + i * 128
            for c in range(8):
                S.op("pe", lambda c=c: nc.tensor.matmul(PS[6][:, 0:256], lhsT=uT[:, c, t0q:t0q + 128],
                                                        rhs=wq[:, c, :], start=(c == 0), stop=(c == 7)),
                     reads=["uT", wqk], writes=[pk(6)])
            return rope(6, 1 + NOTH + i, 128)

        def q_tr(i, r):
            for h in range(2):
                S.op("pe", lambda h=h: nc.tensor.transpose(PSB[7][:, h * 128:(h + 1) * 128],
                                                           kbb[r][:, h * 128:(h + 1) * 128], ident[:, :]),
                     reads=[f"kbb{r}", "ident"], writes=[pk(7)])
            S.op("dve", lambda: nc.vector.tensor_copy(
                QTz[i % 2][0:64, :, 0:128], PSB[7][0:64, 0:256].rearrange("p (h t) -> p h t", t=128)),
                reads=[pk(7)], writes=[f"QT{i % 2}"])
            S.op("dve", lambda: nc.vector.tensor_copy(
                QTz[i % 2][64:128, :, 128:256], PSB[7][64:128, 0:256].rearrange("p (h t) -> p h t", t=128)),
                reads=[pk(7)], writes=[f"QT{i % 2}"])

        items = []
        for i in range(NOWN):
            qblk = 1 + NOTH + i
            for h in range(2):
                oth = [(1 + j, OTH0 + j * 128, 128) for j in range(i + 1)]
                own = [(1 + NOTH + j, OWN0 + j * 128, 128) for j in range(i + 1)]
                lst = [own[-1], (0, 0, 128)] + oth + own[:-1]
                groups = [lst[a:a + 2] for a in range(0, len(lst), 2)]
                for gi, grp in enumerate(groups):
                    items.append(dict(i=i, h=h, grp=grp, first=(gi == 0), last=(gi == len(groups) - 1), qblk=qblk))

        def emit_st(it, r):
            i, h, grp = it["i"], it["h"], it["grp"]
            qt = QTz[i % 2]
            for s_, (kb, kt0, kbs) in enumerate(grp):
                bank = r
                c0 = (s_ % 2) * 256
                S.op("pe", lambda bank=bank, c0=c0, kt0=kt0, kbs=kbs: nc.tensor.matmul(
                    PS[bank][0:kbs, c0:c0 + 256], lhsT=KT[:, h, kt0:kt0 + kbs], rhs=qt[:, h, :], start=True, stop=True),
                    reads=[f"KT{kb}", f"QT{i % 2}"], writes=[pk(bank)])

        def emit_exp_pv(it, r, ob):
            i, h, grp, qblk = it["i"], it["h"], it["grp"], it["qblk"]
            bs = grp[0][2]
            ncol = len(grp) * 128 if bs == 128 else 128
            nk = len(grp)
            ptile = Pt[r // 2][r % 2]
            pkey = f"Pt{r}"
            S.op("act", lambda: nc.scalar.activation(
                out=ptile[0:bs, 0:nk * 256], in_=PS[r][0:bs, 0:nk * 256], func=AF.Exp, scale=0.125),
                reads=[pk(r)], writes=[pkey])
            if grp[0][0] == qblk:
                c0 = 0
                S.op("dve", lambda c0=c0: nc.vector.tensor_tensor(
                    out=ptile[:, c0:c0 + 256].rearrange("p (b t) -> p b t", t=128),
                    in0=ptile[:, c0:c0 + 256].rearrange("p (b t) -> p b t", t=128),
                    in1=tri[:].unsqueeze(1).to_broadcast([128, 2, 128]), op=ALU.mult),
                    reads=[pkey, "tri"], writes=[pkey])
            for s_, (kb, kt0, kbs) in enumerate(grp):
                c0 = s_ * 256
                for b in range(2):
                    lastmm = it["last"] and s_ == len(grp) - 1 and b == 1
                    S.op("pe", lambda b=b, c0=c0, kb=kb, kbs=kbs, lastmm=lastmm: nc.tensor.matmul(
                        PS[4 + ob][:, b * 129:(b + 1) * 129], lhsT=ptile[0:kbs, c0 + b * 128:c0 + (b + 1) * 128],
                        rhs=Vaug[0:kbs, kb, h, 0:129], start=(it["first"] and s_ == 0 and b == 0), stop=lastmm,
                        skip_group_check=True),
                        reads=[pkey, f"V{kb}", "Vones"], writes=[pk(4 + ob)])

        RING = 4
        dq = []
        bcount = [0]

        def flush(upto=None, nmax=None):
            n = 0
            while dq and (upto is None or dq[0][0] <= upto) and (nmax is None or n < nmax):
                _, e_, fn_, rd_, wr_ = dq.pop(0)
                S.op(e_, fn_, reads=rd_, writes=wr_)
                n += 1

        def finalize(i, h, ob):
            bi = bcount[0]
            bcount[0] += 1
            flush(upto=bi - RING)
            k = bi % RING
            kb_ = (bi // 2) % 2 if False else (i % 2)
            S.op("dve", lambda: nc.vector.tensor_copy(oraw[k][:, :], PS[4 + ob][:, 0:258]),
                 reads=[pk(4 + ob)], writes=[f"oraw{k}"])
            pb = i % 2
            D_ = lambda e_, fn_, rd_, wr_: dq.append((bi, e_, fn_, rd_, wr_))
            D_("dve", lambda: nc.vector.reciprocal(rlr[k][:, 0:1], oraw[k][:, 128:129]), [f"oraw{k}"], [f"rl0{k}"])
            D_("dve", lambda: nc.vector.reciprocal(rlr[k][:, 1:2], oraw[k][:, 257:258]), [f"oraw{k}"], [f"rl1{k}"])
            D_("dve", lambda: nc.vector.tensor_scalar(out=t2r[k][:], in0=oraw[k][:, 129:257], scalar1=rlr[k][:, 1:2],
                                                      scalar2=lams[:, 4:5], op0=ALU.mult, op1=ALU.mult),
               [f"oraw{k}", f"rl1{k}", "neglam"], [f"t2{k}"])
            D_("dve", lambda: nc.vector.scalar_tensor_tensor(
                out=o2p[pb][:, h, :], in0=oraw[k][:, 0:128], scalar=rlr[k][:, 0:1], in1=t2r[k][:], op0=ALU.mult, op1=ALU.add),
               [f"oraw{k}", f"rl0{k}", f"t2{k}"], [f"o2{pb}{h}"])
            D_("pool", lambda: nc.gpsimd.tensor_tensor(out=t2r[k][:], in0=o2p[pb][:, h, :], in1=o2p[pb][:, h, :], op=ALU.mult),
               [f"o2{pb}{h}"], [f"t2{k}"])
            D_("dve", lambda: nc.vector.reduce_sum(out=ssap[pb][:, h:h + 1], in_=t2r[k][:], axis=AX.X),
               [f"t2{k}"], [f"ssa{pb}{h}"])
            if h == 1:
                D_("act", lambda: nc.scalar.activation(out=ssap[pb][:, 2:4], in_=ssap[pb][:, 0:2], func=AF.Ln, bias=EPS,
                                                       scale=1.0 / 128.0), [f"ssa{pb}0", f"ssa{pb}1"], [f"ssaln{pb}"])
                D_("act", lambda: nc.scalar.activation(out=ssap[pb][:, 2:4], in_=ssap[pb][:, 2:4], func=AF.Exp, scale=-0.5),
                   [f"ssaln{pb}"], [f"ssar{pb}"])
                for h2 in range(2):
                    hh = 2 * hp + h2
                    D_("dve", lambda h2=h2, hh=hh: nc.vector.scalar_tensor_tensor(
                        out=o_a[:, i, hh * 128:(hh + 1) * 128], in0=o2p[pb][:, h2, :], scalar=ssap[pb][:, 2 + h2:3 + h2],
                        in1=sw8[:], op0=ALU.mult, op1=ALU.mult), [f"o2{pb}{h2}", f"ssar{pb}", "sw8"], [f"o_a{i}"])

        LA = 3
        qr = q_proj(0)
        q_tr(0, qr)
        for n0 in range(LA):
            emit_st(items[n0], n0 % 4)
        qr_next = None
        pending = None
        for n_, it in enumerate(items):
            r = n_ % 4
            i, h = it["i"], it["h"]
            ob = (2 * i + h) % 2
            if it["first"] and h == 0 and i + 1 < NOWN:
                qr_next = q_proj(i + 1)
            if n_ + LA < len(items):
                nx = items[n_ + LA]
                if nx["first"] and nx["h"] == 0:
                    q_tr(nx["i"], qr_next)
                emit_st(nx, (n_ + LA) % 4)
            emit_exp_pv(it, r, ob)
            flush(nmax=2)
            if it["last"]:
                finalize(i, h, ob)
        flush()
    if debug:
        S.dma("sp", lambda: nc.sync.dma_start(out=D["d_oa"][:, :, :], in_=o_a[:]), reads=[f"o_a{i}" for i in range(NOWN)])
    S.barrier()
    A.reset(MARK_P2)
    if stop == "da":
        S.wait_all_dma("sp")
        S.arena_peak = A.peak
        top.close()
        return

    oagT = sbuf(None, "oagT", [128, 8, 2048], BF16)
    obgT = sbuf(None, "obgT", [128, 8, 2048], BF16)
    wz = [sbuf(None, f"wz{i}", [128, 8, 128], BF16) for i in range(2)]
    gz = [sbuf(None, f"gz{i}", [128, 512], F32) for i in range(2)]

    def gate_stage(src, srckey, col0, dst, dkey):
        for c in range(8):
            wb = c % 2
            S.dma("pool", lambda c=c, wb=wb: nc.gpsimd.dma_start(out=wz[wb][:], in_=WIN[:, :, col0 + c * 128:col0 + (c + 1) * 128]),
                  writes=[f"wz{wb}"])
            for tg in range(4):
                g = (c * 4 + tg) % 2
                ts_ = slice(OWN0 + tg * 512, OWN0 + (tg + 1) * 512)
                for k in range(8):
                    S.op("pe", lambda k=k, wb=wb, ts_=ts_, g=g: nc.tensor.matmul(
                        PS[g][:, :], lhsT=wz[wb][:, k, :], rhs=uT[:, k, ts_], start=(k == 0), stop=(k == 7)),
                        reads=[f"wz{wb}", "uT"], writes=[pk(g)])
                S.op("act", lambda g=g: nc.scalar.activation(out=gz[g][:], in_=PS[g][:, :], func=AF.Silu),
                     reads=[pk(g)], writes=[f"gz{g}"])
                for j in range(4):
                    S.op("pe", lambda j=j, tg=tg, c=c, g=g: nc.tensor.transpose(
                        PSB[2 + g][:, j * 128:(j + 1) * 128], src[:, tg * 4 + j, c * 128:(c + 1) * 128], ident[:]),
                        reads=[srckey, "ident"], writes=[pk(2 + g)])
                S.op("dve", lambda g=g, c=c, tg=tg: nc.vector.tensor_tensor(
                    out=dst[:, c, tg * 512:(tg + 1) * 512], in0=PSB[2 + g][:, 0:512], in1=gz[g][:], op=ALU.mult),
                    reads=[pk(2 + g), f"gz{g}"], writes=[dkey])

    gate_stage(o_a, "o_a", C_AZ, oagT, "oagT")
    gate_stage(o_b, "o_b", C_GZ, obgT, "obgT")
    S.barrier()
    A.reset(MARK_P)
    mT = sbuf(None, "mT", [128, 8, 2048], BF16)
    MARK_C = A.mark()
    wc = [[sbuf(None, f"wc{n}{i}", [128, 8, 256], BF16) for i in range(2)] for n in range(4)]
    assert A.top <= MARK_P2
    A.reset(MARK_P2 + 65536)
    sg = [sbuf(None, f"sg{i}", [128, 512], F32) for i in range(2)]
    tt = [sbuf(None, f"tt{i}", [128, 512], F32) for i in range(2)]
    for d in range(8):
        wb = (d // 2) % 2
        dsub = d % 2
        if dsub == 0:
            ds_ = slice(d * 128, (d + 2) * 128)
            S.dma("pool", lambda wb=wb, ds_=ds_: nc.gpsimd.dma_start(out=wc[0][wb][:], in_=WBA[:, :, ds_]), writes=[f"wc0{wb}"])
            S.dma("pool", lambda wb=wb, ds_=ds_: nc.gpsimd.dma_start(out=wc[1][wb][:], in_=WBB[:, :, ds_]), writes=[f"wc1{wb}"])
            S.dma("pool", lambda wb=wb, d=d: nc.gpsimd.dma_start(out=wc[2][wb][:], in_=WIN[:, :, C_GA + d * 128:C_GA + (d + 2) * 128]),
                  writes=[f"wc2{wb}"])
            S.dma("pool", lambda wb=wb, d=d: nc.gpsimd.dma_start(out=wc[3][wb][:], in_=WIN[:, :, C_GB + d * 128:C_GB + (d + 2) * 128]),
                  writes=[f"wc3{wb}"])
        for tg in range(4):
            ts_ = slice(tg * 512, (tg + 1) * 512)
            tsu = slice(OWN0 + tg * 512, OWN0 + (tg + 1) * 512)
            ph4 = ((d * 4 + tg) % 2) * 4
            srcs = [(oagT, "oagT", ts_), (obgT, "obgT", ts_), (uT, "uT", tsu), (uT, "uT", tsu)]
            for n_ in range(4):
                src, skey, sl = srcs[n_]
                for k in range(8):
                    S.op("pe", lambda n_=n_, k=k, src=src, sl=sl, wb=wb, ph4=ph4: nc.tensor.matmul(
                        PS[ph4 + n_][:, :], lhsT=wc[n_][wb][:, k, dsub * 128:(dsub + 1) * 128], rhs=src[:, k, sl],
                        start=(k == 0), stop=(k == 7)),
                        reads=[f"wc{n_}{wb}", skey], writes=[pk(ph4 + n_)])
            for n_ in range(2):
                S.op("act", lambda n_=n_, ph4=ph4: nc.scalar.activation(out=sg[n_][:], in_=PS[ph4 + 2 + n_][:, :], func=AF.Sigmoid),
                     reads=[pk(ph4 + 2 + n_)], writes=[f"sg{n_}"])
                S.op("dve", lambda n_=n_, ph4=ph4: nc.vector.tensor_tensor(out=tt[n_][:], in0=PS[ph4 + n_][:, :], in1=sg[n_][:],
                                                                          op=ALU.mult),
                     reads=[pk(ph4 + n_), f"sg{n_}"], writes=[f"tt{n_}"])
            S.op("pool", lambda d=d, ts_=ts_: nc.gpsimd.tensor_tensor(out=mT[:, d, ts_], in0=tt[0][:], in1=tt[1][:], op=ALU.add),
                 reads=["tt0", "tt1"], writes=["mT"])
    S.barrier()
    A.reset(MARK_C)
    wout_sb = sbuf(None, "wout_sb", [128, 8, 1024], BF16)
    fnw = sbuf(None, "fnws", [128, 1024], F32)
    for c in range(8):
        S.dma("pool", lambda c=c: nc.gpsimd.dma_start(out=wout_sb[:, c, :], in_=WOUT[:, c, :]), writes=[f"wout{c}"])
    ld(lambda: fnw[:], lambda: D["fnw"][0:1, :].partition_broadcast(128), "fnw")
    NXR = 6
    xr = [sbuf(None, f"xr{i}", [128, 1024], F32) for i in range(NXR)]
    hres = [sbuf(None, f"hres{i}", [128, 1024], F32) for i in range(2)]
    osb = [sbuf(None, f"osb{i}", [128, 1024], F32) for i in range(2)]
    fss = sbuf(None, "fss", [128, 4], F32)
    junk3 = sbuf(None, "junk3", [128, 1024], F32)
    def load_x(i):
        S.dma("sp", lambda: nc.sync.dma_start(out=xr[i % NXR][:], in_=D["xo"][i, :, :]), writes=[f"xr{i % NXR}"])

    for i in range(NXR - 1):
        load_x(i)
    for i in range(NOWN):
        r = i % 2
        if i + NXR - 1 < NOWN:
            load_x(i + NXR - 1)
        for half in range(2):
            bank = (i % 2) * 2 + half
            for k in range(8):
                S.op("pe", lambda k=k, i=i, half=half, bank=bank: nc.tensor.matmul(
                    PS[bank][:, :], lhsT=mT[:, k, i * 128:(i + 1) * 128], rhs=wout_sb[:, k, half * 512:(half + 1) * 512],
                    start=(k == 0), stop=(k == 7)), reads=["mT", f"wout{k}"], writes=[pk(bank)])
            S.op("dve", lambda r=r, half=half, bank=bank: nc.vector.tensor_tensor(
                out=hres[r][:, half * 512:(half + 1) * 512], in0=PS[bank][:, :], in1=xr[i % NXR][:, half * 512:(half + 1) * 512],
                op=ALU.add), reads=[pk(bank), f"xr{i % NXR}"], writes=[f"hres{r}{half}"])
        S.op("act", lambda r=r: nc.scalar.activation(out=junk3[:], in_=hres[r][:], func=AF.Square, accum_out=fss[:, r:r + 1]),
             reads=[f"hres{r}0", f"hres{r}1"], writes=["junk3", f"fss{r}"])
        S.op("act", lambda r=r: nc.scalar.activation(out=fss[:, 2 + r:3 + r], in_=fss[:, r:r + 1], func=AF.Ln, bias=EPS,
                                                     scale=1.0 / 1024.0), reads=[f"fss{r}"], writes=[f"fsl{r}"])
        S.op("act", lambda r=r: nc.scalar.activation(out=fss[:, 2 + r:3 + r], in_=fss[:, 2 + r:3 + r], func=AF.Exp, scale=-0.5),
             reads=[f"fsl{r}"], writes=[f"fsr{r}"])
        S.op("dve", lambda r=r: nc.vector.scalar_tensor_tensor(
            out=osb[r][:], in0=hres[r][:], scalar=fss[:, 2 + r:3 + r], in1=fnw[:], op0=ALU.mult, op1=ALU.mult),
            reads=[f"hres{r}0", f"hres{r}1", f"fsr{r}", "fnw"], writes=[f"osb{r}"])
        S.dma("sp", lambda i=i, r=r: nc.sync.dma_start(out=D["out"][i, :, :], in_=osb[r][:]), reads=[f"osb{r}"])
    S.wait_all_dma("sp")
    S.arena_peak = A.peak
    top.close()


_CACHE = {}


def _program(debug=False, stop=None):
    key = ("prog", debug, stop)
    if key not in _CACHE:
        s1 = Sched(None)
        build(s1, debug, stop)
        nc = bass.Bass("TRN2", target_bir_lowering=False)
        s2 = Sched(nc, need=s1.need)
        build(s2, debug, stop)
        _CACHE[key] = nc
    return _CACHE[key]


def _rope_tables(p):
    half = 8
    inv_freq = (np.float32(500000.0) ** (-np.arange(half, dtype=np.float32) / np.float32(half))).astype(np.float32)
    pos = np.zeros((128, NBLK), np.float32)
    tt = np.arange(128, dtype=np.float32)
    pos[:, 0] = tt
    for j in range(NOTH):
        sb = 2 * j - 1 if p == 0 else 2 * j
        if 0 <= sb < 32:
            pos[:, 1 + j] = 16 + sb * 128 + tt
    for i in range(NOWN):
        pos[:, 1 + NOTH + i] = 16 + (2 * i + p) * 128 + tt
    ang = (pos[:, :, None] * inv_freq[None, None, :]).astype(np.float32)
    cos = np.cos(ang).astype(np.float32)
    sin = np.sin(ang).astype(np.float32)
    cs16 = np.concatenate([cos, cos], axis=-1)
    sn16 = np.concatenate([-sin, sin], axis=-1)
    return np.ascontiguousarray(cs16), np.ascontiguousarray(sn16)


def make_in_maps(x, meta_tokens, norm_w, w_in, lam_q1, lam_k1, lam_q2, lam_k2, da_subln_w,
                 gla_gate_w2, gla_gate_b, gla_norm_w, w_branch_a, w_branch_b, w_out, final_norm_w):
    f32 = np.float32
    x = np.asarray(x, f32)
    meta = np.asarray(meta_tokens, f32)
    shared = {
        "w_in": np.ascontiguousarray(np.asarray(w_in, f32)[0]),
        "w_ba": np.ascontiguousarray(np.asarray(w_branch_a, f32)[0]),
        "w_bb": np.ascontiguousarray(np.asarray(w_branch_b, f32)[0]),
        "w_out": np.ascontiguousarray(np.asarray(w_out, f32)[0]),
        "normw": np.ascontiguousarray(np.asarray(norm_w, f32)[0].reshape(8, 128).T),
        "lamv": np.concatenate([np.asarray(a, f32)[0] for a in (lam_q1, lam_k1, lam_q2, lam_k2)])[None, :].copy(),
        "sublnw": np.asarray(da_subln_w, f32)[0][None, :].copy(),
        "gnw": np.asarray(gla_norm_w, f32)[0][None, :].copy(),
        "fnw": np.asarray(final_norm_w, f32)[None, :].copy(),
        "gateb": np.ascontiguousarray(np.asarray(gla_gate_b, f32)[0].reshape(4, 128).T),
        "w2": np.ascontiguousarray(np.asarray(gla_gate_w2, f32)[0]),
        "tri": np.triu(np.ones((128, 128), f32)),
        "identf": np.eye(128, dtype=f32),
    }
    in_maps = []
    for core in range(8):
        b, p = core // 2, core % 2
        xb = x[b].reshape(32, 128, 1024)
        own = xb[p::2]
        zero = np.zeros((1, 128, 1024), f32)
        if p == 0:
            oth = np.concatenate([zero, xb[1::2]], axis=0)
        else:
            oth = np.concatenate([xb[0::2], zero], axis=0)
        allt = np.concatenate([meta, oth.reshape(-1, 1024), own.reshape(-1, 1024)], axis=0)
        xT = np.ascontiguousarray(allt.T.reshape(8, 128, NTOK).transpose(1, 0, 2))
        cs16, sn16 = _rope_tables(p)
        fc = np.zeros((128, 2), f32)
        fc[:, 0] = p
        fc[:, 1] = 1 - p
        m = dict(shared)
        m.update({"xT": xT, "xo": np.ascontiguousarray(own), "cs16": cs16, "sn16": sn16, "fcol": fc})
        in_maps.append(m)
    return in_maps


def kernel(**inputs):
    debug = bool(os.environ.get("KDEBUG"))
    nc = _program(debug)
    in_maps = make_in_maps(**inputs)
    res = run_bass_kernel_spmd(nc, in_maps, core_ids=list(range(8)))
    out = np.zeros((4, 4096, 1024), np.float32)
    for core in range(8):
        b, p = core // 2, core % 2
        oc = np.asarray(res.results[core]["out"], np.float32)
        out[b].reshape(32, 128, 1024)[p::2] = oc
    if debug:
        kernel.last_results = res.results
    return out
```

```python
import contextlib
import os
import numpy as np
import concourse.bass as bass
import concourse.mybir as mybir
from concourse.bass_utils import run_bass_kernel_spmd

F32 = mybir.dt.float32
BF16 = mybir.dt.bfloat16
AF = mybir.ActivationFunctionType
ALU = mybir.AluOpType
AX = mybir.AxisListType

ENGS = ("pe", "act", "dve", "pool", "sp")

NMETA = 16
NOTH = 17
NOWN = 16
OTH0 = NMETA
OWN0 = NMETA + NOTH * 128
NTOK = OWN0 + NOWN * 128
NBLK = 1 + NOTH + NOWN
EPS = 1e-5
C_AQ, C_AK, C_AV, C_AZ = 0, 1024, 2048, 3072
C_GQ, C_GK, C_GV, C_GZ, C_GLR, C_GA, C_GB = 4096, 4608, 5120, 6144, 7168, 7184, 8208
WCOLS = 9232


class Sched:
    N_DMA_SEMS = {"sp": 8, "pool": 6}

    def __init__(self, nc=None, need=None):
        self.nc = nc
        self.rec = nc is None
        self.need_in = need if need is not None else set()
        self.need = set()
        self.seq = {e: 0 for e in ENGS}
        self.sigcount = {e: 0 for e in ENGS}
        self.sigmap = {}
        self.waited = {e: {} for e in ENGS}
        self.lastw = {}
        self.reads = {}
        self.esems = None
        self.dsems = None
        self.dma_n = {q: 0 for q in self.N_DMA_SEMS}
        self.dma_waited = {e: {} for e in ENGS}
        self.ninstr = 0

    def eng(self, e):
        nc = self.nc
        return {"pe": nc.tensor, "act": nc.scalar, "dve": nc.vector,
                "pool": nc.gpsimd, "sp": nc.sync}[e]

    def _wait_tok(self, e, tok):
        if tok[0] == "eng":
            _, pe_, ps_ = tok
            if self.rec:
                self.need.add((pe_, ps_))
                return
            val = self.sigmap[(pe_, ps_)]
            if self.waited[e].get(pe_, 0) >= val:
                return
            self.waited[e][pe_] = val
            self.eng(e).wait_ge(self.esems[pe_], val)
        else:
            _, q, idx, val = tok
            if self.rec:
                return
            key = (q, idx)
            if self.dma_waited[e].get(key, 0) >= val:
                return
            self.dma_waited[e][key] = val
            self.eng(e).wait_ge(self.dsems[q][idx], val)

    def _deps(self, e, reads, writes):
        toks = []
        for k in reads:
            w = self.lastw.get(k)
            if w is not None:
                toks.append(w)
        for k in writes:
            w = self.lastw.get(k)
            if w is not None:
                toks.append(w)
            for r in self.reads.get(k, ()):
                toks.append(r)
        best = {}
        out = []
        for t in toks:
            if t[0] == "eng":
                if t[1] == "pe" and e == "pe":
                    continue
                if best.get(t[1], -1) < t[2]:
                    best[t[1]] = t[2]
            elif t not in out:
                out.append(t)
        for pe_, ps_ in best.items():
            out.append(("eng", pe_, ps_))
        return out

    def _commit(self, tok, reads, writes):
        for k in reads:
            lst = self.reads.setdefault(k, [])
            if tok[0] == "eng":
                lst[:] = [r for r in lst if not (r[0] == "eng" and r[1] == tok[1])]
            lst.append(tok)
        for k in writes:
            self.lastw[k] = tok
            self.reads[k] = []

    def op(self, e, fn, reads=(), writes=()):
        self.ninstr += 1
        psr = [k for k in reads if k.startswith("ps")]
        if psr:
            reads = [k for k in reads if not k.startswith("ps")]
            writes = list(writes) + psr
        for t in self._deps(e, reads, writes):
            self._wait_tok(e, t)
        s = self.seq[e]
        self.seq[e] += 1
        tok = ("eng", e, s)
        if not self.rec:
            ins = fn()
            if (e, s) in self.need_in:
                ins.then_inc(self.esems[e], 1)
                self.sigcount[e] += 1
                self.sigmap[(e, s)] = self.sigcount[e]
        self._commit(tok, reads, writes)
        return tok

    def dma(self, q, fn, reads=(), writes=()):
        self.ninstr += 1
        n = self.dma_n[q]
        P = self.N_DMA_SEMS[q]
        idx = n % P
        val = 16 * (n // P + 1)
        self.dma_n[q] += 1
        if n >= P:
            self._wait_tok(q, ("dma", q, idx, val - 16))
        for t in self._deps(q, reads, writes):
            self._wait_tok(q, t)
        tok = ("dma", q, idx, val)
        if not self.rec:
            fn().then_inc(self.dsems[q][idx], 16)
        self._commit(tok, reads, writes)
        return tok

    def wait_all_dma(self, e):
        for q, P in self.N_DMA_SEMS.items():
            n = self.dma_n[q]
            for idx in range(min(P, n)):
                cnt = (n - 1 - idx) // P + 1
                self._wait_tok(e, ("dma", q, idx, 16 * cnt))

    def barrier(self):
        toks = []
        for e in ("pe", "act", "dve", "pool"):
            if self.seq[e] > 0:
                toks.append(("eng", e, self.seq[e] - 1))
        for e in ENGS:
            for t in toks:
                if t[1] != e:
                    self._wait_tok(e, t)
            self.wait_all_dma(e)
        self.lastw = {}
        self.reads = {}


class Arena:
    def __init__(self, nbytes):
        self.nbytes = nbytes
        self.top = 0
        self.t = None
        self.peak = 0

    def mark(self):
        return self.top

    def reset(self, m):
        self.top = m

    def alloc(self, shape, dt):
        esz = 2 if dt == BF16 else 4
        n = int(np.prod(shape[1:]))
        nb = (n * esz + 63) // 64 * 64
        off = self.top
        self.top += nb
        self.peak = max(self.peak, self.top)
        assert self.top <= self.nbytes, f"arena overflow {self.top} > {self.nbytes}"
        if self.t is None:
            return None
        ap = self.t[:, off // 2: off // 2 + n * esz // 2]
        if dt == F32:
            ap = ap.bitcast(F32)
        if len(shape) == 3:
            ap = ap.rearrange("p (a b) -> p a b", a=shape[1])
        elif len(shape) == 4:
            ap = ap.rearrange("p (a b c) -> p a b c", a=shape[1], b=shape[2])
        return ap


ARENA_BYTES = 212736


def build(S, debug=False, stop=None):
    nc = S.nc
    rec = S.rec
    D = {}
    T = {}
    top = contextlib.ExitStack()

    A = Arena(ARENA_BYTES)
    if not rec:
        A.t = top.enter_context(nc.sbuf_tensor("arena", [128, ARENA_BYTES // 2], BF16))

    def sbuf(es, name, shape, dt):
        return A.alloc(shape, dt)

    if not rec:
        def din(name, shape, dt=F32):
            D[name] = nc.dram_tensor(name, shape, dt, kind="ExternalInput").ap()
        din("xT", [128, 8, NTOK])
        din("xo", [NOWN, 128, 1024])
        din("w_in", [1024, WCOLS])
        din("w_ba", [1024, 1024])
        din("w_bb", [1024, 1024])
        din("w_out", [1024, 1024])
        din("normw", [128, 8])
        din("lamv", [1, 256])
        din("sublnw", [1, 128])
        din("gnw", [1, 256])
        din("fnw", [1, 1024])
        din("gateb", [128, 4])
        din("w2", [16, 512])
        din("cs16", [128, NBLK, 16])
        din("sn16", [128, NBLK, 16])
        din("fcol", [128, 2])
        din("tri", [128, 128])
        din("identf", [128, 128])
        D["out"] = nc.dram_tensor("out", [NOWN, 128, 1024], F32, kind="ExternalOutput").ap()
        if debug:
            D["d_ut"] = nc.dram_tensor("d_ut", [128, 8, NTOK], BF16, kind="ExternalOutput").ap()
            D["d_ob"] = nc.dram_tensor("d_ob", [128, NOWN, 1024], BF16, kind="ExternalOutput").ap()
            D["d_oa"] = nc.dram_tensor("d_oa", [128, NOWN, 1024], BF16, kind="ExternalOutput").ap()
        S.esems = {e: top.enter_context(nc.semaphore("es_" + e)) for e in ENGS}
        S.dsems = {q: [top.enter_context(nc.semaphore(f"ds_{q}{i}")) for i in range(n)]
                   for q, n in S.N_DMA_SEMS.items()}
        WIN = D["w_in"].rearrange("(c p) n -> p c n", p=128)
        WBA = D["w_ba"].rearrange("(c p) n -> p c n", p=128)
        WBB = D["w_bb"].rearrange("(c p) n -> p c n", p=128)
        WOUT = D["w_out"].rearrange("(c p) n -> p c n", p=128)

    uT = sbuf(None, "uT", [128, 8, NTOK], BF16)
    ident = sbuf(None, "ident", [128, 128], BF16)
    tri = sbuf(None, "tris", [128, 128], F32)
    ones_bf = sbuf(None, "ones_bf", [128, 128], BF16)
    normw = sbuf(None, "normws", [128, 8], F32)
    fcol = sbuf(None, "fcols", [128, 2], F32)
    lams = sbuf(None, "lams", [128, 8], F32)
    gateb = sbuf(None, "gatebs", [128, 4], F32)
    negb = sbuf(None, "negb", [128, 4], F32)
    MARK_P = A.mark()
    lamv = sbuf(None, "lamvs", [128, 256], F32)
    lamt = sbuf(None, "lamt", [128, 128], F32)
    PS = [None] * 8
    PSB = [None] * 8
    if not rec:
        for i in range(8):
            PS[i] = top.enter_context(nc.psum_tensor(f"ps{i}", [128, 512], F32))
            PSB[i] = PS[i][:].bitcast(BF16)

    def pk(i, part="a"):
        return f"ps{i}"

    def pw(i):
        return [f"ps{i}"]

    def ld(dst, src, key, q="sp"):
        S.dma(q, lambda: S.eng(q).dma_start(out=dst(), in_=src()), writes=[key])

    ld(lambda: normw[:], lambda: D["normw"][:, :], "normw")
    ld(lambda: ident[:], lambda: D["identf"][:, :], "ident", q="pool")
    ld(lambda: tri[:], lambda: D["tri"][:, :], "tri")
    ld(lambda: fcol[:], lambda: D["fcol"][:, :], "fcol")
    ld(lambda: gateb[:], lambda: D["gateb"][:, :], "gateb")
    ld(lambda: lamv[:], lambda: D["lamv"][0:1, :].partition_broadcast(128), "lamv")
    S.op("pool", lambda: nc.gpsimd.memset(ones_bf[:], 1.0), writes=["ones_bf"])
    S.op("pool", lambda: nc.gpsimd.tensor_scalar(out=negb[:], in0=gateb[:], scalar1=-1.0, scalar2=None,
                                                 op0=ALU.mult), reads=["gateb"], writes=["negb"])
    S.op("dve", lambda: nc.vector.tensor_tensor(out=lamt[:, 0:64], in0=lamv[:, 0:64], in1=lamv[:, 64:128],
                                                op=ALU.mult), reads=["lamv"], writes=["lamt0"])
    S.op("dve", lambda: nc.vector.tensor_tensor(out=lamt[:, 64:128], in0=lamv[:, 128:192], in1=lamv[:, 192:256],
                                                op=ALU.mult), reads=["lamv"], writes=["lamt1"])
    S.op("dve", lambda: nc.vector.reduce_sum(out=lams[:, 0:1], in_=lamt[:, 0:64], axis=AX.X),
         reads=["lamt0"], writes=["lams0"])
    S.op("dve", lambda: nc.vector.reduce_sum(out=lams[:, 1:2], in_=lamt[:, 64:128], axis=AX.X),
         reads=["lamt1"], writes=["lams1"])
    S.op("act", lambda: nc.scalar.activation(out=lams[:, 2:4], in_=lams[:, 0:2], func=AF.Exp),
         reads=["lams0", "lams1"], writes=["lams23"])
    S.op("dve", lambda: nc.vector.tensor_tensor(out=lams[:, 5:6], in0=lams[:, 3:4], in1=lams[:, 2:3],
                                                op=ALU.subtract), reads=["lams23"], writes=["lams5"])
    S.op("dve", lambda: nc.vector.tensor_scalar(out=lams[:, 4:5], in0=lams[:, 5:6], scalar1=-0.2, scalar2=None,
                                                op0=ALU.add), reads=["lams5"], writes=["neglam"])

    A.reset(MARK_P)
    o_b = sbuf(None, "o_b", [128, NOWN, 1024], BF16)
    MARK_P1 = A.mark()
    gnw = sbuf(None, "gnws", [128, 256], F32)
    w2b = sbuf(None, "w2b", [128, 512], BF16)
    ld(lambda: gnw[:], lambda: D["gnw"][0:1, :].partition_broadcast(128), "gnw")
    ld(lambda: w2b[0:16, :], lambda: D["w2"][:, :], "w2b", q="pool")
    w_gq = sbuf(None, "w_gq", [128, 8, 512], BF16)
    w_gk = sbuf(None, "w_gk", [128, 8, 512], BF16)
    w_gv = sbuf(None, "w_gv", [128, 8, 1024], BF16)
    w_glr = sbuf(None, "w_glr", [128, 8, 16], BF16)
    for c in range(8):
        S.dma("pool", lambda c=c: nc.gpsimd.dma_start(out=w_gk[:, c, :], in_=WIN[:, c, C_GK:C_GK + 512]), writes=["w_gk"])
        S.dma("pool", lambda c=c: nc.gpsimd.dma_start(out=w_gv[:, c, :], in_=WIN[:, c, C_GV:C_GV + 1024]), writes=["w_gv"])
        S.dma("pool", lambda c=c: nc.gpsimd.dma_start(out=w_gq[:, c, :], in_=WIN[:, c, C_GQ:C_GQ + 512]), writes=["w_gq"])
    S.dma("pool", lambda: nc.gpsimd.dma_start(out=w_glr[:], in_=WIN[:, :, C_GLR:C_GLR + 16]), writes=["w_glr"])
    MARK_G = A.mark()
    A.reset(ARENA_BYTES - 57344)
    assert A.top >= MARK_G
    xs = [sbuf(None, f"xs{i}", [128, 8, 512], F32) for i in range(2)]
    sq = [sbuf(None, f"sq{i}", [128, 8, 512], BF16) for i in range(2)]
    rb = [sbuf(None, f"rb{i}", [128, 512], F32) for i in range(2)]
    chunks = [(0, 16)]
    t = 16
    while t < NTOK:
        n = min(512, NTOK - t)
        chunks.append((t, n))
        t += n
    for ci, (t0, n) in enumerate(chunks):
        b = ci % 2
        S.dma("sp", lambda b=b, t0=t0, n=n: nc.sync.dma_start(out=xs[b][:, :, 0:n], in_=D["xT"][:, :, t0:t0 + n]),
              writes=[f"xs{b}"])
        S.op("act", lambda b=b, n=n: nc.scalar.activation(out=sq[b][:, :, 0:n], in_=xs[b][:, :, 0:n], func=AF.Square),
             reads=[f"xs{b}"], writes=[f"sq{b}"])
        for c in range(8):
            S.op("pe", lambda b=b, n=n, c=c: nc.tensor.matmul(PS[b][:, 0:n], lhsT=ones_bf[:, :], rhs=sq[b][:, c, 0:n],
                                                             start=(c == 0), stop=(c == 7)),
                 reads=[f"sq{b}", "ones_bf"], writes=[pk(b)])
        S.op("act", lambda b=b, n=n: nc.scalar.activation(out=rb[b][:, 0:n], in_=PS[b][:, 0:n], func=AF.Ln,
                                                          bias=EPS, scale=1.0 / 1024.0),
             reads=[pk(b)], writes=[f"rb{b}"])
        S.op("act", lambda b=b, n=n: nc.scalar.activation(out=rb[b][:, 0:n], in_=rb[b][:, 0:n], func=AF.Exp, scale=-0.5),
             reads=[f"rb{b}"], writes=[f"rb{b}"])
        for c in range(8):
            S.op("dve", lambda b=b, n=n, c=c, t0=t0: nc.vector.scalar_tensor_tensor(
                out=uT[:, c, t0:t0 + n], in0=xs[b][:, c, 0:n], scalar=normw[:, c:c + 1], in1=rb[b][:, 0:n],
                op0=ALU.mult, op1=ALU.mult), reads=[f"xs{b}", f"rb{b}", "normw"], writes=["uT"])
    S.barrier()
    A.reset(MARK_G)
    if debug:
        S.dma("sp", lambda: nc.sync.dma_start(out=D["d_ut"][:, :, :], in_=uT[:]), reads=["uT"])
    if stop == "p0":
        S.wait_all_dma("sp")
        S.arena_peak = A.peak
        top.close()
        return

    glrT = sbuf(None, "glrT", [128, 512], BF16)
    spb = sbuf(None, "spb", [128, 512], F32)
    cb = sbuf(None, "cb", [128, 512], F32)
    enb = sbuf(None, "enb", [128, 512], F32)
    eb = sbuf(None, "eb", [128, 512], F32)
    ktmp = sbuf(None, "ktmp", [128, 512], F32)
    khT = sbuf(None, "khT", [128, 512], BF16)
    smask = sbuf(None, "smask", [128, 512], F32)
    dd = sbuf(None, "dd", [128, 4], F32)
    qtT = sbuf(None, "qtT", [128, 4, 512], BF16)
    ktT = sbuf(None, "ktT", [128, 4, 512], BF16)
    kh_own = sbuf(None, "kh_own", [128, 4, 4, 128], BF16)
    kh_oth = sbuf(None, "kh_oth", [128, 4, 4, 128], BF16)
    kh_pre = sbuf(None, "kh_pre", [128, 2, 4, 128], BF16)
    v_own = sbuf(None, "v_own", [128, 4, 1024], BF16)
    v_oth = sbuf(None, "v_oth", [128, 4, 1024], BF16)
    v_pre = sbuf(None, "v_pre", [128, 2, 1024], BF16)
    dec_own = sbuf(None, "dec_own", [128, 4, 4], F32)
    dec_oth = sbuf(None, "dec_oth", [128, 4, 4], F32)
    dec_pre = sbuf(None, "dec_pre", [128, 4, 2], F32)
    Sst = sbuf(None, "Sst", [128, 4, 256], F32)
    Sbf = sbuf(None, "Sbf", [128, 4, 256], BF16)
    a1 = sbuf(None, "a1", [128, 4], F32)
    AT = sbuf(None, "AT", [128, 512], BF16)
    ssq = sbuf(None, "ssq", [128, 8], F32)
    junk = sbuf(None, "junk", [128, 256], F32)

    S.op("pool", lambda: nc.gpsimd.memset(smask[:], 1.0), writes=["smask"])
    S.op("pool", lambda: nc.gpsimd.memset(smask[:].rearrange("p (b t) -> p b t", t=128)[:, :, 0:1], 0.0),
         writes=["smask"])

    DKS = 128 ** -0.5

    def gla_pre(kind, t0, nblk, bs):
        n = nblk * bs
        own = kind == "own"
        kh_t = {"own": kh_own, "oth": kh_oth, "meta": kh_pre, "X": kh_pre}[kind]
        v_t = {"own": v_own, "oth": v_oth, "meta": v_pre, "X": v_pre}[kind]
        dec_t = {"own": dec_own, "oth": dec_oth, "meta": dec_pre, "X": dec_pre}[kind]
        kkey = {"own": "kh_own", "oth": "kh_oth", "meta": "kh_pre0", "X": "kh_pre1"}[kind]
        vkey = {"own": "v_own", "oth": "v_oth", "meta": "v_pre0", "X": "v_pre1"}[kind]
        dkey = {"own": "dec_own", "oth": "dec_oth", "meta": "dec_pre0", "X": "dec_pre1"}[kind]
        boff = 1 if kind == "X" else 0
        for c in range(8):
            S.op("pe", lambda c=c: nc.tensor.matmul(PS[0][0:16, 0:n], lhsT=w_glr[:, c, :], rhs=uT[:, c, t0:t0 + n],
                                                    start=(c == 0), stop=(c == 7)),
                 reads=["w_glr", "uT"], writes=[pk(0)])
        S.op("dve", lambda: nc.vector.tensor_copy(glrT[0:16, 0:n], PS[0][0:16, 0:n]), reads=[pk(0)], writes=["glrT"])
        for j in range(nblk):
            for half in range(2):
                bank = 5 + half
                for c in range(8):
                    S.op("pe", lambda c=c, j=j, half=half, bank=bank: nc.tensor.matmul(
                        PS[bank][0:bs, :], lhsT=uT[:, c, t0 + j * bs:t0 + (j + 1) * bs],
                        rhs=w_gv[:, c, half * 512:(half + 1) * 512], start=(c == 0), stop=(c == 7)),
                        reads=["uT", "w_gv"], writes=pw(bank))
                S.op("act", lambda j=j, half=half, bank=bank: nc.scalar.copy(
                    v_t[0:bs, boff + j, half * 512:(half + 1) * 512], PS[bank][0:bs, :]),
                    reads=pw(bank), writes=[vkey])
        for h in range(4):
            hs = slice(h * 128, (h + 1) * 128)
            S.op("pe", lambda hs=hs: nc.tensor.matmul(PS[1][:, 0:n], lhsT=w2b[0:16, hs], rhs=glrT[0:16, 0:n],
                                                      start=True, stop=True),
                 reads=["w2b", "glrT"], writes=pw(1))
            S.op("act", lambda h=h: nc.scalar.activation(out=spb[:, 0:n], in_=PS[1][:, 0:n], func=AF.Exp,
                                                         bias=negb[:, h:h + 1], scale=-1.0),
                 reads=pw(1) + ["negb"], writes=["spb"])
            S.op("act", lambda: nc.scalar.activation(out=spb[:, 0:n], in_=spb[:, 0:n], func=AF.Ln, bias=1.0, scale=1.0),
                 reads=["spb"], writes=["spb"])
            S.op("dve", lambda: nc.vector.tensor_tensor_scan(out=cb[:, 0:n], data0=smask[:, 0:n], data1=spb[:, 0:n],
                                                             initial=0.0, op0=ALU.mult, op1=ALU.add),
                 reads=["smask", "spb"], writes=["cb"])
            S.op("act", lambda: nc.scalar.activation(out=enb[:, 0:n], in_=cb[:, 0:n], func=AF.Exp, scale=1.0 / 16.0),
                 reads=["cb"], writes=["enb"])
            for c in range(8):
                S.op("pe", lambda c=c, hs=hs: nc.tensor.matmul(PS[2][:, 0:n], lhsT=w_gk[:, c, hs], rhs=uT[:, c, t0:t0 + n],
                                                               start=(c == 0), stop=(c == 7)),
                     reads=["w_gk", "uT"], writes=pw(2))
            if own:
                S.op("act", lambda: nc.scalar.activation(out=eb[:, 0:n], in_=cb[:, 0:n], func=AF.Exp, scale=-1.0 / 16.0),
                     reads=["cb"], writes=["eb"])
                S.op("dve", lambda h=h: nc.vector.tensor_tensor(out=ktT[:, h, 0:n], in0=PS[2][:, 0:n], in1=enb[:, 0:n],
                                                                op=ALU.mult),
                     reads=pw(2) + ["enb"], writes=[f"ktT{h}"])
                S.op("pool", lambda h=h: nc.gpsimd.tensor_copy(
                    dec_t[:, h, 0:nblk], eb[:, 0:n].rearrange("p (b t) -> p b t", t=bs)[:, :, bs - 1]),
                    reads=["eb"], writes=[dkey])
                S.op("pool", lambda h=h: nc.gpsimd.tensor_tensor(
                    out=khT[:, 0:n].rearrange("p (b t) -> p b t", t=bs),
                    in0=ktT[:, h, 0:n].rearrange("p (b t) -> p b t", t=bs),
                    in1=eb[:, 0:n].rearrange("p (b t) -> p b t", t=bs)[:, :, bs - 1:bs].to_broadcast([128, nblk, bs]),
                    op=ALU.mult), reads=[f"ktT{h}", "eb"], writes=["khT"])
                for c in range(8):
                    S.op("pe", lambda c=c, hs=hs: nc.tensor.matmul(PS[3][:, 0:n], lhsT=w_gq[:, c, hs],
                                                                   rhs=uT[:, c, t0:t0 + n], start=(c == 0), stop=(c == 7)),
                         reads=["w_gq", "uT"], writes=[pk(3)])
                S.op("dve", lambda h=h: nc.vector.scalar_tensor_tensor(
                    out=qtT[:, h, 0:n], in0=PS[3][:, 0:n], scalar=DKS, in1=eb[:, 0:n], op0=ALU.mult, op1=ALU.mult),
                    reads=[pk(3), "eb"], writes=[f"qtT{h}"])
            else:
                S.op("act", lambda h=h: nc.scalar.activation(
                    out=dec_t[:, h, boff:boff + nblk], in_=cb[:, 0:n].rearrange("p (b t) -> p b t", t=bs)[:, :, bs - 1],
                    func=AF.Exp, scale=-1.0 / 16.0), reads=["cb"], writes=[dkey])
                S.op("dve", lambda: nc.vector.tensor_tensor(out=ktmp[:, 0:n], in0=PS[2][:, 0:n], in1=enb[:, 0:n],
                                                            op=ALU.mult), reads=pw(2) + ["enb"], writes=["ktmp"])
                S.op("pool", lambda h=h: nc.gpsimd.tensor_tensor(
                    out=khT[:, 0:n].rearrange("p (b t) -> p b t", t=bs),
                    in0=ktmp[:, 0:n].rearrange("p (b t) -> p b t", t=bs),
                    in1=dec_t[:, h, boff:boff + nblk].unsqueeze(2).to_broadcast([128, nblk, bs]),
                    op=ALU.mult), reads=["ktmp", dkey], writes=["khT"])
            for j in range(nblk):
                S.op("pe", lambda j=j: nc.tensor.transpose(PSB[4][0:bs, j * 128:(j + 1) * 128],
                                                           khT[:, j * bs:(j + 1) * bs], ident[:]),
                     reads=["khT", "ident"], writes=[pk(4)])
            S.op("dve", lambda h=h: nc.vector.tensor_copy(
                kh_t[0:bs, boff:boff + nblk, h, :],
                PSB[4][0:bs, 0:nblk * 128].rearrange("p (b d) -> p b d", d=128)),
                reads=[pk(4)], writes=[kkey])

    def state_update(h, kh_t, blk, v_t, vblk, dec_t, dblk, bs, kkey, vkey, dkey, first=False):
        bank = 1 + (h // 2)
        cs_ = slice((h % 2) * 256, (h % 2) * 256 + 256)
        hv = slice(h * 256, (h + 1) * 256)
        S.op("pe", lambda: nc.tensor.matmul(PS[bank][:, cs_], lhsT=kh_t[0:bs, blk, h, :], rhs=v_t[0:bs, vblk, hv],
                                            start=True, stop=True),
             reads=[kkey, vkey], writes=[pk(bank, "ab"[h % 2])])
        if first:
            S.op("dve", lambda: nc.vector.tensor_copy(Sst[:, h, :], PS[bank][:, cs_]),
                 reads=[pk(bank, "ab"[h % 2])], writes=[f"S{h}"])
        else:
            S.op("dve", lambda: nc.vector.scalar_tensor_tensor(
                out=Sst[:, h, :], in0=Sst[:, h, :], scalar=dec_t[:, h, dblk:dblk + 1], in1=PS[bank][:, cs_],
                op0=ALU.mult, op1=ALU.add), reads=[f"S{h}", dkey, pk(bank, "ab"[h % 2])], writes=[f"S{h}"])

    def cast_state(h):
        S.op("pool", lambda: nc.gpsimd.tensor_copy(Sbf[:, h, :], Sst[:, h, :]), reads=[f"S{h}"], writes=[f"Sbf{h}"])

    def early_exit():
        S.barrier()
        if debug:
            S.dma("sp", lambda: nc.sync.dma_start(out=D["d_ob"][:, :, :], in_=o_b[:]), reads=["o_b"])
        S.wait_all_dma("sp")
        S.arena_peak = A.peak
        top.close()

    gla_pre("meta", 0, 1, 16)
    if stop == "g1":
        return early_exit()
    gla_pre("X", OTH0, 1, 128)
    if stop == "g1b":
        return early_exit()
    for h in range(4):
        state_update(h, kh_pre, 0, v_pre, 0, dec_pre, 0, 16, "kh_pre0", "v_pre0", "dec_pre0", first=True)
    S.op("dve", lambda: nc.vector.tensor_scalar(out=a1[:], in0=dec_pre[:, :, 1], scalar1=fcol[:, 0:1],
                                                scalar2=fcol[:, 1:2], op0=ALU.mult, op1=ALU.add),
         reads=["dec_pre1", "fcol"], writes=["a1"])
    for h in range(4):
        bank = 1 + (h // 2)
        cs_ = slice((h % 2) * 256, (h % 2) * 256 + 256)
        hv = slice(h * 256, (h + 1) * 256)
        S.op("pe", lambda h=h, bank=bank, cs_=cs_, hv=hv: nc.tensor.matmul(
            PS[bank][:, cs_], lhsT=kh_pre[:, 1, h, :], rhs=v_pre[:, 1, hv], start=True, stop=True),
            reads=["kh_pre1", "v_pre1"], writes=[pk(bank, "ab"[h % 2])])
        S.op("dve", lambda h=h: nc.vector.tensor_scalar(out=junk[:], in0=Sst[:, h, :], scalar1=a1[:, h:h + 1],
                                                        scalar2=None, op0=ALU.mult),
             reads=[f"S{h}", "a1"], writes=["junk"])
        S.op("dve", lambda h=h, bank=bank, cs_=cs_: nc.vector.scalar_tensor_tensor(
            out=Sst[:, h, :], in0=PS[bank][:, cs_], scalar=fcol[:, 0:1], in1=junk[:], op0=ALU.mult, op1=ALU.add),
            reads=[pk(bank, "ab"[h % 2]), "junk", "fcol"], writes=[f"S{h}"])
        cast_state(h)

    if stop == "g2":
        return early_exit()
    for s in range(4):
        gla_pre("own", OWN0 + s * 512, 4, 128)
        gla_pre("oth", OTH0 + 128 + s * 512, 4, 128)
        if stop == "g3":
            return early_exit()
        for j in range(4):
            i = 4 * s + j
            bsl = slice(j * 128, (j + 1) * 128)
            for h in range(4):
                S.op("pe", lambda h=h, bsl=bsl: nc.tensor.matmul(PS[7][:, h * 128:(h + 1) * 128], lhsT=ktT[:, h, bsl],
                                                                 rhs=qtT[:, h, bsl], start=True, stop=True),
                     reads=[f"ktT{h}", f"qtT{h}"], writes=[pk(7)])
            S.op("dve", lambda: nc.vector.tensor_tensor(
                out=AT[:].rearrange("p (h t) -> p h t", t=128), in0=PS[7][:].rearrange("p (h t) -> p h t", t=128),
                in1=tri[:].unsqueeze(1).to_broadcast([128, 4, 128]), op=ALU.mult),
                reads=[pk(7), "tri"], writes=["AT"])
            if stop == "g4":
                return early_exit()
            for h in range(4):
                bank = 5 + (h // 2)
                cs_ = slice((h % 2) * 256, (h % 2) * 256 + 256)
                hv = slice(h * 256, (h + 1) * 256)
                S.op("pe", lambda h=h, bank=bank, cs_=cs_, hv=hv, j=j: nc.tensor.matmul(
                    PS[bank][:, cs_], lhsT=AT[:, h * 128:(h + 1) * 128], rhs=v_own[:, j, hv], start=True, stop=False),
                    reads=["AT", "v_own"], writes=[pk(bank, "ab"[h % 2])])
                S.op("pe", lambda h=h, bank=bank, cs_=cs_, bsl=bsl: nc.tensor.matmul(
                    PS[bank][:, cs_], lhsT=qtT[:, h, bsl], rhs=Sbf[:, h, :], start=False, stop=True),
                    reads=[f"qtT{h}", f"Sbf{h}"], writes=[pk(bank, "ab"[h % 2])])
            if stop == "g5":
                return early_exit()
            for h in range(4):
                state_update(h, kh_own, j, v_own, j, dec_own, j, 128, "kh_own", "v_own", "dec_own")
                cast_state(h)
            if stop == "g6":
                return early_exit()
            for h in range(4):
                bank = 5 + (h // 2)
                cs_ = slice((h % 2) * 256, (h % 2) * 256 + 256)
                S.op("act", lambda h=h, bank=bank, cs_=cs_: nc.scalar.activation(
                    out=junk[:], in_=PS[bank][:, cs_], func=AF.Square, accum_out=ssq[:, h:h + 1]),
                    reads=[pk(bank, "ab"[h % 2])], writes=["junk", f"ssq{h}"])
            S.op("act", lambda: nc.scalar.activation(out=ssq[:, 4:8], in_=ssq[:, 0:4], func=AF.Ln, bias=EPS,
                                                     scale=1.0 / 256.0),
                 reads=[f"ssq{h}" for h in range(4)], writes=["ssqln"])
            S.op("act", lambda: nc.scalar.activation(out=ssq[:, 4:8], in_=ssq[:, 4:8], func=AF.Exp, scale=-0.5),
                 reads=["ssqln"], writes=["ssqr"])
            for h in range(4):
                bank = 5 + (h // 2)
                cs_ = slice((h % 2) * 256, (h % 2) * 256 + 256)
                S.op("dve", lambda h=h, bank=bank, cs_=cs_, i=i: nc.vector.scalar_tensor_tensor(
                    out=o_b[:, i, h * 256:(h + 1) * 256], in0=PS[bank][:, cs_], scalar=ssq[:, 4 + h:5 + h],
                    in1=gnw[:], op0=ALU.mult, op1=ALU.mult),
                    reads=[pk(bank, "ab"[h % 2]), "ssqr", "gnw"], writes=[f"o_b{i}_{h}"])
            if stop == "g7":
                return early_exit()
            for h in range(4):
                state_update(h, kh_oth, j, v_oth, j, dec_oth, j, 128, "kh_oth", "v_oth", "dec_oth")
                cast_state(h)
            if stop == "g8":
                return early_exit()
        if stop == "g9":
            return early_exit()
        if stop == "g10" and s == 1:
            return early_exit()
    S.barrier()
    A.reset(MARK_P1)
    if debug:
        S.dma("sp", lambda: nc.sync.dma_start(out=D["d_ob"][:, :, :], in_=o_b[:]), reads=["o_b"])
    if stop == "gla":
        S.wait_all_dma("sp")
        S.arena_peak = A.peak
        top.close()
        return

    o_a = sbuf(None, "o_a", [128, NOWN, 1024], BF16)
    MARK_P2 = A.mark()
    cs16 = sbuf(None, "cs16s", [128, NBLK, 16], F32)
    sn16 = sbuf(None, "sn16s", [128, NBLK, 16], F32)
    sw8 = sbuf(None, "sw8", [128, 128], F32)
    ld(lambda: cs16[:], lambda: D["cs16"][:, :, :], "cs16")
    ld(lambda: sn16[:], lambda: D["sn16"][:, :, :], "sn16")
    ld(lambda: sw8[:], lambda: D["sublnw"][0:1, :].partition_broadcast(128), "sw8")
    S.op("pool", lambda: nc.gpsimd.tensor_scalar(out=sw8[:], in0=sw8[:], scalar1=0.8, scalar2=None,
                                                 op0=ALU.mult), reads=["sw8"], writes=["sw8"])
    wq_sb = [sbuf(None, f"wq{i}", [128, 8, 256], BF16) for i in range(2)]
    wk_sb = sbuf(None, "wk", [128, 8, 256], BF16)
    wv_sb = sbuf(None, "wv", [128, 8, 256], BF16)
    KT = sbuf(None, "KT", [128, 2, NTOK], BF16)
    Vaug = sbuf(None, "Vaug", [128, NBLK, 2, 130], BF16)
    Pt = [[sbuf(None, f"Pt{b}{r}", [128, 512], BF16) for r in range(2)] for b in range(2)]
    kf = [sbuf(None, f"kf{i}", [128, 256], F32) for i in range(3)]
    kbb = [sbuf(None, f"kbb{i}", [128, 256], BF16) for i in range(3)]
    rt1 = [sbuf(None, f"rt1{i}", [128, 4, 16], F32) for i in range(3)]
    rt2 = [sbuf(None, f"rt2{i}", [128, 4, 16], F32) for i in range(3)]
    o2 = sbuf(None, "o2", [128, 2, 128], F32)
    ssa = sbuf(None, "ssa", [128, 4], F32)
    ssap = [ssa, sbuf(None, "ssa_b", [128, 4], F32)]
    oraw = [sbuf(None, f"oraw{k}", [128, 258], F32) for k in range(4)]
    rlr = [sbuf(None, f"rlr{k}", [128, 2], F32) for k in range(4)]
    t2r = [sbuf(None, f"t2r{k}", [128, 128], F32) for k in range(4)]
    o2p = [o2, sbuf(None, "o2_b", [128, 2, 128], F32)]

    S.op("pool", lambda: nc.gpsimd.memset(Vaug[:, :, :, 128:130], 1.0), writes=["Vones"])
    S.op("pool", lambda: nc.gpsimd.memset(Vaug[:, 0, :, :], 0.0), reads=["Vones"], writes=["Vones", "V0"])
    S.op("pool", lambda: nc.gpsimd.tensor_copy(Vaug[:, 0, :, 128:129], tri[:, 15:16].unsqueeze(1).to_broadcast([128, 2, 1])),
         reads=["tri", "Vones"], writes=["Vones"])
    S.op("pool", lambda: nc.gpsimd.tensor_copy(Vaug[:, 1, :, 128:129], fcol[:, 0:1].unsqueeze(1).to_broadcast([128, 2, 1])),
         reads=["fcol", "Vones"], writes=["Vones"])

    def blk_t0(blk):
        if blk == 0:
            return 0, 16
        if blk <= NOTH:
            return OTH0 + (blk - 1) * 128, 128
        return OWN0 + (blk - 1 - NOTH) * 128, 128

    rope_n = [0]
    zeros_bf = sbuf(None, "zeros_bf", [128, 128], BF16)
    S.op("pool", lambda: nc.gpsimd.memset(zeros_bf[:], 0.0), writes=["zeros_bf"])
    QTz = [sbuf(None, f"QTz{k}", [128, 2, 256], BF16) for k in range(2)]
    for k in range(2):
        S.op("pool", lambda k=k: nc.gpsimd.memset(QTz[k][:], 0.0), writes=[f"QT{k}"])

    def rope(src_bank, blk, bs, cast_eng="dve"):
        r = rope_n[0] % 3
        rope_n[0] += 1
        S.op("dve", lambda: nc.vector.tensor_copy(kf[r][0:bs, :], PS[src_bank][0:bs, 0:256]),
             reads=[pk(src_bank)], writes=[f"kf{r}"])
        if cast_eng == "act":
            S.op("act", lambda: nc.scalar.copy(kbb[r][0:bs, :], kf[r][0:bs, :]),
                 reads=[f"kf{r}"], writes=[f"kbb{r}"])
        else:
            S.op("dve", lambda: nc.vector.tensor_copy(kbb[r][0:bs, :], kf[r][0:bs, :]), reads=[f"kf{r}"], writes=[f"kbb{r}"])
        rv = lambda t_: t_[0:bs, :].rearrange("p (g d) -> p g d", g=4)[:, :, 0:16]
        S.op("pool", lambda: nc.gpsimd.tensor_tensor(out=rt1[r][0:bs], in0=rv(kf[r]),
                                                     in1=cs16[0:bs, blk:blk + 1, :].to_broadcast([bs, 4, 16]), op=ALU.mult),
             reads=[f"kf{r}", "cs16"], writes=[f"rt1{r}"])
        S.op("pool", lambda: nc.gpsimd.tensor_tensor(out=rt2[r][0:bs, :, 0:8], in0=rv(kf[r])[:, :, 8:16],
                                                     in1=sn16[0:bs, blk:blk + 1, 0:8].to_broadcast([bs, 4, 8]), op=ALU.mult),
             reads=[f"kf{r}", "sn16"], writes=[f"rt2a{r}"])
        S.op("pool", lambda: nc.gpsimd.tensor_tensor(out=rt2[r][0:bs, :, 8:16], in0=rv(kf[r])[:, :, 0:8],
                                                     in1=sn16[0:bs, blk:blk + 1, 8:16].to_broadcast([bs, 4, 8]), op=ALU.mult),
             reads=[f"kf{r}", "sn16"], writes=[f"rt2b{r}"])
        S.op("pool", lambda: nc.gpsimd.tensor_tensor(out=rv(kbb[r]), in0=rt1[r][0:bs], in1=rt2[r][0:bs], op=ALU.add),
             reads=[f"rt1{r}", f"rt2a{r}", f"rt2b{r}", f"kbb{r}"], writes=[f"kbb{r}"])
        return r

    def transposes(r, bs, bank, dst_fn, dkey):
        for h in range(2):
            S.op("pe", lambda h=h: nc.tensor.transpose(PSB[bank][:, h * 128:h * 128 + bs],
                                                       kbb[r][0:bs, h * 128:(h + 1) * 128], ident[0:bs, 0:bs]),
                 reads=[f"kbb{r}", "ident"], writes=[pk(bank)])
        S.op("dve", lambda: nc.vector.tensor_copy(
            dst_fn(), PSB[bank][:, 0:256].rearrange("p (h t) -> p h t", t=128)[:, :, 0:bs]),
            reads=[pk(bank)], writes=[dkey])

    def load_w(dst, col0, key):
        S.dma("pool", lambda: nc.gpsimd.dma_start(out=dst[:], in_=WIN[:, :, col0:col0 + 256]), writes=[key])

    load_w(wk_sb, C_AK, "wk")
    load_w(wv_sb, C_AV, "wv")
    load_w(wq_sb[0], C_AQ, "wq0")
    for hp in range(4):
        wq = wq_sb[hp % 2]
        wqk = f"wq{hp % 2}"
        if hp + 1 < 4:
            load_w(wq_sb[(hp + 1) % 2], C_AQ + (hp + 1) * 256, f"wq{(hp + 1) % 2}")
        prev = None
        for blk in range(NBLK):
            t0, bs = blk_t0(blk)
            kbank = (6, 2)[blk % 2]
            vbank = (7, 3)[blk % 2]
            for c in range(8):
                S.op("pe", lambda c=c, t0=t0, bs=bs, kbank=kbank: nc.tensor.matmul(
                    PS[kbank][0:bs, 0:256], lhsT=uT[:, c, t0:t0 + bs], rhs=wk_sb[:, c, :], start=(c == 0), stop=(c == 7)),
                    reads=["uT", "wk"], writes=[pk(kbank)])
            r = rope(kbank, blk, bs, cast_eng="act")
            for c in range(8):
                S.op("pe", lambda c=c, t0=t0, bs=bs, vbank=vbank: nc.tensor.matmul(
                    PS[vbank][0:bs, 0:256], lhsT=uT[:, c, t0:t0 + bs], rhs=wv_sb[:, c, :], start=(c == 0), stop=(c == 7)),
                    reads=["uT", "wv"], writes=[pk(vbank)])
            S.op("act", lambda blk=blk, bs=bs, vbank=vbank: nc.scalar.copy(
                Vaug[0:bs, blk, :, 0:128], PS[vbank][0:bs, 0:256].rearrange("p (h d) -> p h d", d=128)),
                reads=[pk(vbank)], writes=[f"V{blk}"])
            if prev is not None:
                pr, pbs, pt0, pblk = prev
                transposes(pr, pbs, (0, 1)[pblk % 2], lambda pt0=pt0, pbs=pbs: KT[:, :, pt0:pt0 + pbs], f"KT{pblk}")
            prev = (r, bs, t0, blk)
        pr, pbs, pt0, pblk = prev
        transposes(pr, pbs, (0, 1)[pblk % 2], lambda: KT[:, :, pt0:pt0 + pbs], f"KT{pblk}")
        if hp + 1 < 4:
            load_w(wk_sb, C_AK + (hp + 1) * 256, "wk")
            load_w(wv_sb, C_AV + (hp + 1) * 256, "wv")

        def q_proj(i):
            t0q = OWN0 + i * 128
            for c in range(8):
                S.op("pe", lambda c=c: nc.tensor.matmul(PS[6][:, 0:256], lhsT=uT[:, c, t0q:t0q + 128],
                                                        rhs=wq[:, c, :], start=(c == 0), stop=(c == 7)),
                     reads=["uT", wqk], writes=[pk(6)])
            return rope(6, 1 + NOTH + i, 128)

        def q_tr(i, r):
            for h in range(2):
                S.op("pe", lambda h=h: nc.tensor.transpose(PSB[7][:, h * 128:(h + 1) * 128],
                                                           kbb[r][:, h * 128:(h + 1) * 128], ident[:, :]),
                     reads=[f"kbb{r}", "ident"], writes=[pk(7)])
            S.op("dve", lambda: nc.vector.tensor_copy(
                QTz[i % 2][0:64, :, 0:128], PSB[7][0:64, 0:256].rearrange("p (h t) -> p h t", t=128)),
                reads=[pk(7)], writes=[f"QT{i % 2}"])
            S.op("dve", lambda: nc.vector.tensor_copy(
                QTz[i % 2][64:128, :, 128:256], PSB[7][64:128, 0:256].rearrange("p (h t) -> p h t", t=128)),
                reads=[pk(7)], writes=[f"QT{i % 2}"])

        items = []
        for i in range(NOWN):
            qblk = 1 + NOTH + i
            for h in range(2):
                oth = [(1 + j, OTH0 + j * 128, 128) for j in range(i + 1)]
                own = [(1 + NOTH + j, OWN0 + j * 128, 128) for j in range(i + 1)]
                lst = [own[-1], (0, 0, 128)] + oth + own[:-1]
                groups = [lst[a:a + 2] for a in range(0, len(lst), 2)]
                for gi, grp in enumerate(groups):
                    items.append(dict(i=i, h=h, grp=grp, first=(gi == 0), last=(gi == len(groups) - 1), qblk=qblk))

        def emit_st(it, r):
            i, h, grp = it["i"], it["h"], it["grp"]
            qt = QTz[i % 2]
            for s_, (kb, kt0, kbs) in enumerate(grp):
                bank = r
                c0 = (s_ % 2) * 256
                S.op("pe", lambda bank=bank, c0=c0, kt0=kt0, kbs=kbs: nc.tensor.matmul(
                    PS[bank][0:kbs, c0:c0 + 256], lhsT=KT[:, h, kt0:kt0 + kbs], rhs=qt[:, h, :], start=True, stop=True),
                    reads=[f"KT{kb}", f"QT{i % 2}"], writes=[pk(bank)])

        def emit_exp_pv(it, r, ob):
            i, h, grp, qblk = it["i"], it["h"], it["grp"], it["qblk"]
            bs = grp[0][2]
            ncol = len(grp) * 128 if bs == 128 else 128
            nk = len(grp)
            ptile = Pt[r // 2][r % 2]
            pkey = f"Pt{r}"
            S.op("act", lambda: nc.scalar.activation(
                out=ptile[0:bs, 0:nk * 256], in_=PS[r][0:bs, 0:nk * 256], func=AF.Exp, scale=0.125),
                reads=[pk(r)], writes=[pkey])
            if grp[0][0] == qblk:
                c0 = 0
                S.op("dve", lambda c0=c0: nc.vector.tensor_tensor(
                    out=ptile[:, c0:c0 + 256].rearrange("p (b t) -> p b t", t=128),
                    in0=ptile[:, c0:c0 + 256].rearrange("p (b t) -> p b t", t=128),
                    in1=tri[:].unsqueeze(1).to_broadcast([128, 2, 128]), op=ALU.mult),
                    reads=[pkey, "tri"], writes=[pkey])
            for s_, (kb, kt0, kbs) in enumerate(grp):
                c0 = s_ * 256
                for b in range(2):
                    lastmm = it["last"] and s_ == len(grp) - 1 and b == 1
                    S.op("pe", lambda b=b, c0=c0, kb=kb, kbs=kbs, lastmm=lastmm: nc.tensor.matmul(
                        PS[4 + ob][:, b * 129:(b + 1) * 129], lhsT=ptile[0:kbs, c0 + b * 128:c0 + (b + 1) * 128],
                        rhs=Vaug[0:kbs, kb, h, 0:129], start=(it["first"] and s_ == 0 and b == 0), stop=lastmm,
                        skip_group_check=True),
                        reads=[pkey, f"V{kb}", "Vones"], writes=[pk(4 + ob)])

        RING = 4
        dq = []
        bcount = [0]

        def flush(upto=None, nmax=None):
            n = 0
            while dq and (upto is None or dq[0][0] <= upto) and (nmax is None or n < nmax):
                _, e_, fn_, rd_, wr_ = dq.pop(0)
                S.op(e_, fn_, reads=rd_, writes=wr_)
                n += 1

        def finalize(i, h, ob):
            bi = bcount[0]
            bcount[0] += 1
            flush(upto=bi - RING)
            k = bi % RING
            kb_ = (bi // 2) % 2 if False else (i % 2)
            S.op("dve", lambda: nc.vector.tensor_copy(oraw[k][:, :], PS[4 + ob][:, 0:258]),
                 reads=[pk(4 + ob)], writes=[f"oraw{k}"])
            pb = i % 2
            D_ = lambda e_, fn_, rd_, wr_: dq.append((bi, e_, fn_, rd_, wr_))
            D_("dve", lambda: nc.vector.reciprocal(rlr[k][:, 0:1], oraw[k][:, 128:129]), [f"oraw{k}"], [f"rl0{k}"])
            D_("dve", lambda: nc.vector.reciprocal(rlr[k][:, 1:2], oraw[k][:, 257:258]), [f"oraw{k}"], [f"rl1{k}"])
            D_("dve", lambda: nc.vector.tensor_scalar(out=t2r[k][:], in0=oraw[k][:, 129:257], scalar1=rlr[k][:, 1:2],
                                                      scalar2=lams[:, 4:5], op0=ALU.mult, op1=ALU.mult),
               [f"oraw{k}", f"rl1{k}", "neglam"], [f"t2{k}"])
            D_("dve", lambda: nc.vector.scalar_tensor_tensor(
                out=o2p[pb][:, h, :], in0=oraw[k][:, 0:128], scalar=rlr[k][:, 0:1], in1=t2r[k][:], op0=ALU.mult, op1=ALU.add),
               [f"oraw{k}", f"rl0{k}", f"t2{k}"], [f"o2{pb}{h}"])
            D_("pool", lambda: nc.gpsimd.tensor_tensor(out=t2r[k][:], in0=o2p[pb][:, h, :], in1=o2p[pb][:, h, :], op=ALU.mult),
               [f"o2{pb}{h}"], [f"t2{k}"])
            D_("dve", lambda: nc.vector.reduce_sum(out=ssap[pb][:, h:h + 1], in_=t2r[k][:], axis=AX.X),
               [f"t2{k}"], [f"ssa{pb}{h}"])
            if h == 1:
                D_("act", lambda: nc.scalar.activation(out=ssap[pb][:, 2:4], in_=ssap[pb][:, 0:2], func=AF.Ln, bias=EPS,
                                                       scale=1.0 / 128.0), [f"ssa{pb}0", f"ssa{pb}1"], [f"ssaln{pb}"])
                D_("act", lambda: nc.scalar.activation(out=ssap[pb][:, 2:4], in_=ssap[pb][:, 2:4], func=AF.Exp, scale=-0.5),
                   [f"ssaln{pb}"], [f"ssar{pb}"])
                for h2 in range(2):
                    hh = 2 * hp + h2
                    D_("dve", lambda h2=h2, hh=hh: nc.vector.scalar_tensor_tensor(
                        out=o_a[:, i, hh * 128:(hh + 1) * 128], in0=o2p[pb][:, h2, :], scalar=ssap[pb][:, 2 + h2:3 + h2],
                        in1=sw8[:], op0=ALU.mult, op1=ALU.mult), [f"o2{pb}{h2}", f"ssar{pb}", "sw8"], [f"o_a{i}"])

        LA = 3
        qr = q_proj(0)
        q_tr(0, qr)
        for n0 in range(LA):
            emit_st(items[n0], n0 % 4)
        qr_next = None
        pending = None
        for n_, it in enumerate(items):
            r = n_ % 4
            i, h = it["i"], it["h"]
            ob = (2 * i + h) % 2
            if it["first"] and h == 0 and i + 1 < NOWN:
                qr_next = q_proj(i + 1)
            if n_ + LA < len(items):
                nx = items[n_ + LA]
                if nx["first"] and nx["h"] == 0:
                    q_tr(nx["i"], qr_next)
                emit_st(nx, (n_ + LA) % 4)
            emit_exp_pv(it, r, ob)
            flush(nmax=2)
            if it["last"]:
                finalize(i, h, ob)
        flush()
    if debug:
        S.dma("sp", lambda: nc.sync.dma_start(out=D["d_oa"][:, :, :], in_=o_a[:]), reads=[f"o_a{i}" for i in range(NOWN)])
    S.barrier()
    A.reset(MARK_P2)
    if stop == "da":
        S.wait_all_dma("sp")
        S.arena_peak = A.peak
        top.close()
        return

    oagT = sbuf(None, "oagT", [128, 8, 2048], BF16)
    obgT = sbuf(None, "obgT", [128, 8, 2048], BF16)
    wz = [sbuf(None, f"wz{i}", [128, 8, 128], BF16) for i in range(2)]
    gz = [sbuf(None, f"gz{i}", [128, 512], F32) for i in range(2)]

    def gate_stage(src, srckey, col0, dst, dkey):
        for c in range(8):
            wb = c % 2
            S.dma("pool", lambda c=c, wb=wb: nc.gpsimd.dma_start(out=wz[wb][:], in_=WIN[:, :, col0 + c * 128:col0 + (c + 1) * 128]),
                  writes=[f"wz{wb}"])
            for tg in range(4):
                g = (c * 4 + tg) % 2
                ts_ = slice(OWN0 + tg * 512, OWN0 + (tg + 1) * 512)
                for k in range(8):
                    S.op("pe", lambda k=k, wb=wb, ts_=ts_, g=g: nc.tensor.matmul(
                        PS[g][:, :], lhsT=wz[wb][:, k, :], rhs=uT[:, k, ts_], start=(k == 0), stop=(k == 7)),
                        reads=[f"wz{wb}", "uT"], writes=[pk(g)])
                S.op("act", lambda g=g: nc.scalar.activation(out=gz[g][:], in_=PS[g][:, :], func=AF.Silu),
                     reads=[pk(g)], writes=[f"gz{g}"])
                for j in range(4):
                    S.op("pe", lambda j=j, tg=tg, c=c, g=g: nc.tensor.transpose(
                        PSB[2 + g][:, j * 128:(j + 1) * 128], src[:, tg * 4 + j, c * 128:(c + 1) * 128], ident[:]),
                        reads=[srckey, "ident"], writes=[pk(2 + g)])
                S.op("dve", lambda g=g, c=c, tg=tg: nc.vector.tensor_tensor(
                    out=dst[:, c, tg * 512:(tg + 1) * 512], in0=PSB[2 + g][:, 0:512], in1=gz[g][:], op=ALU.mult),
                    reads=[pk(2 + g), f"gz{g}"], writes=[f"{dkey}{c}_{tg}"])

    gate_stage(o_a, "o_a", C_AZ, oagT, "oagT")
    gate_stage(o_b, "o_b", C_GZ, obgT, "obgT")
    S.barrier()
    A.reset(MARK_P)
    mT = sbuf(None, "mT", [128, 8, 2048], BF16)
    MARK_C = A.mark()
    wc = [[sbuf(None, f"wc{n}{i}", [128, 8, 256], BF16) for i in range(2)] for n in range(4)]
    assert A.top <= MARK_P2
    A.reset(MARK_P2 + 65536)
    sg = [sbuf(None, f"sg{i}", [128, 512], F32) for i in range(2)]
    tt = [sbuf(None, f"tt{i}", [128, 512], F32) for i in range(2)]
    for d in range(8):
        wb = (d // 2) % 2
        dsub = d % 2
        if dsub == 0:
            ds_ = slice(d * 128, (d + 2) * 128)
            S.dma("pool", lambda wb=wb, ds_=ds_: nc.gpsimd.dma_start(out=wc[0][wb][:], in_=WBA[:, :, ds_]), writes=[f"wc0{wb}"])
            S.dma("pool", lambda wb=wb, ds_=ds_: nc.gpsimd.dma_start(out=wc[1][wb][:], in_=WBB[:, :, ds_]), writes=[f"wc1{wb}"])
            S.dma("pool", lambda wb=wb, d=d: nc.gpsimd.dma_start(out=wc[2][wb][:], in_=WIN[:, :, C_GA + d * 128:C_GA + (d + 2) * 128]),
                  writes=[f"wc2{wb}"])
            S.dma("pool", lambda wb=wb, d=d: nc.gpsimd.dma_start(out=wc[3][wb][:], in_=WIN[:, :, C_GB + d * 128:C_GB + (d + 2) * 128]),
                  writes=[f"wc3{wb}"])
        for tg in range(4):
            ts_ = slice(tg * 512, (tg + 1) * 512)
            tsu = slice(OWN0 + tg * 512, OWN0 + (tg + 1) * 512)
            ph4 = ((d * 4 + tg) % 2) * 4
            srcs = [(oagT, "oagT", ts_), (obgT, "obgT", ts_), (uT, "uT", tsu), (uT, "uT", tsu)]
            for n_ in range(4):
                src, skey, sl = srcs[n_]
                for k in range(8):
                    S.op("pe", lambda n_=n_, k=k, src=src, sl=sl, wb=wb, ph4=ph4: nc.tensor.matmul(
                        PS[ph4 + n_][:, :], lhsT=wc[n_][wb][:, k, dsub * 128:(dsub + 1) * 128], rhs=src[:, k, sl],
                        start=(k == 0), stop=(k == 7)),
                        reads=[f"wc{n_}{wb}", skey], writes=[pk(ph4 + n_)])
            for n_ in range(2):
                S.op("act", lambda n_=n_, ph4=ph4: nc.scalar.activation(out=sg[n_][:], in_=PS[ph4 + 2 + n_][:, :], func=AF.Sigmoid),
                     reads=[pk(ph4 + 2 + n_)], writes=[f"sg{n_}"])
                S.op("dve", lambda n_=n_, ph4=ph4: nc.vector.tensor_tensor(out=tt[n_][:], in0=PS[ph4 + n_][:, :], in1=sg[n_][:],
                                                                          op=ALU.mult),
                     reads=[pk(ph4 + n_), f"sg{n_}"], writes=[f"tt{n_}"])
            S.op("pool", lambda d=d, ts_=ts_: nc.gpsimd.tensor_tensor(out=mT[:, d, ts_], in0=tt[0][:], in1=tt[1][:], op=ALU.add),
                 reads=["tt0", "tt1"], writes=[f"mT{d}_{tg}"])
    S.barrier()
    A.reset(MARK_C)
    wout_sb = sbuf(None, "wout_sb", [128, 8, 1024], BF16)
    fnw = sbuf(None, "fnws", [128, 1024], F32)
    for c in range(8):
        S.dma("pool", lambda c=c: nc.gpsimd.dma_start(out=wout_sb[:, c, :], in_=WOUT[:, c, :]), writes=[f"wout{c}"])
    ld(lambda: fnw[:], lambda: D["fnw"][0:1, :].partition_broadcast(128), "fnw")
    NXR = 6
    xr = [sbuf(None, f"xr{i}", [128, 1024], F32) for i in range(NXR)]
    hres = [sbuf(None, f"hres{i}", [128, 1024], F32) for i in range(2)]
    osb = [sbuf(None, f"osb{i}", [128, 1024], F32) for i in range(2)]
    fss = sbuf(None, "fss", [128, 4], F32)
    junk3 = sbuf(None, "junk3", [128, 1024], F32)
    def load_x(i):
        S.dma("sp", lambda: nc.sync.dma_start(out=xr[i % NXR][:], in_=D["xo"][i, :, :]), writes=[f"xr{i % NXR}"])

    for i in range(NXR - 1):
        load_x(i)
    for i in range(NOWN):
        r = i % 2
        if i + NXR - 1 < NOWN:
            load_x(i + NXR - 1)
        for half in range(2):
            bank = (i % 2) * 2 + half
            for k in range(8):
                S.op("pe", lambda k=k, i=i, half=half, bank=bank: nc.tensor.matmul(
                    PS[bank][:, :], lhsT=mT[:, k, i * 128:(i + 1) * 128], rhs=wout_sb[:, k, half * 512:(half + 1) * 512],
                    start=(k == 0), stop=(k == 7)), reads=["mT", f"wout{k}"], writes=[pk(bank)])
            S.op("dve", lambda r=r, half=half, bank=bank: nc.vector.tensor_tensor(
                out=hres[r][:, half * 512:(half + 1) * 512], in0=PS[bank][:, :], in1=xr[i % NXR][:, half * 512:(half + 1) * 512],
                op=ALU.add), reads=[pk(bank), f"xr{i % NXR}"], writes=[f"hres{r}{half}"])
        S.op("act", lambda r=r: nc.scalar.activation(out=junk3[:], in_=hres[r][:], func=AF.Square, accum_out=fss[:, r:r + 1]),
             reads=[f"hres{r}0", f"hres{r}1"], writes=["junk3", f"fss{r}"])
        S.op("act", lambda r=r: nc.scalar.activation(out=fss[:, 2 + r:3 + r], in_=fss[:, r:r + 1], func=AF.Ln, bias=EPS,
                                                     scale=1.0 / 1024.0), reads=[f"fss{r}"], writes=[f"fsl{r}"])
        S.op("act", lambda r=r: nc.scalar.activation(out=fss[:, 2 + r:3 + r], in_=fss[:, 2 + r:3 + r], func=AF.Exp, scale=-0.5),
             reads=[f"fsl{r}"], writes=[f"fsr{r}"])
        S.op("dve", lambda r=r: nc.vector.scalar_tensor_tensor(
            out=osb[r][:], in0=hres[r][:], scalar=fss[:, 2 + r:3 + r], in1=fnw[:], op0=ALU.mult, op1=ALU.mult),
            reads=[f"hres{r}0", f"hres{r}1", f"fsr{r}", "fnw"], writes=[f"osb{r}"])
        S.dma("sp", lambda i=i, r=r: nc.sync.dma_start(out=D["out"][i, :, :], in_=osb[r][:]), reads=[f"osb{r}"])
    S.wait_all_dma("sp")
    S.arena_peak = A.peak
    top.close()


_CACHE = {}


def _program(debug=False, stop=None):
    key = ("prog", debug, stop)
    if key not in _CACHE:
        s1 = Sched(None)
        build(s1, debug, stop)
        nc = bass.Bass("TRN2", target_bir_lowering=False)
        s2 = Sched(nc, need=s1.need)
        build(s2, debug, stop)
        _CACHE[key] = nc
    return _CACHE[key]


def _rope_tables(p):
    half = 8
    inv_freq = (np.float32(500000.0) ** (-np.arange(half, dtype=np.float32) / np.float32(half))).astype(np.float32)
    pos = np.zeros((128, NBLK), np.float32)
    tt = np.arange(128, dtype=np.float32)
    pos[:, 0] = tt
    for j in range(NOTH):
        sb = 2 * j - 1 if p == 0 else 2 * j
        if 0 <= sb < 32:
            pos[:, 1 + j] = 16 + sb * 128 + tt
    for i in range(NOWN):
        pos[:, 1 + NOTH + i] = 16 + (2 * i + p) * 128 + tt
    ang = (pos[:, :, None] * inv_freq[None, None, :]).astype(np.float32)
    cos = np.cos(ang).astype(np.float32)
    sin = np.sin(ang).astype(np.float32)
    cs16 = np.concatenate([cos, cos], axis=-1)
    sn16 = np.concatenate([-sin, sin], axis=-1)
    return np.ascontiguousarray(cs16), np.ascontiguousarray(sn16)


def make_in_maps(x, meta_tokens, norm_w, w_in, lam_q1, lam_k1, lam_q2, lam_k2, da_subln_w,
                 gla_gate_w2, gla_gate_b, gla_norm_w, w_branch_a, w_branch_b, w_out, final_norm_w):
    f32 = np.float32
    x = np.asarray(x, f32)
    meta = np.asarray(meta_tokens, f32)
    shared = {
        "w_in": np.ascontiguousarray(np.asarray(w_in, f32)[0]),
        "w_ba": np.ascontiguousarray(np.asarray(w_branch_a, f32)[0]),
        "w_bb": np.ascontiguousarray(np.asarray(w_branch_b, f32)[0]),
        "w_out": np.ascontiguousarray(np.asarray(w_out, f32)[0]),
        "normw": np.ascontiguousarray(np.asarray(norm_w, f32)[0].reshape(8, 128).T),
        "lamv": np.concatenate([np.asarray(a, f32)[0] for a in (lam_q1, lam_k1, lam_q2, lam_k2)])[None, :].copy(),
        "sublnw": np.asarray(da_subln_w, f32)[0][None, :].copy(),
        "gnw": np.asarray(gla_norm_w, f32)[0][None, :].copy(),
        "fnw": np.asarray(final_norm_w, f32)[None, :].copy(),
        "gateb": np.ascontiguousarray(np.asarray(gla_gate_b, f32)[0].reshape(4, 128).T),
        "w2": np.ascontiguousarray(np.asarray(gla_gate_w2, f32)[0]),
        "tri": np.triu(np.ones((128, 128), f32)),
        "identf": np.eye(128, dtype=f32),
    }
    in_maps = []
    for core in range(8):
        b, p = core // 2, core % 2
        xb = x[b].reshape(32, 128, 1024)
        own = xb[p::2]
        zero = np.zeros((1, 128, 1024), f32)
        if p == 0:
            oth = np.concatenate([zero, xb[1::2]], axis=0)
        else:
            oth = np.concatenate([xb[0::2], zero], axis=0)
        allt = np.concatenate([meta, oth.reshape(-1, 1024), own.reshape(-1, 1024)], axis=0)
        xT = np.ascontiguousarray(allt.T.reshape(8, 128, NTOK).transpose(1, 0, 2))
        cs16, sn16 = _rope_tables(p)
        fc = np.zeros((128, 2), f32)
        fc[:, 0] = p
        fc[:, 1] = 1 - p
        m = dict(shared)
        m.update({"xT": xT, "xo": np.ascontiguousarray(own), "cs16": cs16, "sn16": sn16, "fcol": fc})
        in_maps.append(m)
    return in_maps


def kernel(**inputs):
    debug = bool(os.environ.get("KDEBUG"))
    nc = _program(debug)
    in_maps = make_in_maps(**inputs)
    res = run_bass_kernel_spmd(nc, in_maps, core_ids=list(range(8)))
    out = np.zeros((4, 4096, 1024), np.float32)
    for core in range(8):
        b, p = core // 2, core % 2
        oc = np.asarray(res.results[core]["out"], np.float32)
        out[b].reshape(32, 128, 1024)[p::2] = oc
    if debug:
        kernel.last_results = res.results
    return out
```

```python
import contextlib
import os
import numpy as np
import concourse.bass as bass
import concourse.mybir as mybir
from concourse.bass_utils import run_bass_kernel_spmd

F32 = mybir.dt.float32
BF16 = mybir.dt.bfloat16
AF = mybir.ActivationFunctionType
ALU = mybir.AluOpType
AX = mybir.AxisListType

ENGS = ("pe", "act", "dve", "pool", "sp")

NMETA = 16
NOTH = 17
NOWN = 16
OTH0 = NMETA
OWN0 = NMETA + NOTH * 128
NTOK = OWN0 + NOWN * 128
NBLK = 1 + NOTH + NOWN
EPS = 1e-5
C_AQ, C_AK, C_AV, C_AZ = 0, 1024, 2048, 3072
C_GQ, C_GK, C_GV, C_GZ, C_GLR, C_GA, C_GB = 4096, 4608, 5120, 6144, 7168, 7184, 8208
WCOLS = 9232


class Sched:
    N_DMA_SEMS = {"sp": 8, "pool": 6}

    def __init__(self, nc=None, need=None):
        self.nc = nc
        self.rec = nc is None
        self.need_in = need if need is not None else set()
        self.need = set()
        self.seq = {e: 0 for e in ENGS}
        self.sigcount = {e: 0 for e in ENGS}
        self.sigmap = {}
        self.waited = {e: {} for e in ENGS}
        self.lastw = {}
        self.reads = {}
        self.esems = None
        self.dsems = None
        self.dma_n = {q: 0 for q in self.N_DMA_SEMS}
        self.dma_waited = {e: {} for e in ENGS}
        self.ninstr = 0

    def eng(self, e):
        nc = self.nc
        return {"pe": nc.tensor, "act": nc.scalar, "dve": nc.vector,
                "pool": nc.gpsimd, "sp": nc.sync}[e]

    def _wait_tok(self, e, tok):
        if tok[0] == "eng":
            _, pe_, ps_ = tok
            if self.rec:
                self.need.add((pe_, ps_))
                return
            val = self.sigmap[(pe_, ps_)]
            if self.waited[e].get(pe_, 0) >= val:
                return
            self.waited[e][pe_] = val
            self.eng(e).wait_ge(self.esems[pe_], val)
        else:
            _, q, idx, val = tok
            if self.rec:
                return
            key = (q, idx)
            if self.dma_waited[e].get(key, 0) >= val:
                return
            self.dma_waited[e][key] = val
            self.eng(e).wait_ge(self.dsems[q][idx], val)

    def _deps(self, e, reads, writes):
        toks = []
        for k in reads:
            w = self.lastw.get(k)
            if w is not None:
                toks.append(w)
        for k in writes:
            w = self.lastw.get(k)
            if w is not None:
                toks.append(w)
            for r in self.reads.get(k, ()):
                toks.append(r)
        best = {}
        out = []
        for t in toks:
            if t[0] == "eng":
                if t[1] == "pe" and e == "pe":
                    continue
                if best.get(t[1], -1) < t[2]:
                    best[t[1]] = t[2]
            elif t not in out:
                out.append(t)
        for pe_, ps_ in best.items():
            out.append(("eng", pe_, ps_))
        return out

    def _commit(self, tok, reads, writes):
        for k in reads:
            lst = self.reads.setdefault(k, [])
            if tok[0] == "eng":
                lst[:] = [r for r in lst if not (r[0] == "eng" and r[1] == tok[1])]
            lst.append(tok)
        for k in writes:
            self.lastw[k] = tok
            self.reads[k] = []

    def op(self, e, fn, reads=(), writes=()):
        self.ninstr += 1
        psr = [k for k in reads if k.startswith("ps")]
        if psr:
            reads = [k for k in reads if not k.startswith("ps")]
            writes = list(writes) + psr
        for t in self._deps(e, reads, writes):
            self._wait_tok(e, t)
        s = self.seq[e]
        self.seq[e] += 1
        tok = ("eng", e, s)
        if not self.rec:
            ins = fn()
            if (e, s) in self.need_in:
                ins.then_inc(self.esems[e], 1)
                self.sigcount[e] += 1
                self.sigmap[(e, s)] = self.sigcount[e]
        self._commit(tok, reads, writes)
        return tok

    def dma(self, q, fn, reads=(), writes=()):
        self.ninstr += 1
        n = self.dma_n[q]
        P = self.N_DMA_SEMS[q]
        idx = n % P
        val = 16 * (n // P + 1)
        self.dma_n[q] += 1
        if n >= P:
            self._wait_tok(q, ("dma", q, idx, val - 16))
        for t in self._deps(q, reads, writes):
            self._wait_tok(q, t)
        tok = ("dma", q, idx, val)
        if not self.rec:
            fn().then_inc(self.dsems[q][idx], 16)
        self._commit(tok, reads, writes)
        return tok

    def wait_all_dma(self, e):
        for q, P in self.N_DMA_SEMS.items():
            n = self.dma_n[q]
            for idx in range(min(P, n)):
                cnt = (n - 1 - idx) // P + 1
                self._wait_tok(e, ("dma", q, idx, 16 * cnt))

    def barrier(self):
        toks = []
        for e in ("pe", "act", "dve", "pool"):
            if self.seq[e] > 0:
                toks.append(("eng", e, self.seq[e] - 1))
        for e in ENGS:
            for t in toks:
                if t[1] != e:
                    self._wait_tok(e, t)
            self.wait_all_dma(e)
        self.lastw = {}
        self.reads = {}


class Arena:
    def __init__(self, nbytes):
        self.nbytes = nbytes
        self.top = 0
        self.t = None
        self.peak = 0

    def mark(self):
        return self.top

    def reset(self, m):
        self.top = m

    def alloc(self, shape, dt):
        esz = 2 if dt == BF16 else 4
        n = int(np.prod(shape[1:]))
        nb = (n * esz + 63) // 64 * 64
        off = self.top
        self.top += nb
        self.peak = max(self.peak, self.top)
        assert self.top <= self.nbytes, f"arena overflow {self.top} > {self.nbytes}"
        if self.t is None:
            return None
        ap = self.t[:, off // 2: off // 2 + n * esz // 2]
        if dt == F32:
            ap = ap.bitcast(F32)
        if len(shape) == 3:
            ap = ap.rearrange("p (a b) -> p a b", a=shape[1])
        elif len(shape) == 4:
            ap = ap.rearrange("p (a b c) -> p a b c", a=shape[1], b=shape[2])
        return ap


ARENA_BYTES = 212736


def build(S, debug=False, stop=None):
    nc = S.nc
    rec = S.rec
    D = {}
    T = {}
    top = contextlib.ExitStack()

    A = Arena(ARENA_BYTES)
    if not rec:
        A.t = top.enter_context(nc.sbuf_tensor("arena", [128, ARENA_BYTES // 2], BF16))

    def sbuf(es, name, shape, dt):
        return A.alloc(shape, dt)

    if not rec:
        def din(name, shape, dt=F32):
            D[name] = nc.dram_tensor(name, shape, dt, kind="ExternalInput").ap()
        din("xT", [128, 8, NTOK])
        din("xo", [NOWN, 128, 1024])
        din("w_in", [1024, WCOLS])
        din("w_ba", [1024, 1024])
        din("w_bb", [1024, 1024])
        din("w_out", [1024, 1024])
        din("normw", [128, 8])
        din("lamv", [1, 256])
        din("sublnw", [1, 128])
        din("gnw", [1, 256])
        din("fnw", [1, 1024])
        din("gateb", [128, 4])
        din("w2", [16, 512])
        din("cs16", [128, NBLK, 16])
        din("sn16", [128, NBLK, 16])
        din("fcol", [128, 2])
        din("tri", [128, 128])
        din("identf", [128, 128])
        D["out"] = nc.dram_tensor("out", [NOWN, 128, 1024], F32, kind="ExternalOutput").ap()
        if debug:
            D["d_ut"] = nc.dram_tensor("d_ut", [128, 8, NTOK], BF16, kind="ExternalOutput").ap()
            D["d_ob"] = nc.dram_tensor("d_ob", [128, NOWN, 1024], BF16, kind="ExternalOutput").ap()
            D["d_oa"] = nc.dram_tensor("d_oa", [128, NOWN, 1024], BF16, kind="ExternalOutput").ap()
        S.esems = {e: top.enter_context(nc.semaphore("es_" + e)) for e in ENGS}
        S.dsems = {q: [top.enter_context(nc.semaphore(f"ds_{q}{i}")) for i in range(n)]
                   for q, n in S.N_DMA_SEMS.items()}
        WIN = D["w_in"].rearrange("(c p) n -> p c n", p=128)
        WBA = D["w_ba"].rearrange("(c p) n -> p c n", p=128)
        WBB = D["w_bb"].rearrange("(c p) n -> p c n", p=128)
        WOUT = D["w_out"].rearrange("(c p) n -> p c n", p=128)

    uT = sbuf(None, "uT", [128, 8, NTOK], BF16)
    ident = sbuf(None, "ident", [128, 128], BF16)
    tri = sbuf(None, "tris", [128, 128], F32)
    ones_bf = sbuf(None, "ones_bf", [128, 128], BF16)
    normw = sbuf(None, "normws", [128, 8], F32)
    fcol = sbuf(None, "fcols", [128, 2], F32)
    lams = sbuf(None, "lams", [128, 8], F32)
    gateb = sbuf(None, "gatebs", [128, 4], F32)
    negb = sbuf(None, "negb", [128, 4], F32)
    MARK_P = A.mark()
    lamv = sbuf(None, "lamvs", [128, 256], F32)
    lamt = sbuf(None, "lamt", [128, 128], F32)
    PS = [None] * 8
    PSB = [None] * 8
    if not rec:
        for i in range(8):
            PS[i] = top.enter_context(nc.psum_tensor(f"ps{i}", [128, 512], F32))
            PSB[i] = PS[i][:].bitcast(BF16)

    def pk(i, part="a"):
        return f"ps{i}"

    def pw(i):
        return [f"ps{i}"]

    def ld(dst, src, key, q="sp"):
        S.dma(q, lambda: S.eng(q).dma_start(out=dst(), in_=src()), writes=[key])

    ld(lambda: normw[:], lambda: D["normw"][:, :], "normw")
    ld(lambda: ident[:], lambda: D["identf"][:, :], "ident", q="pool")
    ld(lambda: tri[:], lambda: D["tri"][:, :], "tri")
    ld(lambda: fcol[:], lambda: D["fcol"][:, :], "fcol")
    ld(lambda: gateb[:], lambda: D["gateb"][:, :], "gateb")
    ld(lambda: lamv[:], lambda: D["lamv"][0:1, :].partition_broadcast(128), "lamv")
    S.op("pool", lambda: nc.gpsimd.memset(ones_bf[:], 1.0), writes=["ones_bf"])
    S.op("pool", lambda: nc.gpsimd.tensor_scalar(out=negb[:], in0=gateb[:], scalar1=-1.0, scalar2=None,
                                                 op0=ALU.mult), reads=["gateb"], writes=["negb"])
    S.op("dve", lambda: nc.vector.tensor_tensor(out=lamt[:, 0:64], in0=lamv[:, 0:64], in1=lamv[:, 64:128],
                                                op=ALU.mult), reads=["lamv"], writes=["lamt0"])
    S.op("dve", lambda: nc.vector.tensor_tensor(out=lamt[:, 64:128], in0=lamv[:, 128:192], in1=lamv[:, 192:256],
                                                op=ALU.mult), reads=["lamv"], writes=["lamt1"])
    S.op("dve", lambda: nc.vector.reduce_sum(out=lams[:, 0:1], in_=lamt[:, 0:64], axis=AX.X),
         reads=["lamt0"], writes=["lams0"])
    S.op("dve", lambda: nc.vector.reduce_sum(out=lams[:, 1:2], in_=lamt[:, 64:128], axis=AX.X),
         reads=["lamt1"], writes=["lams1"])
    S.op("act", lambda: nc.scalar.activation(out=lams[:, 2:4], in_=lams[:, 0:2], func=AF.Exp),
         reads=["lams0", "lams1"], writes=["lams23"])
    S.op("dve", lambda: nc.vector.tensor_tensor(out=lams[:, 5:6], in0=lams[:, 3:4], in1=lams[:, 2:3],
                                                op=ALU.subtract), reads=["lams23"], writes=["lams5"])
    S.op("dve", lambda: nc.vector.tensor_scalar(out=lams[:, 4:5], in0=lams[:, 5:6], scalar1=-0.2, scalar2=None,
                                                op0=ALU.add), reads=["lams5"], writes=["neglam"])

    A.reset(MARK_P)
    o_b = sbuf(None, "o_b", [128, NOWN, 1024], BF16)
    MARK_P1 = A.mark()
    gnw = sbuf(None, "gnws", [128, 256], F32)
    w2b = sbuf(None, "w2b", [128, 512], BF16)
    ld(lambda: gnw[:], lambda: D["gnw"][0:1, :].partition_broadcast(128), "gnw")
    ld(lambda: w2b[0:16, :], lambda: D["w2"][:, :], "w2b", q="pool")
    w_gq = sbuf(None, "w_gq", [128, 8, 512], BF16)
    w_gk = sbuf(None, "w_gk", [128, 8, 512], BF16)
    w_gv = sbuf(None, "w_gv", [128, 8, 1024], BF16)
    w_glr = sbuf(None, "w_glr", [128, 8, 16], BF16)
    for c in range(8):
        S.dma("pool", lambda c=c: nc.gpsimd.dma_start(out=w_gk[:, c, :], in_=WIN[:, c, C_GK:C_GK + 512]), writes=["w_gk"])
        S.dma("pool", lambda c=c: nc.gpsimd.dma_start(out=w_gv[:, c, :], in_=WIN[:, c, C_GV:C_GV + 1024]), writes=["w_gv"])
        S.dma("pool", lambda c=c: nc.gpsimd.dma_start(out=w_gq[:, c, :], in_=WIN[:, c, C_GQ:C_GQ + 512]), writes=["w_gq"])
    S.dma("pool", lambda: nc.gpsimd.dma_start(out=w_glr[:], in_=WIN[:, :, C_GLR:C_GLR + 16]), writes=["w_glr"])
    MARK_G = A.mark()
    A.reset(ARENA_BYTES - 57344)
    assert A.top >= MARK_G
    xs = [sbuf(None, f"xs{i}", [128, 8, 512], F32) for i in range(2)]
    sq = [sbuf(None, f"sq{i}", [128, 8, 512], BF16) for i in range(2)]
    rb = [sbuf(None, f"rb{i}", [128, 512], F32) for i in range(2)]
    chunks = [(0, 16)]
    t = 16
    while t < NTOK:
        n = min(512, NTOK - t)
        chunks.append((t, n))
        t += n
    for ci, (t0, n) in enumerate(chunks):
        b = ci % 2
        S.dma("sp", lambda b=b, t0=t0, n=n: nc.sync.dma_start(out=xs[b][:, :, 0:n], in_=D["xT"][:, :, t0:t0 + n]),
              writes=[f"xs{b}"])
        S.op("act", lambda b=b, n=n: nc.scalar.activation(out=sq[b][:, :, 0:n], in_=xs[b][:, :, 0:n], func=AF.Square),
             reads=[f"xs{b}"], writes=[f"sq{b}"])
        for c in range(8):
            S.op("pe", lambda b=b, n=n, c=c: nc.tensor.matmul(PS[b][:, 0:n], lhsT=ones_bf[:, :], rhs=sq[b][:, c, 0:n],
                                                             start=(c == 0), stop=(c == 7)),
                 reads=[f"sq{b}", "ones_bf"], writes=[pk(b)])
        S.op("act", lambda b=b, n=n: nc.scalar.activation(out=rb[b][:, 0:n], in_=PS[b][:, 0:n], func=AF.Ln,
                                                          bias=EPS, scale=1.0 / 1024.0),
             reads=[pk(b)], writes=[f"rb{b}"])
        S.op("act", lambda b=b, n=n: nc.scalar.activation(out=rb[b][:, 0:n], in_=rb[b][:, 0:n], func=AF.Exp, scale=-0.5),
             reads=[f"rb{b}"], writes=[f"rb{b}"])
        for c in range(8):
            S.op("dve", lambda b=b, n=n, c=c, t0=t0: nc.vector.scalar_tensor_tensor(
                out=uT[:, c, t0:t0 + n], in0=xs[b][:, c, 0:n], scalar=normw[:, c:c + 1], in1=rb[b][:, 0:n],
                op0=ALU.mult, op1=ALU.mult), reads=[f"xs{b}", f"rb{b}", "normw"], writes=["uT"])
    S.barrier()
    A.reset(MARK_G)
    if debug:
        S.dma("sp", lambda: nc.sync.dma_start(out=D["d_ut"][:, :, :], in_=uT[:]), reads=["uT"])
    if stop == "p0":
        S.wait_all_dma("sp")
        S.arena_peak = A.peak
        top.close()
        return

    glrT = sbuf(None, "glrT", [128, 512], BF16)
    spb_l = [sbuf(None, f"spb{i}", [128, 512], F32) for i in range(2)]
    cb_l = [sbuf(None, f"cb{i}", [128, 512], F32) for i in range(2)]
    enb = sbuf(None, "enb", [128, 512], F32)
    eb = sbuf(None, "eb", [128, 512], F32)
    ktmp = sbuf(None, "ktmp", [128, 512], F32)
    khT = sbuf(None, "khT", [128, 512], BF16)
    smask = sbuf(None, "smask", [128, 512], F32)
    dd = sbuf(None, "dd", [128, 4], F32)
    qtT = sbuf(None, "qtT", [128, 4, 512], BF16)
    ktT = sbuf(None, "ktT", [128, 4, 512], BF16)
    kh_own = sbuf(None, "kh_own", [128, 4, 4, 128], BF16)
    kh_oth = sbuf(None, "kh_oth", [128, 4, 4, 128], BF16)
    kh_pre = sbuf(None, "kh_pre", [128, 2, 4, 128], BF16)
    v_own = sbuf(None, "v_own", [128, 4, 1024], BF16)
    v_oth = sbuf(None, "v_oth", [128, 4, 1024], BF16)
    v_pre = sbuf(None, "v_pre", [128, 2, 1024], BF16)
    dec_own = sbuf(None, "dec_own", [128, 4, 4], F32)
    dec_oth = sbuf(None, "dec_oth", [128, 4, 4], F32)
    dec_pre = sbuf(None, "dec_pre", [128, 4, 2], F32)
    Sst = sbuf(None, "Sst", [128, 4, 256], F32)
    Sbf = sbuf(None, "Sbf", [128, 4, 256], BF16)
    a1 = sbuf(None, "a1", [128, 4], F32)
    AT = sbuf(None, "AT", [128, 512], BF16)
    ssq = sbuf(None, "ssq", [128, 8], F32)
    junk = sbuf(None, "junk", [128, 256], F32)

    S.op("pool", lambda: nc.gpsimd.memset(smask[:], 1.0), writes=["smask"])
    S.op("pool", lambda: nc.gpsimd.memset(smask[:].rearrange("p (b t) -> p b t", t=128)[:, :, 0:1], 0.0),
         writes=["smask"])

    DKS = 128 ** -0.5

    def gla_pre(kind, t0, nblk, bs):
        n = nblk * bs
        own = kind == "own"
        kh_t = {"own": kh_own, "oth": kh_oth, "meta": kh_pre, "X": kh_pre}[kind]
        v_t = {"own": v_own, "oth": v_oth, "meta": v_pre, "X": v_pre}[kind]
        dec_t = {"own": dec_own, "oth": dec_oth, "meta": dec_pre, "X": dec_pre}[kind]
        kkey = {"own": "kh_own", "oth": "kh_oth", "meta": "kh_pre0", "X": "kh_pre1"}[kind]
        vkey = {"own": "v_own", "oth": "v_oth", "meta": "v_pre0", "X": "v_pre1"}[kind]
        dkey = {"own": "dec_own", "oth": "dec_oth", "meta": "dec_pre0", "X": "dec_pre1"}[kind]
        boff = 1 if kind == "X" else 0
        for c in range(8):
            S.op("pe", lambda c=c: nc.tensor.matmul(PS[0][0:16, 0:n], lhsT=w_glr[:, c, :], rhs=uT[:, c, t0:t0 + n],
                                                    start=(c == 0), stop=(c == 7)),
                 reads=["w_glr", "uT"], writes=[pk(0)])
        S.op("dve", lambda: nc.vector.tensor_copy(glrT[0:16, 0:n], PS[0][0:16, 0:n]), reads=[pk(0)], writes=["glrT"])
        for j in range(nblk):
            for half in range(2):
                bank = 5 + half
                for c in range(8):
                    S.op("pe", lambda c=c, j=j, half=half, bank=bank: nc.tensor.matmul(
                        PS[bank][0:bs, :], lhsT=uT[:, c, t0 + j * bs:t0 + (j + 1) * bs],
                        rhs=w_gv[:, c, half * 512:(half + 1) * 512], start=(c == 0), stop=(c == 7)),
                        reads=["uT", "w_gv"], writes=pw(bank))
                S.op("act", lambda j=j, half=half, bank=bank: nc.scalar.copy(
                    v_t[0:bs, boff + j, half * 512:(half + 1) * 512], PS[bank][0:bs, :]),
                    reads=pw(bank), writes=[vkey])
        for h in range(4):
            hs = slice(h * 128, (h + 1) * 128)
            spb = spb_l[h % 2]
            cb = cb_l[h % 2]
            skey = f"spb{h % 2}"
            ckey = f"cb{h % 2}"
            S.op("pe", lambda hs=hs: nc.tensor.matmul(PS[1][:, 0:n], lhsT=w2b[0:16, hs], rhs=glrT[0:16, 0:n],
                                                      start=True, stop=True),
                 reads=["w2b", "glrT"], writes=pw(1))
            S.op("act", lambda h=h: nc.scalar.activation(out=spb[:, 0:n], in_=PS[1][:, 0:n], func=AF.Exp,
                                                         bias=negb[:, h:h + 1], scale=-1.0),
                 reads=pw(1) + ["negb"], writes=[skey])
            S.op("act", lambda: nc.scalar.activation(out=spb[:, 0:n], in_=spb[:, 0:n], func=AF.Ln, bias=1.0, scale=1.0),
                 reads=[skey], writes=[skey])
            S.op("dve", lambda: nc.vector.tensor_tensor_scan(out=cb[:, 0:n], data0=smask[:, 0:n], data1=spb[:, 0:n],
                                                             initial=0.0, op0=ALU.mult, op1=ALU.add),
                 reads=["smask", skey], writes=[ckey])
            S.op("act", lambda: nc.scalar.activation(out=enb[:, 0:n], in_=cb[:, 0:n], func=AF.Exp, scale=1.0 / 16.0),
                 reads=[ckey], writes=["enb"])
            for c in range(8):
                S.op("pe", lambda c=c, hs=hs: nc.tensor.matmul(PS[2][:, 0:n], lhsT=w_gk[:, c, hs], rhs=uT[:, c, t0:t0 + n],
                                                               start=(c == 0), stop=(c == 7)),
                     reads=["w_gk", "uT"], writes=pw(2))
            if own:
                S.op("act", lambda: nc.scalar.activation(out=eb[:, 0:n], in_=cb[:, 0:n], func=AF.Exp, scale=-1.0 / 16.0),
                     reads=[ckey], writes=["eb"])
                S.op("dve", lambda h=h: nc.vector.tensor_tensor(out=ktT[:, h, 0:n], in0=PS[2][:, 0:n], in1=enb[:, 0:n],
                                                                op=ALU.mult),
                     reads=pw(2) + ["enb"], writes=[f"ktT{h}"])
                S.op("pool", lambda h=h: nc.gpsimd.tensor_copy(
                    dec_t[:, h, 0:nblk], eb[:, 0:n].rearrange("p (b t) -> p b t", t=bs)[:, :, bs - 1]),
                    reads=["eb"], writes=[dkey])
                S.op("pool", lambda h=h: nc.gpsimd.tensor_tensor(
                    out=khT[:, 0:n].rearrange("p (b t) -> p b t", t=bs),
                    in0=ktT[:, h, 0:n].rearrange("p (b t) -> p b t", t=bs),
                    in1=eb[:, 0:n].rearrange("p (b t) -> p b t", t=bs)[:, :, bs - 1:bs].to_broadcast([128, nblk, bs]),
                    op=ALU.mult), reads=[f"ktT{h}", "eb"], writes=["khT"])
                for c in range(8):
                    S.op("pe", lambda c=c, hs=hs: nc.tensor.matmul(PS[3][:, 0:n], lhsT=w_gq[:, c, hs],
                                                                   rhs=uT[:, c, t0:t0 + n], start=(c == 0), stop=(c == 7)),
                         reads=["w_gq", "uT"], writes=[pk(3)])
                S.op("dve", lambda h=h: nc.vector.scalar_tensor_tensor(
                    out=qtT[:, h, 0:n], in0=PS[3][:, 0:n], scalar=DKS, in1=eb[:, 0:n], op0=ALU.mult, op1=ALU.mult),
                    reads=[pk(3), "eb"], writes=[f"qtT{h}"])
            else:
                S.op("act", lambda h=h: nc.scalar.activation(
                    out=dec_t[:, h, boff:boff + nblk], in_=cb[:, 0:n].rearrange("p (b t) -> p b t", t=bs)[:, :, bs - 1],
                    func=AF.Exp, scale=-1.0 / 16.0), reads=[ckey], writes=[dkey])
                S.op("dve", lambda: nc.vector.tensor_tensor(out=ktmp[:, 0:n], in0=PS[2][:, 0:n], in1=enb[:, 0:n],
                                                            op=ALU.mult), reads=pw(2) + ["enb"], writes=["ktmp"])
                S.op("pool", lambda h=h: nc.gpsimd.tensor_tensor(
                    out=khT[:, 0:n].rearrange("p (b t) -> p b t", t=bs),
                    in0=ktmp[:, 0:n].rearrange("p (b t) -> p b t", t=bs),
                    in1=dec_t[:, h, boff:boff + nblk].unsqueeze(2).to_broadcast([128, nblk, bs]),
                    op=ALU.mult), reads=["ktmp", dkey], writes=["khT"])
            for j in range(nblk):
                S.op("pe", lambda j=j: nc.tensor.transpose(PSB[4][0:bs, j * 128:(j + 1) * 128],
                                                           khT[:, j * bs:(j + 1) * bs], ident[:]),
                     reads=["khT", "ident"], writes=[pk(4)])
            S.op("dve", lambda h=h: nc.vector.tensor_copy(
                kh_t[0:bs, boff:boff + nblk, h, :],
                PSB[4][0:bs, 0:nblk * 128].rearrange("p (b d) -> p b d", d=128)),
                reads=[pk(4)], writes=[kkey])

    def state_update(h, kh_t, blk, v_t, vblk, dec_t, dblk, bs, kkey, vkey, dkey, first=False):
        bank = 1 + (h // 2)
        cs_ = slice((h % 2) * 256, (h % 2) * 256 + 256)
        hv = slice(h * 256, (h + 1) * 256)
        S.op("pe", lambda: nc.tensor.matmul(PS[bank][:, cs_], lhsT=kh_t[0:bs, blk, h, :], rhs=v_t[0:bs, vblk, hv],
                                            start=True, stop=True),
             reads=[kkey, vkey], writes=[pk(bank, "ab"[h % 2])])
        if first:
            S.op("dve", lambda: nc.vector.tensor_copy(Sst[:, h, :], PS[bank][:, cs_]),
                 reads=[pk(bank, "ab"[h % 2])], writes=[f"S{h}"])
        else:
            S.op("dve", lambda: nc.vector.scalar_tensor_tensor(
                out=Sst[:, h, :], in0=Sst[:, h, :], scalar=dec_t[:, h, dblk:dblk + 1], in1=PS[bank][:, cs_],
                op0=ALU.mult, op1=ALU.add), reads=[f"S{h}", dkey, pk(bank, "ab"[h % 2])], writes=[f"S{h}"])

    def cast_state(h):
        S.op("pool", lambda: nc.gpsimd.tensor_copy(Sbf[:, h, :], Sst[:, h, :]), reads=[f"S{h}"], writes=[f"Sbf{h}"])

    def early_exit():
        S.barrier()
        if debug:
            S.dma("sp", lambda: nc.sync.dma_start(out=D["d_ob"][:, :, :], in_=o_b[:]), reads=["o_b"])
        S.wait_all_dma("sp")
        S.arena_peak = A.peak
        top.close()

    gla_pre("meta", 0, 1, 16)
    if stop == "g1":
        return early_exit()
    gla_pre("X", OTH0, 1, 128)
    if stop == "g1b":
        return early_exit()
    for h in range(4):
        state_update(h, kh_pre, 0, v_pre, 0, dec_pre, 0, 16, "kh_pre0", "v_pre0", "dec_pre0", first=True)
    S.op("dve", lambda: nc.vector.tensor_scalar(out=a1[:], in0=dec_pre[:, :, 1], scalar1=fcol[:, 0:1],
                                                scalar2=fcol[:, 1:2], op0=ALU.mult, op1=ALU.add),
         reads=["dec_pre1", "fcol"], writes=["a1"])
    for h in range(4):
        bank = 1 + (h // 2)
        cs_ = slice((h % 2) * 256, (h % 2) * 256 + 256)
        hv = slice(h * 256, (h + 1) * 256)
        S.op("pe", lambda h=h, bank=bank, cs_=cs_, hv=hv: nc.tensor.matmul(
            PS[bank][:, cs_], lhsT=kh_pre[:, 1, h, :], rhs=v_pre[:, 1, hv], start=True, stop=True),
            reads=["kh_pre1", "v_pre1"], writes=[pk(bank, "ab"[h % 2])])
        S.op("dve", lambda h=h: nc.vector.tensor_scalar(out=junk[:], in0=Sst[:, h, :], scalar1=a1[:, h:h + 1],
                                                        scalar2=None, op0=ALU.mult),
             reads=[f"S{h}", "a1"], writes=["junk"])
        S.op("dve", lambda h=h, bank=bank, cs_=cs_: nc.vector.scalar_tensor_tensor(
            out=Sst[:, h, :], in0=PS[bank][:, cs_], scalar=fcol[:, 0:1], in1=junk[:], op0=ALU.mult, op1=ALU.add),
            reads=[pk(bank, "ab"[h % 2]), "junk", "fcol"], writes=[f"S{h}"])
        cast_state(h)

    if stop == "g2":
        return early_exit()
    for s in range(4):
        gla_pre("own", OWN0 + s * 512, 4, 128)
        gla_pre("oth", OTH0 + 128 + s * 512, 4, 128)
        if stop == "g3":
            return early_exit()
        for j in range(4):
            i = 4 * s + j
            bsl = slice(j * 128, (j + 1) * 128)
            for h in range(4):
                S.op("pe", lambda h=h, bsl=bsl: nc.tensor.matmul(PS[7][:, h * 128:(h + 1) * 128], lhsT=ktT[:, h, bsl],
                                                                 rhs=qtT[:, h, bsl], start=True, stop=True),
                     reads=[f"ktT{h}", f"qtT{h}"], writes=[pk(7)])
            S.op("dve", lambda: nc.vector.tensor_tensor(
                out=AT[:].rearrange("p (h t) -> p h t", t=128), in0=PS[7][:].rearrange("p (h t) -> p h t", t=128),
                in1=tri[:].unsqueeze(1).to_broadcast([128, 4, 128]), op=ALU.mult),
                reads=[pk(7), "tri"], writes=["AT"])
            if stop == "g4":
                return early_exit()
            for h in range(4):
                bank = 5 + (h // 2)
                cs_ = slice((h % 2) * 256, (h % 2) * 256 + 256)
                hv = slice(h * 256, (h + 1) * 256)
                S.op("pe", lambda h=h, bank=bank, cs_=cs_, hv=hv, j=j: nc.tensor.matmul(
                    PS[bank][:, cs_], lhsT=AT[:, h * 128:(h + 1) * 128], rhs=v_own[:, j, hv], start=True, stop=False),
                    reads=["AT", "v_own"], writes=[pk(bank, "ab"[h % 2])])
                S.op("pe", lambda h=h, bank=bank, cs_=cs_, bsl=bsl: nc.tensor.matmul(
                    PS[bank][:, cs_], lhsT=qtT[:, h, bsl], rhs=Sbf[:, h, :], start=False, stop=True),
                    reads=[f"qtT{h}", f"Sbf{h}"], writes=[pk(bank, "ab"[h % 2])])
            if stop == "g5":
                return early_exit()
            for h in range(4):
                state_update(h, kh_own, j, v_own, j, dec_own, j, 128, "kh_own", "v_own", "dec_own")
                cast_state(h)
            if stop == "g6":
                return early_exit()
            for h in range(4):
                bank = 5 + (h // 2)
                cs_ = slice((h % 2) * 256, (h % 2) * 256 + 256)
                S.op("act", lambda h=h, bank=bank, cs_=cs_: nc.scalar.activation(
                    out=junk[:], in_=PS[bank][:, cs_], func=AF.Square, accum_out=ssq[:, h:h + 1]),
                    reads=[pk(bank, "ab"[h % 2])], writes=["junk", f"ssq{h}"])
            S.op("act", lambda: nc.scalar.activation(out=ssq[:, 4:8], in_=ssq[:, 0:4], func=AF.Ln, bias=EPS,
                                                     scale=1.0 / 256.0),
                 reads=[f"ssq{h}" for h in range(4)], writes=["ssqln"])
            S.op("act", lambda: nc.scalar.activation(out=ssq[:, 4:8], in_=ssq[:, 4:8], func=AF.Exp, scale=-0.5),
                 reads=["ssqln"], writes=["ssqr"])
            for h in range(4):
                bank = 5 + (h // 2)
                cs_ = slice((h % 2) * 256, (h % 2) * 256 + 256)
                S.op("dve", lambda h=h, bank=bank, cs_=cs_, i=i: nc.vector.scalar_tensor_tensor(
                    out=o_b[:, i, h * 256:(h + 1) * 256], in0=PS[bank][:, cs_], scalar=ssq[:, 4 + h:5 + h],
                    in1=gnw[:], op0=ALU.mult, op1=ALU.mult),
                    reads=[pk(bank, "ab"[h % 2]), "ssqr", "gnw"], writes=[f"o_b{i}_{h}"])
            if stop == "g7":
                return early_exit()
            for h in range(4):
                state_update(h, kh_oth, j, v_oth, j, dec_oth, j, 128, "kh_oth", "v_oth", "dec_oth")
                cast_state(h)
            if stop == "g8":
                return early_exit()
        if stop == "g9":
            return early_exit()
        if stop == "g10" and s == 1:
            return early_exit()
    S.barrier()
    A.reset(MARK_P1)
    if debug:
        S.dma("sp", lambda: nc.sync.dma_start(out=D["d_ob"][:, :, :], in_=o_b[:]), reads=["o_b"])
    if stop == "gla":
        S.wait_all_dma("sp")
        S.arena_peak = A.peak
        top.close()
        return

    o_a = sbuf(None, "o_a", [128, NOWN, 1024], BF16)
    MARK_P2 = A.mark()
    cs16 = sbuf(None, "cs16s", [128, NBLK, 16], F32)
    sn16 = sbuf(None, "sn16s", [128, NBLK, 16], F32)
    sw8 = sbuf(None, "sw8", [128, 128], F32)
    ld(lambda: cs16[:], lambda: D["cs16"][:, :, :], "cs16")
    ld(lambda: sn16[:], lambda: D["sn16"][:, :, :], "sn16")
    ld(lambda: sw8[:], lambda: D["sublnw"][0:1, :].partition_broadcast(128), "sw8")
    S.op("pool", lambda: nc.gpsimd.tensor_scalar(out=sw8[:], in0=sw8[:], scalar1=0.8, scalar2=None,
                                                 op0=ALU.mult), reads=["sw8"], writes=["sw8"])
    wq_sb = [sbuf(None, f"wq{i}", [128, 8, 256], BF16) for i in range(2)]
    wk_sb = sbuf(None, "wk", [128, 8, 256], BF16)
    wv_sb = sbuf(None, "wv", [128, 8, 256], BF16)
    KT = sbuf(None, "KT", [128, 2, NTOK], BF16)
    Vaug = sbuf(None, "Vaug", [128, NBLK, 2, 130], BF16)
    Pt = [[sbuf(None, f"Pt{b}{r}", [128, 512], BF16) for r in range(2)] for b in range(2)]
    kf = [sbuf(None, f"kf{i}", [128, 256], F32) for i in range(3)]
    kbb = [sbuf(None, f"kbb{i}", [128, 256], BF16) for i in range(3)]
    rt1 = [sbuf(None, f"rt1{i}", [128, 4, 16], F32) for i in range(3)]
    rt2 = [sbuf(None, f"rt2{i}", [128, 4, 16], F32) for i in range(3)]
    o2 = sbuf(None, "o2", [128, 2, 128], F32)
    ssa = sbuf(None, "ssa", [128, 4], F32)
    ssap = [ssa, sbuf(None, "ssa_b", [128, 4], F32)]
    oraw = [sbuf(None, f"oraw{k}", [128, 258], F32) for k in range(4)]
    rlr = [sbuf(None, f"rlr{k}", [128, 2], F32) for k in range(4)]
    t2r = [sbuf(None, f"t2r{k}", [128, 128], F32) for k in range(4)]
    o2p = [o2, sbuf(None, "o2_b", [128, 2, 128], F32)]

    S.op("pool", lambda: nc.gpsimd.memset(Vaug[:, :, :, 128:130], 1.0), writes=["Vones"])
    S.op("pool", lambda: nc.gpsimd.memset(Vaug[:, 0, :, :], 0.0), reads=["Vones"], writes=["Vones", "V0"])
    S.op("pool", lambda: nc.gpsimd.tensor_copy(Vaug[:, 0, :, 128:129], tri[:, 15:16].unsqueeze(1).to_broadcast([128, 2, 1])),
         reads=["tri", "Vones"], writes=["Vones"])
    S.op("pool", lambda: nc.gpsimd.tensor_copy(Vaug[:, 1, :, 128:129], fcol[:, 0:1].unsqueeze(1).to_broadcast([128, 2, 1])),
         reads=["fcol", "Vones"], writes=["Vones"])

    def blk_t0(blk):
        if blk == 0:
            return 0, 16
        if blk <= NOTH:
            return OTH0 + (blk - 1) * 128, 128
        return OWN0 + (blk - 1 - NOTH) * 128, 128

    rope_n = [0]
    zeros_bf = sbuf(None, "zeros_bf", [128, 128], BF16)
    S.op("pool", lambda: nc.gpsimd.memset(zeros_bf[:], 0.0), writes=["zeros_bf"])
    QTz = [sbuf(None, f"QTz{k}", [128, 2, 256], BF16) for k in range(2)]
    for k in range(2):
        S.op("pool", lambda k=k: nc.gpsimd.memset(QTz[k][:], 0.0), writes=[f"QT{k}"])

    def rope(src_bank, blk, bs, cast_eng="dve"):
        r = rope_n[0] % 3
        rope_n[0] += 1
        S.op("dve", lambda: nc.vector.tensor_copy(kf[r][0:bs, :], PS[src_bank][0:bs, 0:256]),
             reads=[pk(src_bank)], writes=[f"kf{r}"])
        if cast_eng == "act":
            S.op("act", lambda: nc.scalar.copy(kbb[r][0:bs, :], kf[r][0:bs, :]),
                 reads=[f"kf{r}"], writes=[f"kbb{r}"])
        else:
            S.op("dve", lambda: nc.vector.tensor_copy(kbb[r][0:bs, :], kf[r][0:bs, :]), reads=[f"kf{r}"], writes=[f"kbb{r}"])
        rv = lambda t_: t_[0:bs, :].rearrange("p (g d) -> p g d", g=4)[:, :, 0:16]
        S.op("pool", lambda: nc.gpsimd.tensor_tensor(out=rt1[r][0:bs], in0=rv(kf[r]),
                                                     in1=cs16[0:bs, blk:blk + 1, :].to_broadcast([bs, 4, 16]), op=ALU.mult),
             reads=[f"kf{r}", "cs16"], writes=[f"rt1{r}"])
        S.op("pool", lambda: nc.gpsimd.tensor_tensor(out=rt2[r][0:bs, :, 0:8], in0=rv(kf[r])[:, :, 8:16],
                                                     in1=sn16[0:bs, blk:blk + 1, 0:8].to_broadcast([bs, 4, 8]), op=ALU.mult),
             reads=[f"kf{r}", "sn16"], writes=[f"rt2a{r}"])
        S.op("pool", lambda: nc.gpsimd.tensor_tensor(out=rt2[r][0:bs, :, 8:16], in0=rv(kf[r])[:, :, 0:8],
                                                     in1=sn16[0:bs, blk:blk + 1, 8:16].to_broadcast([bs, 4, 8]), op=ALU.mult),
             reads=[f"kf{r}", "sn16"], writes=[f"rt2b{r}"])
        S.op("pool", lambda: nc.gpsimd.tensor_tensor(out=rv(kbb[r]), in0=rt1[r][0:bs], in1=rt2[r][0:bs], op=ALU.add),
             reads=[f"rt1{r}", f"rt2a{r}", f"rt2b{r}", f"kbb{r}"], writes=[f"kbb{r}"])
        return r

    def transposes(r, bs, bank, dst_fn, dkey):
        for h in range(2):
            S.op("pe", lambda h=h: nc.tensor.transpose(PSB[bank][:, h * 128:h * 128 + bs],
                                                       kbb[r][0:bs, h * 128:(h + 1) * 128], ident[0:bs, 0:bs]),
                 reads=[f"kbb{r}", "ident"], writes=[pk(bank)])
        S.op("dve", lambda: nc.vector.tensor_copy(
            dst_fn(), PSB[bank][:, 0:256].rearrange("p (h t) -> p h t", t=128)[:, :, 0:bs]),
            reads=[pk(bank)], writes=[dkey])

    def load_w(dst, col0, key):
        S.dma("pool", lambda: nc.gpsimd.dma_start(out=dst[:], in_=WIN[:, :, col0:col0 + 256]), writes=[key])

    load_w(wk_sb, C_AK, "wk")
    load_w(wv_sb, C_AV, "wv")
    load_w(wq_sb[0], C_AQ, "wq0")
    for hp in range(4):
        wq = wq_sb[hp % 2]
        wqk = f"wq{hp % 2}"
        if hp + 1 < 4:
            load_w(wq_sb[(hp + 1) % 2], C_AQ + (hp + 1) * 256, f"wq{(hp + 1) % 2}")
        prev = None
        for blk in range(NBLK):
            t0, bs = blk_t0(blk)
            kbank = (6, 2)[blk % 2]
            vbank = (7, 3)[blk % 2]
            for c in range(8):
                S.op("pe", lambda c=c, t0=t0, bs=bs, kbank=kbank: nc.tensor.matmul(
                    PS[kbank][0:bs, 0:256], lhsT=uT[:, c, t0:t0 + bs], rhs=wk_sb[:, c, :], start=(c == 0), stop=(c == 7)),
                    reads=["uT", "wk"], writes=[pk(kbank)])
            r = rope(kbank, blk, bs, cast_eng="act")
            for c in range(8):
                S.op("pe", lambda c=c, t0=t0, bs=bs, vbank=vbank: nc.tensor.matmul(
                    PS[vbank][0:bs, 0:256], lhsT=uT[:, c, t0:t0 + bs], rhs=wv_sb[:, c, :], start=(c == 0), stop=(c == 7)),
                    reads=["uT", "wv"], writes=[pk(vbank)])
            S.op("act", lambda blk=blk, bs=bs, vbank=vbank: nc.scalar.copy(
                Vaug[0:bs, blk, :, 0:128], PS[vbank][0:bs, 0:256].rearrange("p (h d) -> p h d", d=128)),
                reads=[pk(vbank)], writes=[f"V{blk}"])
            if prev is not None:
                pr, pbs, pt0, pblk = prev
                transposes(pr, pbs, (0, 1)[pblk % 2], lambda pt0=pt0, pbs=pbs: KT[:, :, pt0:pt0 + pbs], f"KT{pblk}")
            prev = (r, bs, t0, blk)
        pr, pbs, pt0, pblk = prev
        transposes(pr, pbs, (0, 1)[pblk % 2], lambda: KT[:, :, pt0:pt0 + pbs], f"KT{pblk}")
        if hp + 1 < 4:
            load_w(wk_sb, C_AK + (hp + 1) * 256, "wk")
            load_w(wv_sb, C_AV + (hp + 1) * 256, "wv")

        def q_proj(i):
            t0q = OWN0 + i * 128
            for c in range(8):
                S.op("pe", lambda c=c: nc.tensor.matmul(PS[6][:, 0:256], lhsT=uT[:, c, t0q:t0q + 128],
                                                        rhs=wq[:, c, :], start=(c == 0), stop=(c == 7)),
                     reads=["uT", wqk], writes=[pk(6)])
            return rope(6, 1 + NOTH + i, 128)

        def q_tr(i, r):
            for h in range(2):
                S.op("pe", lambda h=h: nc.tensor.transpose(PSB[7][:, h * 128:(h + 1) * 128],
                                                           kbb[r][:, h * 128:(h + 1) * 128], ident[:, :]),
                     reads=[f"kbb{r}", "ident"], writes=[pk(7)])
            S.op("dve", lambda: nc.vector.tensor_copy(
                QTz[i % 2][0:64, :, 0:128], PSB[7][0:64, 0:256].rearrange("p (h t) -> p h t", t=128)),
                reads=[pk(7)], writes=[f"QT{i % 2}"])
            S.op("dve", lambda: nc.vector.tensor_copy(
                QTz[i % 2][64:128, :, 128:256], PSB[7][64:128, 0:256].rearrange("p (h t) -> p h t", t=128)),
                reads=[pk(7)], writes=[f"QT{i % 2}"])

        items = []
        for i in range(NOWN):
            qblk = 1 + NOTH + i
            for h in range(2):
                oth = [(1 + j, OTH0 + j * 128, 128) for j in range(i + 1)]
                own = [(1 + NOTH + j, OWN0 + j * 128, 128) for j in range(i + 1)]
                lst = [own[-1], (0, 0, 128)] + oth + own[:-1]
                groups = [lst[a:a + 2] for a in range(0, len(lst), 2)]
                for gi, grp in enumerate(groups):
                    items.append(dict(i=i, h=h, grp=grp, first=(gi == 0), last=(gi == len(groups) - 1), qblk=qblk))

        def emit_st(it, r):
            i, h, grp = it["i"], it["h"], it["grp"]
            qt = QTz[i % 2]
            for s_, (kb, kt0, kbs) in enumerate(grp):
                bank = r
                c0 = (s_ % 2) * 256
                S.op("pe", lambda bank=bank, c0=c0, kt0=kt0, kbs=kbs: nc.tensor.matmul(
                    PS[bank][0:kbs, c0:c0 + 256], lhsT=KT[:, h, kt0:kt0 + kbs], rhs=qt[:, h, :], start=True, stop=True),
                    reads=[f"KT{kb}", f"QT{i % 2}"], writes=[pk(bank)])

        def emit_exp_pv(it, r, ob):
            i, h, grp, qblk = it["i"], it["h"], it["grp"], it["qblk"]
            bs = grp[0][2]
            ncol = len(grp) * 128 if bs == 128 else 128
            nk = len(grp)
            ptile = Pt[r // 2][r % 2]
            pkey = f"Pt{r}"
            S.op("act", lambda: nc.scalar.activation(
                out=ptile[0:bs, 0:nk * 256], in_=PS[r][0:bs, 0:nk * 256], func=AF.Exp, scale=0.125),
                reads=[pk(r)], writes=[pkey])
            if grp[0][0] == qblk:
                c0 = 0
                S.op("dve", lambda c0=c0: nc.vector.tensor_tensor(
                    out=ptile[:, c0:c0 + 256].rearrange("p (b t) -> p b t", t=128),
                    in0=ptile[:, c0:c0 + 256].rearrange("p (b t) -> p b t", t=128),
                    in1=tri[:].unsqueeze(1).to_broadcast([128, 2, 128]), op=ALU.mult),
                    reads=[pkey, "tri"], writes=[pkey])
            for s_, (kb, kt0, kbs) in enumerate(grp):
                c0 = s_ * 256
                for b in range(2):
                    lastmm = it["last"] and s_ == len(grp) - 1 and b == 1
                    S.op("pe", lambda b=b, c0=c0, kb=kb, kbs=kbs, lastmm=lastmm: nc.tensor.matmul(
                        PS[4 + ob][:, b * 129:(b + 1) * 129], lhsT=ptile[0:kbs, c0 + b * 128:c0 + (b + 1) * 128],
                        rhs=Vaug[0:kbs, kb, h, 0:129], start=(it["first"] and s_ == 0 and b == 0), stop=lastmm,
                        skip_group_check=True),
                        reads=[pkey, f"V{kb}", "Vones"], writes=[pk(4 + ob)])

        RING = 4
        dq = []
        bcount = [0]

        def flush(upto=None, nmax=None):
            n = 0
            while dq and (upto is None or dq[0][0] <= upto) and (nmax is None or n < nmax):
                _, e_, fn_, rd_, wr_ = dq.pop(0)
                S.op(e_, fn_, reads=rd_, writes=wr_)
                n += 1

        def finalize(i, h, ob):
            bi = bcount[0]
            bcount[0] += 1
            flush(upto=bi - RING)
            k = bi % RING
            kb_ = (bi // 2) % 2 if False else (i % 2)
            S.op("dve", lambda: nc.vector.tensor_copy(oraw[k][:, :], PS[4 + ob][:, 0:258]),
                 reads=[pk(4 + ob)], writes=[f"oraw{k}"])
            pb = i % 2
            D_ = lambda e_, fn_, rd_, wr_: dq.append((bi, e_, fn_, rd_, wr_))
            D_("dve", lambda: nc.vector.reciprocal(rlr[k][:, 0:1], oraw[k][:, 128:129]), [f"oraw{k}"], [f"rl0{k}"])
            D_("dve", lambda: nc.vector.reciprocal(rlr[k][:, 1:2], oraw[k][:, 257:258]), [f"oraw{k}"], [f"rl1{k}"])
            D_("dve", lambda: nc.vector.tensor_scalar(out=t2r[k][:], in0=oraw[k][:, 129:257], scalar1=rlr[k][:, 1:2],
                                                      scalar2=lams[:, 4:5], op0=ALU.mult, op1=ALU.mult),
               [f"oraw{k}", f"rl1{k}", "neglam"], [f"t2{k}"])
            D_("dve", lambda: nc.vector.scalar_tensor_tensor(
                out=o2p[pb][:, h, :], in0=oraw[k][:, 0:128], scalar=rlr[k][:, 0:1], in1=t2r[k][:], op0=ALU.mult, op1=ALU.add),
               [f"oraw{k}", f"rl0{k}", f"t2{k}"], [f"o2{pb}{h}"])
            D_("pool", lambda: nc.gpsimd.tensor_tensor(out=t2r[k][:], in0=o2p[pb][:, h, :], in1=o2p[pb][:, h, :], op=ALU.mult),
               [f"o2{pb}{h}"], [f"t2{k}"])
            D_("dve", lambda: nc.vector.reduce_sum(out=ssap[pb][:, h:h + 1], in_=t2r[k][:], axis=AX.X),
               [f"t2{k}"], [f"ssa{pb}{h}"])
            if h == 1:
                D_("act", lambda: nc.scalar.activation(out=ssap[pb][:, 2:4], in_=ssap[pb][:, 0:2], func=AF.Ln, bias=EPS,
                                                       scale=1.0 / 128.0), [f"ssa{pb}0", f"ssa{pb}1"], [f"ssaln{pb}"])
                D_("act", lambda: nc.scalar.activation(out=ssap[pb][:, 2:4], in_=ssap[pb][:, 2:4], func=AF.Exp, scale=-0.5),
                   [f"ssaln{pb}"], [f"ssar{pb}"])
                for h2 in range(2):
                    hh = 2 * hp + h2
                    D_("dve", lambda h2=h2, hh=hh: nc.vector.scalar_tensor_tensor(
                        out=o_a[:, i, hh * 128:(hh + 1) * 128], in0=o2p[pb][:, h2, :], scalar=ssap[pb][:, 2 + h2:3 + h2],
                        in1=sw8[:], op0=ALU.mult, op1=ALU.mult), [f"o2{pb}{h2}", f"ssar{pb}", "sw8"], [f"o_a{i}"])

        LA = 3
        qr = q_proj(0)
        q_tr(0, qr)
        for n0 in range(LA):
            emit_st(items[n0], n0 % 4)
        qr_next = None
        pending = None
        for n_, it in enumerate(items):
            r = n_ % 4
            i, h = it["i"], it["h"]
            ob = (2 * i + h) % 2
            if it["first"] and h == 0 and i + 1 < NOWN:
                qr_next = q_proj(i + 1)
            if n_ + LA < len(items):
                nx = items[n_ + LA]
                if nx["first"] and nx["h"] == 0:
                    q_tr(nx["i"], qr_next)
                emit_st(nx, (n_ + LA) % 4)
            emit_exp_pv(it, r, ob)
            flush(nmax=2)
            if it["last"]:
                finalize(i, h, ob)
        flush()
    if debug:
        S.dma("sp", lambda: nc.sync.dma_start(out=D["d_oa"][:, :, :], in_=o_a[:]), reads=[f"o_a{i}" for i in range(NOWN)])
    S.barrier()
    A.reset(MARK_P2)
    if stop == "da":
        S.wait_all_dma("sp")
        S.arena_peak = A.peak
        top.close()
        return

    oagT = sbuf(None, "oagT", [128, 8, 2048], BF16)
    obgT = sbuf(None, "obgT", [128, 8, 2048], BF16)
    wz = [sbuf(None, f"wz{i}", [128, 8, 128], BF16) for i in range(2)]
    gz = [sbuf(None, f"gz{i}", [128, 512], F32) for i in range(2)]

    def gate_stage(src, srckey, col0, dst, dkey):
        for c in range(8):
            wb = c % 2
            S.dma("pool", lambda c=c, wb=wb: nc.gpsimd.dma_start(out=wz[wb][:], in_=WIN[:, :, col0 + c * 128:col0 + (c + 1) * 128]),
                  writes=[f"wz{wb}"])
            for tg in range(4):
                g = (c * 4 + tg) % 2
                ts_ = slice(OWN0 + tg * 512, OWN0 + (tg + 1) * 512)
                for k in range(8):
                    S.op("pe", lambda k=k, wb=wb, ts_=ts_, g=g: nc.tensor.matmul(
                        PS[g][:, :], lhsT=wz[wb][:, k, :], rhs=uT[:, k, ts_], start=(k == 0), stop=(k == 7)),
                        reads=[f"wz{wb}", "uT"], writes=[pk(g)])
                S.op("act", lambda g=g: nc.scalar.activation(out=gz[g][:], in_=PS[g][:, :], func=AF.Silu),
                     reads=[pk(g)], writes=[f"gz{g}"])
                for j in range(4):
                    S.op("pe", lambda j=j, tg=tg, c=c, g=g: nc.tensor.transpose(
                        PSB[2 + g][:, j * 128:(j + 1) * 128], src[:, tg * 4 + j, c * 128:(c + 1) * 128], ident[:]),
                        reads=[srckey, "ident"], writes=[pk(2 + g)])
                S.op("dve", lambda g=g, c=c, tg=tg: nc.vector.tensor_tensor(
                    out=dst[:, c, tg * 512:(tg + 1) * 512], in0=PSB[2 + g][:, 0:512], in1=gz[g][:], op=ALU.mult),
                    reads=[pk(2 + g), f"gz{g}"], writes=[f"{dkey}{c}_{tg}"])

    gate_stage(o_a, "o_a", C_AZ, oagT, "oagT")
    gate_stage(o_b, "o_b", C_GZ, obgT, "obgT")
    S.barrier()
    A.reset(MARK_P)
    mT = sbuf(None, "mT", [128, 8, 2048], BF16)
    MARK_C = A.mark()
    wc = [[sbuf(None, f"wc{n}{i}", [128, 8, 256], BF16) for i in range(2)] for n in range(4)]
    assert A.top <= MARK_P2
    A.reset(MARK_P2 + 65536)
    sg = [sbuf(None, f"sg{i}", [128, 512], F32) for i in range(2)]
    tt = [sbuf(None, f"tt{i}", [128, 512], F32) for i in range(2)]
    for d in range(8):
        wb = (d // 2) % 2
        dsub = d % 2
        if dsub == 0:
            ds_ = slice(d * 128, (d + 2) * 128)
            S.dma("pool", lambda wb=wb, ds_=ds_: nc.gpsimd.dma_start(out=wc[0][wb][:], in_=WBA[:, :, ds_]), writes=[f"wc0{wb}"])
            S.dma("pool", lambda wb=wb, ds_=ds_: nc.gpsimd.dma_start(out=wc[1][wb][:], in_=WBB[:, :, ds_]), writes=[f"wc1{wb}"])
            S.dma("pool", lambda wb=wb, d=d: nc.gpsimd.dma_start(out=wc[2][wb][:], in_=WIN[:, :, C_GA + d * 128:C_GA + (d + 2) * 128]),
                  writes=[f"wc2{wb}"])
            S.dma("pool", lambda wb=wb, d=d: nc.gpsimd.dma_start(out=wc[3][wb][:], in_=WIN[:, :, C_GB + d * 128:C_GB + (d + 2) * 128]),
                  writes=[f"wc3{wb}"])
        for tg in range(4):
            ts_ = slice(tg * 512, (tg + 1) * 512)
            tsu = slice(OWN0 + tg * 512, OWN0 + (tg + 1) * 512)
            ph4 = ((d * 4 + tg) % 2) * 4
            srcs = [(oagT, "oagT", ts_), (obgT, "obgT", ts_), (uT, "uT", tsu), (uT, "uT", tsu)]
            for n_ in range(4):
                src, skey, sl = srcs[n_]
                for k in range(8):
                    S.op("pe", lambda n_=n_, k=k, src=src, sl=sl, wb=wb, ph4=ph4: nc.tensor.matmul(
                        PS[ph4 + n_][:, :], lhsT=wc[n_][wb][:, k, dsub * 128:(dsub + 1) * 128], rhs=src[:, k, sl],
                        start=(k == 0), stop=(k == 7)),
                        reads=[f"wc{n_}{wb}", skey], writes=[pk(ph4 + n_)])
            for n_ in range(2):
                S.op("act", lambda n_=n_, ph4=ph4: nc.scalar.activation(out=sg[n_][:], in_=PS[ph4 + 2 + n_][:, :], func=AF.Sigmoid),
                     reads=[pk(ph4 + 2 + n_)], writes=[f"sg{n_}"])
                S.op("dve", lambda n_=n_, ph4=ph4: nc.vector.tensor_tensor(out=tt[n_][:], in0=PS[ph4 + n_][:, :], in1=sg[n_][:],
                                                                          op=ALU.mult),
                     reads=[pk(ph4 + n_), f"sg{n_}"], writes=[f"tt{n_}"])
            S.op("pool", lambda d=d, ts_=ts_: nc.gpsimd.tensor_tensor(out=mT[:, d, ts_], in0=tt[0][:], in1=tt[1][:], op=ALU.add),
                 reads=["tt0", "tt1"], writes=[f"mT{d}_{tg}"])
    S.barrier()
    A.reset(MARK_C)
    wout_sb = sbuf(None, "wout_sb", [128, 8, 1024], BF16)
    fnw = sbuf(None, "fnws", [128, 1024], F32)
    for c in range(8):
        S.dma("pool", lambda c=c: nc.gpsimd.dma_start(out=wout_sb[:, c, :], in_=WOUT[:, c, :]), writes=[f"wout{c}"])
    ld(lambda: fnw[:], lambda: D["fnw"][0:1, :].partition_broadcast(128), "fnw")
    NXR = 6
    xr = [sbuf(None, f"xr{i}", [128, 1024], F32) for i in range(NXR)]
    hres = [sbuf(None, f"hres{i}", [128, 1024], F32) for i in range(2)]
    osb = [sbuf(None, f"osb{i}", [128, 1024], F32) for i in range(2)]
    fss = sbuf(None, "fss", [128, 4], F32)
    junk3 = sbuf(None, "junk3", [128, 1024], F32)
    def load_x(i):
        S.dma("sp", lambda: nc.sync.dma_start(out=xr[i % NXR][:], in_=D["xo"][i, :, :]), writes=[f"xr{i % NXR}"])

    for i in range(NXR - 1):
        load_x(i)
    for i in range(NOWN):
        r = i % 2
        if i + NXR - 1 < NOWN:
            load_x(i + NXR - 1)
        for half in range(2):
            bank = (i % 2) * 2 + half
            for k in range(8):
                S.op("pe", lambda k=k, i=i, half=half, bank=bank: nc.tensor.matmul(
                    PS[bank][:, :], lhsT=mT[:, k, i * 128:(i + 1) * 128], rhs=wout_sb[:, k, half * 512:(half + 1) * 512],
                    start=(k == 0), stop=(k == 7)), reads=["mT", f"wout{k}"], writes=[pk(bank)])
            S.op("dve", lambda r=r, half=half, bank=bank: nc.vector.tensor_tensor(
                out=hres[r][:, half * 512:(half + 1) * 512], in0=PS[bank][:, :], in1=xr[i % NXR][:, half * 512:(half + 1) * 512],
                op=ALU.add), reads=[pk(bank), f"xr{i % NXR}"], writes=[f"hres{r}{half}"])
        S.op("act", lambda r=r: nc.scalar.activation(out=junk3[:], in_=hres[r][:], func=AF.Square, accum_out=fss[:, r:r + 1]),
             reads=[f"hres{r}0", f"hres{r}1"], writes=["junk3", f"fss{r}"])
        S.op("act", lambda r=r: nc.scalar.activation(out=fss[:, 2 + r:3 + r], in_=fss[:, r:r + 1], func=AF.Ln, bias=EPS,
                                                     scale=1.0 / 1024.0), reads=[f"fss{r}"], writes=[f"fsl{r}"])
        S.op("act", lambda r=r: nc.scalar.activation(out=fss[:, 2 + r:3 + r], in_=fss[:, 2 + r:3 + r], func=AF.Exp, scale=-0.5),
             reads=[f"fsl{r}"], writes=[f"fsr{r}"])
        S.op("dve", lambda r=r: nc.vector.scalar_tensor_tensor(
            out=osb[r][:], in0=hres[r][:], scalar=fss[:, 2 + r:3 + r], in1=fnw[:], op0=ALU.mult, op1=ALU.mult),
            reads=[f"hres{r}0", f"hres{r}1", f"fsr{r}", "fnw"], writes=[f"osb{r}"])
        S.dma("sp", lambda i=i, r=r: nc.sync.dma_start(out=D["out"][i, :, :], in_=osb[r][:]), reads=[f"osb{r}"])
    S.wait_all_dma("sp")
    S.arena_peak = A.peak
    top.close()


_CACHE = {}


def _program(debug=False, stop=None):
    key = ("prog", debug, stop)
    if key not in _CACHE:
        s1 = Sched(None)
        build(s1, debug, stop)
        nc = bass.Bass("TRN2", target_bir_lowering=False)
        s2 = Sched(nc, need=s1.need)
        build(s2, debug, stop)
        _CACHE[key] = nc
    return _CACHE[key]


def _rope_tables(p):
    half = 8
    inv_freq = (np.float32(500000.0) ** (-np.arange(half, dtype=np.float32) / np.float32(half))).astype(np.float32)
    pos = np.zeros((128, NBLK), np.float32)
    tt = np.arange(128, dtype=np.float32)
    pos[:, 0] = tt
    for j in range(NOTH):
        sb = 2 * j - 1 if p == 0 else 2 * j
        if 0 <= sb < 32:
            pos[:, 1 + j] = 16 + sb * 128 + tt
    for i in range(NOWN):
        pos[:, 1 + NOTH + i] = 16 + (2 * i + p) * 128 + tt
    ang = (pos[:, :, None] * inv_freq[None, None, :]).astype(np.float32)
    cos = np.cos(ang).astype(np.float32)
    sin = np.sin(ang).astype(np.float32)
    cs16 = np.concatenate([cos, cos], axis=-1)
    sn16 = np.concatenate([-sin, sin], axis=-1)
    return np.ascontiguousarray(cs16), np.ascontiguousarray(sn16)


def make_in_maps(x, meta_tokens, norm_w, w_in, lam_q1, lam_k1, lam_q2, lam_k2, da_subln_w,
                 gla_gate_w2, gla_gate_b, gla_norm_w, w_branch_a, w_branch_b, w_out, final_norm_w):
    f32 = np.float32
    x = np.asarray(x, f32)
    meta = np.asarray(meta_tokens, f32)
    shared = {
        "w_in": np.ascontiguousarray(np.asarray(w_in, f32)[0]),
        "w_ba": np.ascontiguousarray(np.asarray(w_branch_a, f32)[0]),
        "w_bb": np.ascontiguousarray(np.asarray(w_branch_b, f32)[0]),
        "w_out": np.ascontiguousarray(np.asarray(w_out, f32)[0]),
        "normw": np.ascontiguousarray(np.asarray(norm_w, f32)[0].reshape(8, 128).T),
        "lamv": np.concatenate([np.asarray(a, f32)[0] for a in (lam_q1, lam_k1, lam_q2, lam_k2)])[None, :].copy(),
        "sublnw": np.asarray(da_subln_w, f32)[0][None, :].copy(),
        "gnw": np.asarray(gla_norm_w, f32)[0][None, :].copy(),
        "fnw": np.asarray(final_norm_w, f32)[None, :].copy(),
        "gateb": np.ascontiguousarray(np.asarray(gla_gate_b, f32)[0].reshape(4, 128).T),
        "w2": np.ascontiguousarray(np.asarray(gla_gate_w2, f32)[0]),
        "tri": np.triu(np.ones((128, 128), f32)),
        "identf": np.eye(128, dtype=f32),
    }
    in_maps = []
    for core in range(8):
        b, p = core // 2, core % 2
        xb = x[b].reshape(32, 128, 1024)
        own = xb[p::2]
        zero = np.zeros((1, 128, 1024), f32)
        if p == 0:
            oth = np.concatenate([zero, xb[1::2]], axis=0)
        else:
            oth = np.concatenate([xb[0::2], zero], axis=0)
        allt = np.concatenate([meta, oth.reshape(-1, 1024), own.reshape(-1, 1024)], axis=0)
        xT = np.ascontiguousarray(allt.T.reshape(8, 128, NTOK).transpose(1, 0, 2))
        cs16, sn16 = _rope_tables(p)
        fc = np.zeros((128, 2), f32)
        fc[:, 0] = p
        fc[:, 1] = 1 - p
        m = dict(shared)
        m.update({"xT": xT, "xo": np.ascontiguousarray(own), "cs16": cs16, "sn16": sn16, "fcol": fc})
        in_maps.append(m)
    return in_maps


def kernel(**inputs):
    debug = bool(os.environ.get("KDEBUG"))
    nc = _program(debug)
    in_maps = make_in_maps(**inputs)
    res = run_bass_kernel_spmd(nc, in_maps, core_ids=list(range(8)))
    out = np.zeros((4, 4096, 1024), np.float32)
    for core in range(8):
        b, p = core // 2, core % 2
        oc = np.asarray(res.results[core]["out"], np.float32)
        out[b].reshape(32, 128, 1024)[p::2] = oc
    if debug:
        kernel.last_results = res.results
    return out
```
